# Optimizing a Trainium2 kernel written in Bass

```python
import math
import jax, jax.numpy as jnp
from jax import lax
import numpy as np

D_MODEL = 1024
BATCH = 8
SEQ = 8192
DEPTH = 1

GRID_W = 64
CTX_LEN = 256
D_S5 = D_MODEL // 2
D_CONF = D_MODEL - D_S5
D_MIX = D_S5 + D_CONF
S5_GROUP = 16
S5_GROUPS = D_S5 // S5_GROUP
S5_STATE = 64
CONV_K = 31
FFN_K = 3
D_FF = 2816
N_MOD = 6
EPS = 1e-6
LN_EPS = 1e-5
DT_MIN = 1e-3
DT_MAX = 1e-1
C_STD = 0.5

kernel_name = "hymba_style_s5_conformer_dit_block"


def rmsnorm(h, g):
    hf = h.astype(jnp.float32)
    hf = hf * lax.rsqrt(jnp.mean(hf * hf, axis=-1, keepdims=True) + EPS)
    return (hf * g.astype(jnp.float32)).astype(h.dtype)


def layernorm(h, g, b):
    hf = h.astype(jnp.float32)
    mu = jnp.mean(hf, axis=-1, keepdims=True)
    var = jnp.mean(jnp.square(hf - mu), axis=-1, keepdims=True)
    out = (hf - mu) * lax.rsqrt(var + LN_EPS) * g.astype(jnp.float32) + b.astype(jnp.float32)
    return out.astype(h.dtype)


def modulation(cvec, w_mod, b_mod):
    m = jax.nn.silu(cvec) @ w_mod + b_mod
    return jnp.split(m[:, None, :], N_MOD, axis=-1)


def modulate(h, shift, scale):
    return h * (1.0 + scale) + shift


def dwconv(h, w, b, rows):
    n, l, ch = h.shape
    hs = h if rows is None else h.reshape(n * rows, l // rows, ch)
    out = lax.conv_general_dilated(hs, w[:, None, :].astype(h.dtype), window_strides=(1,),
                                   padding='SAME', dimension_numbers=('NWC', 'WIO', 'NWC'),
                                   feature_group_count=ch)
    return out.reshape(n, l, ch) + b


def cmul(ar, ai, br, bi):
    return ar * br - ai * bi, ar * bi + ai * br


def _ssm_combine(e1, e2):
    a1r, a1i, b1r, b1i = e1
    a2r, a2i, b2r, b2i = e2
    ar, ai = cmul(a2r, a2i, a1r, a1i)
    br, bi = cmul(a2r, a2i, b1r, b1i)
    return ar, ai, br + b2r, bi + b2i


def s5_discretise(lam_re, lam_im, log_dt, b_re, b_im):
    f32 = jnp.float32
    lr, li = lam_re.astype(f32), lam_im.astype(f32)
    dt = jnp.exp(log_dt.astype(f32))[:, None]
    mag = jnp.exp(lr * dt)
    lbr, lbi = mag * jnp.cos(li * dt), mag * jnp.sin(li * dt)
    nr, ni = lbr - 1.0, lbi
    den = lr * lr + li * li
    qr = (nr * lr + ni * li) / den
    qi = (ni * lr - nr * li) / den
    bbr, bbi = cmul(qr[..., None], qi[..., None], b_re.astype(f32), b_im.astype(f32))
    return lbr, lbi, bbr, bbi


def s5_scan(u, disc, h0, reverse):
    lbr, lbi, bbr, bbi = disc
    uf = u.astype(jnp.float32)
    bur = jnp.einsum('nlgc,gpc->nlgp', uf, bbr)
    bui = jnp.einsum('nlgc,gpc->nlgp', uf, bbi)
    if h0 is not None:
        idx = -1 if reverse else 0
        hr, hi = cmul(lbr, lbi, h0[0], h0[1])
        bur = bur.at[:, idx].add(hr)
        bui = bui.at[:, idx].add(hi)
    l = u.shape[1]
    ar = jnp.broadcast_to(lbr[None, None], (1, l) + lbr.shape)
    ai = jnp.broadcast_to(lbi[None, None], (1, l) + lbi.shape)
    _, _, sr, si = lax.associative_scan(_ssm_combine, (ar, ai, bur, bui), reverse=reverse, axis=1)
    return sr, si


def s5_mixer(u, state_f, state_b, c_f, c_b, d, w_glu, b_glu):
    n, l = u.shape[:2]
    f32 = jnp.float32
    y = (jnp.einsum('nlgp,gcp->nlgc', state_f[0], c_f[0].astype(f32))
         - jnp.einsum('nlgp,gcp->nlgc', state_f[1], c_f[1].astype(f32))
         + jnp.einsum('nlgp,gcp->nlgc', state_b[0], c_b[0].astype(f32))
         - jnp.einsum('nlgp,gcp->nlgc', state_b[1], c_b[1].astype(f32)))
    y = y + d.reshape(S5_GROUPS, S5_GROUP).astype(f32) * u.astype(f32)
    y = jax.nn.gelu(y.reshape(n, l, D_S5).astype(u.dtype))
    return y * jax.nn.sigmoid(y @ w_glu + b_glu)


def conformer_conv(z, w, b, ln_g, ln_b, rows):
    v, g = jnp.split(z, 2, axis=-1)
    h = dwconv(v * jax.nn.sigmoid(g), w, b, rows)
    return jax.nn.silu(layernorm(h, ln_g, ln_b))


def conv_ffn(h, g, shift, scale, w_up, cw, cb, w_down, rows):
    hn = modulate(rmsnorm(h, g), shift, scale)
    v, gt = jnp.split(hn @ w_up, 2, axis=-1)
    return (v * jax.nn.silu(dwconv(gt, cw, cb, rows))) @ w_down


def setup_inputs(seed: int = 0) -> dict:
    key = jax.random.key(seed)
    ks = jax.random.split(key, 32)
    nrm = jax.random.normal
    G, P = S5_GROUPS, S5_STATE
    lam_im0 = math.pi * jnp.arange(P, dtype=jnp.float32)
    return {
        "x": nrm(ks[0], (BATCH, SEQ, D_MODEL), jnp.float32),
        "c": nrm(ks[1], (BATCH, D_MODEL), jnp.float32),
        "ctx": nrm(ks[2], (BATCH, CTX_LEN, D_MODEL), jnp.float32),
        "c_ctx": nrm(ks[3], (D_MODEL,), jnp.float32),
        "norm1_g": 1.0 + 0.05 * nrm(ks[4], (DEPTH, D_MODEL)),
        "norm2_g": 1.0 + 0.05 * nrm(ks[5], (DEPTH, D_MODEL)),
        "w_mod": 0.5 * D_MODEL ** -0.5 * nrm(ks[6], (DEPTH, D_MODEL, N_MOD * D_MODEL)),
        "b_mod": 0.02 * nrm(ks[7], (DEPTH, N_MOD * D_MODEL)),
        "w_in": D_MODEL ** -0.5 * nrm(ks[8], (DEPTH, D_MODEL, D_S5 + 2 * D_CONF)),
        "s5_lam_re": -0.5 + 0.01 * nrm(ks[9], (DEPTH, 2, G, P)),
        "s5_lam_im": lam_im0 + 0.01 * nrm(ks[10], (DEPTH, 2, G, P)),
        "s5_log_dt": jax.random.uniform(ks[11], (DEPTH, 2, G), minval=math.log(DT_MIN), maxval=math.log(DT_MAX)),
        "s5_b_re": (2 * S5_GROUP) ** -0.5 * nrm(ks[12], (DEPTH, 2, G, P, S5_GROUP)),
        "s5_b_im": (2 * S5_GROUP) ** -0.5 * nrm(ks[13], (DEPTH, 2, G, P, S5_GROUP)),
        "s5_c_re": C_STD * nrm(ks[14], (DEPTH, 2, G, S5_GROUP, P)),
        "s5_c_im": C_STD * nrm(ks[15], (DEPTH, 2, G, S5_GROUP, P)),
        "s5_d": nrm(ks[16], (DEPTH, D_S5)),
        "w_glu": D_S5 ** -0.5 * nrm(ks[17], (DEPTH, D_S5, D_S5)),
        "b_glu": 0.02 * nrm(ks[18], (DEPTH, D_S5)),
        "conf_w": CONV_K ** -0.5 * nrm(ks[19], (DEPTH, CONV_K, D_CONF)),
        "conf_b": 0.02 * nrm(ks[20], (DEPTH, D_CONF)),
        "conf_ln_g": 1.0 + 0.05 * nrm(ks[21], (DEPTH, D_CONF)),
        "conf_ln_b": 0.02 * nrm(ks[22], (DEPTH, D_CONF)),
        "w_out": D_MIX ** -0.5 * nrm(ks[23], (DEPTH, D_MIX, D_MODEL)),
        "ffn_w_up": D_MODEL ** -0.5 * nrm(ks[24], (DEPTH, D_MODEL, 2 * D_FF)),
        "ffn_conv_w": FFN_K ** -0.5 * nrm(ks[25], (DEPTH, FFN_K, D_FF)),
        "ffn_conv_b": 0.02 * nrm(ks[26], (DEPTH, D_FF)),
        "ffn_w_down": D_FF ** -0.5 * nrm(ks[27], (DEPTH, D_FF, D_MODEL)),
        "final_g": 1.0 + 0.05 * nrm(ks[28], (D_MODEL,)),
    }


def reference(x, c, ctx, c_ctx, norm1_g, norm2_g, w_mod, b_mod, w_in, s5_lam_re, s5_lam_im,
              s5_log_dt, s5_b_re, s5_b_im, s5_c_re, s5_c_im, s5_d, w_glu, b_glu, conf_w, conf_b,
              conf_ln_g, conf_ln_b, w_out, ffn_w_up, ffn_conv_w, ffn_conv_b, ffn_w_down, final_g):
    nb, seq = x.shape[0], x.shape[1]
    rows = seq // GRID_W
    n_ctx = ctx.shape[1]
    h, hc = x, ctx
    for i in range(DEPTH):
        sh1, sc1, g1, sh2, sc2, g2 = modulation(c, w_mod[i], b_mod[i])
        csh1, csc1, cg1, csh2, csc2, cg2 = modulation(c_ctx[None], w_mod[i], b_mod[i])
        disc_f = s5_discretise(s5_lam_re[i, 0], s5_lam_im[i, 0], s5_log_dt[i, 0], s5_b_re[i, 0], s5_b_im[i, 0])
        disc_b = s5_discretise(s5_lam_re[i, 1], s5_lam_im[i, 1], s5_log_dt[i, 1], s5_b_re[i, 1], s5_b_im[i, 1])
        c_f = (s5_c_re[i, 0], s5_c_im[i, 0])
        c_b = (s5_c_re[i, 1], s5_c_im[i, 1])

        zc = modulate(rmsnorm(hc, norm1_g[i]), csh1, csc1) @ w_in[i]
        uc = zc[..., :D_S5].reshape(nb, n_ctx, S5_GROUPS, S5_GROUP)
        stc_f = s5_scan(uc, disc_f, None, False)
        stc_b = s5_scan(uc, disc_b, None, True)
        h0_f = (stc_f[0][:, -1], stc_f[1][:, -1])
        h0_b = (stc_b[0][:, 0], stc_b[1][:, 0])

        zx = modulate(rmsnorm(h, norm1_g[i]), sh1, sc1) @ w_in[i]
        ux = zx[..., :D_S5].reshape(nb, seq, S5_GROUPS, S5_GROUP)
        stx_f = s5_scan(ux, disc_f, h0_f, False)
        stx_b = s5_scan(ux, disc_b, h0_b, True)
        mix = jnp.concatenate([
            s5_mixer(ux, stx_f, stx_b, c_f, c_b, s5_d[i], w_glu[i], b_glu[i]),
            conformer_conv(zx[..., D_S5:], conf_w[i], conf_b[i], conf_ln_g[i], conf_ln_b[i], rows),
        ], axis=-1)
        h = h + g1 * (mix @ w_out[i])
        h = h + g2 * conv_ffn(h, norm2_g[i], sh2, sc2, ffn_w_up[i], ffn_conv_w[i], ffn_conv_b[i], ffn_w_down[i], rows)

        if i + 1 < DEPTH:
            mix_c = jnp.concatenate([
                s5_mixer(uc, stc_f, stc_b, c_f, c_b, s5_d[i], w_glu[i], b_glu[i]),
                conformer_conv(zc[..., D_S5:], conf_w[i], conf_b[i], conf_ln_g[i], conf_ln_b[i], None),
            ], axis=-1)
            hc = hc + cg1 * (mix_c @ w_out[i])
            hc = hc + cg2 * conv_ffn(hc, norm2_g[i], csh2, csc2, ffn_w_up[i], ffn_conv_w[i], ffn_conv_b[i], ffn_w_down[i], None)
    return rmsnorm(h, final_g)
```

```python
import numpy as np
from contextlib import ExitStack
import concourse.bass as bass
import concourse.mybir as mybir
from concourse.bass_utils import run_bass_kernel_spmd

F32 = mybir.dt.float32
BF16 = mybir.dt.bfloat16
I32 = mybir.dt.int32
AF = mybir.ActivationFunctionType
ALU = mybir.AluOpType
AX = mybir.AxisListType

ENG_ATTR = {"pe": "tensor", "act": "scalar", "dve": "vector", "pool": "gpsimd", "sp": "sync"}
_DTSIZE = {F32: 4, BF16: 2, I32: 4}


class Prog:
    def __init__(self, nc, es):
        self.nc = nc
        self.es = es
        self.streams = {e: [] for e in ENG_ATTR}
        self.sem = {}
        self.tick = {}
        self.waited = {e: {} for e in ENG_ATTR}
        self.w = {}
        self.r = {}
        self.nops = {e: 0 for e in ENG_ATTR}
        for e in ENG_ATTR:
            self._mksem(("eng", e))

    def _mksem(self, key):
        if key not in self.sem:
            name = "s_" + "_".join(str(k) for k in key)
            self.sem[key] = self.es.enter_context(self.nc.semaphore(name))
            self.tick[key] = 0

    def _deps(self, eng, reads, writes):
        deps = {}

        def add(ev):
            k, v = ev
            if k == ("eng", "pe") and eng == "pe":
                return
            if deps.get(k, 0) < v:
                deps[k] = v

        for r in reads:
            if r in self.w:
                add(self.w[r])
        for w_ in writes:
            if w_ in self.w:
                add(self.w[w_])
            for k, v in self.r.get(w_, {}).items():
                add((k, v))
        out = []
        for k, v in deps.items():
            if self.waited[eng].get(k, 0) < v:
                self.waited[eng][k] = v
                out.append((self.sem[k], v))
        return out

    def _commit(self, ev, reads, writes):
        for r in reads:
            d = self.r.setdefault(r, {})
            if d.get(ev[0], 0) < ev[1]:
                d[ev[0]] = ev[1]
        for w_ in writes:
            self.w[w_] = ev
            self.r[w_] = {}

    def op(self, eng, fn, reads=(), writes=(), signal=True):
        waits = self._deps(eng, reads, writes)
        key = ("eng", eng)
        if signal:
            self.tick[key] += 1
            ev = (key, self.tick[key])
        else:
            ev = (key, self.tick[key] + 1)
        self._commit(ev, reads, writes)
        sem = self.sem[key]
        self.nops[eng] += 1

        def emit(e):
            for s, v in waits:
                e.wait_ge(s, v)
            ins = fn(e)
            if signal:
                ins.then_inc(sem, 1)

        self.streams[eng].append(emit)

    def dma(self, q, out, in_, reads=(), writes=(), semkey=None, **kw):
        key = ("dma", semkey)
        self._mksem(key)
        waits = self._deps(q, reads, writes)
        self.tick[key] += 16
        ev = (key, self.tick[key])
        self._commit(ev, reads, writes)
        sem = self.sem[key]
        self.nops[q] += 1

        def emit(e):
            for s, v in waits:
                e.wait_ge(s, v)
            e.dma_start(out=out, in_=in_, **kw).then_inc(sem, 16)

        self.streams[q].append(emit)

    def barrier(self, engines=None):
        for e in (engines or ENG_ATTR):
            waits = []
            for k, v in self.tick.items():
                if v > 0 and k != ("eng", e) and self.waited[e].get(k, 0) < v:
                    self.waited[e][k] = v
                    waits.append((self.sem[k], v))

            def emit(en, waits=waits):
                for s, v in waits:
                    en.wait_ge(s, v)

            self.streams[e].append(emit)

    def emit_all(self):
        with self.nc.Block() as block:
            for e, attr in ENG_ATTR.items():
                stream = self.streams[e]

                def body(en, stream=stream):
                    for f in stream:
                        f(en)

                getattr(block, attr)(body)


class Arena:
    def __init__(self, nc, es, name, nbytes):
        self.t = es.enter_context(nc.sbuf_tensor(name, [128, nbytes // 4], F32))
        self.cap = nbytes
        self.off = 0
        self.peak = 0

    def alloc(self, free_shape, dtype, parts=128):
        free_shape = [int(s) for s in free_shape]
        n = int(np.prod(free_shape))
        size = n * _DTSIZE[dtype]
        size = (size + 3) // 4 * 4
        off = (self.off + 63) // 64 * 64
        assert off + size <= self.cap, ("SBUF arena overflow", off, size, self.cap)
        self.off = off + size
        self.peak = max(self.peak, self.off)
        ap = self.t[0:parts, off // 4:(off + size) // 4]
        if dtype != F32:
            ap = ap.bitcast(dtype)
            ap = ap[:, 0:n]
        if len(free_shape) > 1:
            names = [chr(ord("a") + i) for i in range(len(free_shape))]
            kw = {names[i]: free_shape[i] for i in range(len(free_shape) - 1)}
            ap = ap.rearrange("p (" + " ".join(names) + ") -> p " + " ".join(names), **kw)
        return ap

    def mark(self):
        return self.off

    def release(self, m):
        self.off = m


D = 1024
DS = 512
G = 32
NST = 64
T = 16
CTX = 256
NCC = CTX // T
DFF = 2816
NF = DFF // 128
KC = 31
EPS = 1e-6
LN_EPS = 1e-5
TWO_PI = float(2 * np.pi)

_SM = {}
_off = 0
for _n, _w in [("cc", 16), ("n1g", 8), ("n2g", 8), ("s5d", 4), ("bglu", 4), ("confb", 4), ("lng", 4), ("lnb", 4),
               ("confw", 4 * KC), ("fcw", NF * 3), ("fcb", NF), ("ident", 128), ("bmask", 128),
               ("lamr", 32), ("lami", 32), ("ldt", 32)]:
    _SM[_n] = (_off, _w)
    _off += _w
NSM = _off


def build(L, dbg=False, stop_after=None):
    assert L % 1024 == 0
    NX = L // T
    N = NX + NCC
    NS = N // 16
    assert N % 16 == 0
    NT = L // 512
    nc = bass.Bass("TRN2", target_bir_lowering=False)
    dt_in = lambda name, shape: nc.dram_tensor(name, shape, F32, kind="ExternalInput").ap()
    x_d = dt_in("x", [L, D])
    ctx_d = dt_in("ctx", [CTX, D])
    smalls = dt_in("smalls", [128, NSM])
    s5b_d = dt_in("s5b", [128, 2, 512])
    s5c_d = dt_in("s5c", [128, 2, 512])
    bmod_d = dt_in("bmod", [1, 6 * D])
    fg_d = dt_in("fg", [1, D])
    wmod_d = dt_in("wmod", [128, 12, 8 * 512])
    win_d = dt_in("win", [128, 8 * 1536])
    wglu_d = dt_in("wglu", [128, 4 * 512])
    wout_d = dt_in("wout", [128, 8, 1024])
    wup_d = dt_in("wup", [NF * 128 * 2, 1024])
    wdn_d = dt_in("wdn", [128, NF, 1024])
    y_d = nc.dram_tensor("y", [L, D], F32, kind="ExternalOutput").ap()
    ikind = "ExternalOutput" if dbg else "Internal"
    u_d = nc.dram_tensor("u_d", [DS, L], BF16, kind=ikind).ap()
    Sd = nc.dram_tensor("Sd", [4, 128, N, 16], F32, kind=ikind).ap()
    Hd = nc.dram_tensor("Hd", [128, 64 * N], BF16, kind=ikind).ap()
    mix_d = nc.dram_tensor("mix_d", [DS, L], BF16, kind=ikind).ap()
    h1_d = nc.dram_tensor("h1_d", [L, D], F32, kind=ikind).ap()
    wu_d = nc.dram_tensor("wu_d", [NF * 128 * 2, 1024], BF16, kind="Internal").ap()
    wo_d = nc.dram_tensor("wo_d", [128, 8 * 1024], BF16, kind="Internal").ap()
    wd_d = nc.dram_tensor("wd_d", [128, NF * 1024], BF16, kind="Internal").ap()
    hw_d = nc.dram_tensor("hw_d", [128, 16 * 2 * 2 * 16 * 32], BF16, kind="Internal").ap()
    lw_d = nc.dram_tensor("lw_d", [128, 2 * 4 * 16 * 128], BF16, kind="Internal").ap()
    sw_d = nc.dram_tensor("sw_d", [128, 2 * 4 * 16 * 2 * 128], BF16, kind="Internal").ap()
    mc_d = nc.dram_tensor("mc_d", [128, 512], F32, kind=ikind).ap()

    es = ExitStack()
    P = Prog(nc, es)

    def finish():
        P.barrier()
        P.emit_all()
        print("[build] ops per engine:", P.nops, "arena peak KB:", A.peak / 1024, flush=True)
        es.close()
        return nc

    A = Arena(nc, es, "arena", 204 * 1024)
    pst = es.enter_context(nc.psum_tensor("ps", [128, 4096], F32))
    ps = pst[:, :]
    psb = ps.bitcast(BF16)
    bank_ctr = [0]

    def nb():
        b = bank_ctr[0] % 8
        bank_ctr[0] += 1
        return b

    def nbg(n):
        b = (bank_ctr[0] + n - 1) // n * n % 8
        bank_ctr[0] = (bank_ctr[0] + n - 1) // n * n + n
        return b

    def B(b, w=512):
        return ps[:, b * 512:b * 512 + w]

    def mm(out, lhsT, rhs, start, stop, reads, writes, signal=None, **kw):
        if signal is None:
            signal = stop
        P.op("pe", lambda e: e.matmul(out, lhsT=lhsT, rhs=rhs, start=start, stop=stop, **kw),
             reads=reads, writes=writes, signal=signal)

    def tt(eng, out, in0, in1, op, reads, writes):
        P.op(eng, lambda e: e.tensor_tensor(out=out, in0=in0, in1=in1, op=op), reads=reads, writes=writes)

    def ts(eng, out, in0, s1, s2, op0, op1, reads, writes):
        if s2 is None:
            P.op(eng, lambda e: e.tensor_scalar(out=out, in0=in0, scalar1=s1, scalar2=None, op0=op0), reads=reads, writes=writes)
        else:
            P.op(eng, lambda e: e.tensor_scalar(out=out, in0=in0, scalar1=s1, scalar2=s2, op0=op0, op1=op1), reads=reads, writes=writes)

    def stt(eng, out, in0, scalar, in1, op0, op1, reads, writes):
        P.op(eng, lambda e: e.scalar_tensor_tensor(out=out, in0=in0, scalar=scalar, in1=in1, op0=op0, op1=op1), reads=reads, writes=writes)

    def act(out, in_, func, reads, writes, **kw):
        P.op("act", lambda e: e.activation(out=out, in_=in_, func=func, **kw), reads=reads, writes=writes)

    def cp(eng, out, in_, reads, writes):
        if eng == "act":
            P.op("act", lambda e: e.copy(out=out, in_=in_), reads=reads, writes=writes)
        else:
            P.op(eng, lambda e: e.tensor_copy(out=out, in_=in_), reads=reads, writes=writes)

    sm = A.alloc([NSM], F32)
    P.dma("sp", sm, smalls[:, :], writes=["sm"], semkey="sm")

    def SMv(name):
        o, w = _SM[name]
        return sm[:, o:o + w]

    ident = SMv("ident")
    bmask = SMv("bmask")
    mods = A.alloc([96], F32)
    mcol = mods[:, 0:48]
    mxcol = mods[:, 48:64]
    a1, a1x, a2 = mods[:, 64:72], mods[:, 72:80], mods[:, 80:88]
    sh1, sh1x, sh2 = mcol[:, 0:8], mxcol[:, 0:8], mcol[:, 24:32]
    identb = A.alloc([128], BF16)
    ones512 = A.alloc([128], BF16)
    scanMA = A.alloc([16, 64], F32)
    scanMB = A.alloc([16, 64], F32)
    cp("dve", identb, ident, ["sm"], ["identb"])
    P.op("pool", lambda e: e.memset(ones512, 1.0 / 512.0), writes=["ones512"])
    persist_mark = A.mark()

    p1_mark = A.mark()
    cs = A.alloc([8, 2], F32)
    st = A.alloc([2, 8, 128], F32)
    gbc = A.alloc([2 * D], F32)
    mch = [A.alloc([512], F32) for _ in range(4)]
    bch = [A.alloc([512], F32) for _ in range(4)]
    tmp4 = A.alloc([4, 128], F32)
    wm = [A.alloc([8, 512], F32) for _ in range(4)]
    wtmp = [A.alloc([1024], F32) for _ in range(2)]
    wst = [A.alloc([1024], BF16) for _ in range(2)]
    act(cs, SMv("cc").rearrange("p (k j) -> p k j", j=2), AF.Silu, ["sm"], ["cs"])
    for j in range(2):
        cp("dve", st[:, j], cs[:, :, j:j + 1].to_broadcast([128, 8, 128]), ["cs"], [("st", j)])
    ident4 = ident.unsqueeze(1).to_broadcast([128, 4, 128])

    p0a_ctr = [0]

    def p0a_load(n):
        sl = p0a_ctr[0] % 4
        p0a_ctr[0] += 1
        P.dma("sp", wm[sl], wmod_d[:, n, :].rearrange("p (k c) -> p k c", k=8), writes=[("wm", sl)], semkey=("wm", sl))
        P.dma("sp", bch[sl], bmod_d[:, n * 512:(n + 1) * 512].partition_broadcast(128), writes=[("bch", sl)], semkey=("bch", sl))
        return sl

    def p0a_chunk(n, sl):
        b = nb()
        for k in range(8):
            mm(B(b), st[:, 0, k, :], wm[sl][:, k, :], k == 0, k == 7, [("st", 0), ("wm", sl)], [("ps", b)])
        if n in (4, 5, 10, 11):
            o0 = {4: 0, 5: 512, 10: 1024, 11: 1536}[n]
            dst, dk = gbc[:, o0:o0 + 512], "gbc"
        else:
            dst, dk = mch[sl], ("mch", sl)
        tt("dve", dst, B(b), bch[sl], ALU.add, [("ps", b), ("bch", sl)], [dk])
        tt("dve", tmp4, dst.rearrange("p (j i) -> p j i", i=128), ident4, ALU.mult, [dk, "sm"], ["tmp4"])
        P.op("dve", lambda e: e.tensor_reduce(out=mcol[:, 4 * n:4 * n + 4], in_=tmp4, axis=AX.X, op=ALU.add), reads=["tmp4"], writes=["mods"])
        if n < 4:
            b = nb()
            for k in range(8):
                mm(B(b), st[:, 1, k, :], wm[sl][:, k, :], k == 0, k == 7, [("st", 1), ("wm", sl)], [("ps", b)])
            tt("dve", mch[sl], B(b), bch[sl], ALU.add, [("ps", b), ("bch", sl)], [("mch", sl)])
            tt("dve", tmp4, mch[sl].rearrange("p (j i) -> p j i", i=128), ident4, ALU.mult, [("mch", sl), "sm"], ["tmp4"])
            P.op("dve", lambda e: e.tensor_reduce(out=mxcol[:, 4 * n:4 * n + 4], in_=tmp4, axis=AX.X, op=ALU.add), reads=["tmp4"], writes=["mods"])

    def p0a_tail():
        stt("dve", a1, mcol[:, 8:16], 1.0, SMv("n1g"), ALU.add, ALU.mult, ["mods", "sm"], ["mods"])
        stt("dve", a1x, mxcol[:, 8:16], 1.0, SMv("n1g"), ALU.add, ALU.mult, ["mods", "sm"], ["mods"])
        stt("dve", a2, mcol[:, 32:40], 1.0, SMv("n2g"), ALU.add, ALU.mult, ["mods", "sm"], ["mods"])
        if dbg:
            P.dma("pool", mc_d[:, 0:96], mods, reads=["mods"], semkey="mcd")

    Swst = [A.alloc([2, 4, 2, 128], BF16) for _ in range(2)]
    sw_dv = sw_d.rearrange("p (d g k r c) -> p d g k r c", d=2, g=4, k=16, r=2)
    Hst = [A.alloc([2, 2, 16, 32], BF16) for _ in range(2)]
    Lst = [A.alloc([2, 4, 128], BF16) for _ in range(2)]
    hw_dv = hw_d.rearrange("p (g j r d q c) -> p g j r d q c", g=4, j=16, r=2, d=2, q=4)
    lw_dv = lw_d.rearrange("p (d g t c) -> p d g t c", d=2, g=4, t=16)
    XB = A.alloc([2, 2, 512], F32)
    XB2 = A.alloc([2, 2, 512], F32)
    Mx2 = [A.alloc([2, 1024], BF16) for _ in range(2)]
    Cx = A.alloc([2, 1024], BF16)
    s5t = A.alloc([24, 32], F32)
    s5i = A.alloc([32], I32)
    pw = A.alloc([2, 2, 32], F32)
    t4 = A.alloc([4, 2, 512], F32)
    P.dma("sp", XB[:, :, 0, :], s5b_d[:, :, :], writes=["XB"], semkey="s5b")
    P.dma("sp", XB[:, :, 1, :], s5c_d[:, :, :], writes=["XB"], semkey="s5c")
    for i_ in range(2):
        P.op("pool", lambda e, i_=i_: e.memset(Hst[i_].rearrange("p d r q c -> p (d r q c)"), 0.0), writes=[("Hst", i_)])
    for i_ in range(2):
        P.op("pool", lambda e, i_=i_: e.memset(Mx2[i_].rearrange("p r c -> p (r c)"), 0.0), writes=[("Mx", i_)])
    P.op("pool", lambda e: e.memset(Cx.rearrange("p r c -> p (r c)"), 0.0), writes=["Cx"])
    lr, li, ldt = SMv("lamr"), SMv("lami"), SMv("ldt")
    R = lambda i: s5t[:, i, :]
    S5 = ["s5t"]
    act(R(0), ldt, AF.Exp, ["sm"], S5)
    tt("dve", R(1), lr, R(0), ALU.mult, ["sm"] + S5, S5)
    tt("dve", R(2), li, R(0), ALU.mult, ["sm"] + S5, S5)
    act(R(3), R(1), AF.Exp, S5, S5)
    ts("dve", R(4), R(2), 1.0 / TWO_PI, None, ALU.mult, None, S5, S5)
    cp("dve", s5i, R(4), S5, ["s5i"])
    cp("dve", R(4), s5i, ["s5i"], S5)
    stt("dve", R(5), R(4), -TWO_PI, R(2), ALU.mult, ALU.add, S5, S5)
    act(R(6), R(5), AF.Sin, S5, S5, scale=0.5)
    act(R(7), R(5), AF.Sin, S5, S5, scale=0.5, bias=float(np.pi / 2))
    stt("dve", R(8), R(6), 2.0, R(7), ALU.mult, ALU.mult, S5, S5)
    tt("dve", R(9), R(6), R(6), ALU.mult, S5, S5)
    ts("dve", R(9), R(9), -2.0, 1.0, ALU.mult, ALU.add, S5, S5)
    lamr, lami = pw[:, 0, 0, :], pw[:, 0, 1, :]
    LAM = A.alloc([2, 32], F32)
    tt("dve", LAM[:, 0, :], R(3), R(9), ALU.mult, S5, ["LAM"])
    tt("dve", LAM[:, 1, :], R(3), R(8), ALU.mult, S5, ["LAM"])
    lam_r, lam_i = LAM[:, 0, :], LAM[:, 1, :]
    ts("dve", R(10), lam_r, -1.0, None, ALU.add, None, ["LAM"], S5)
    tt("dve", R(11), lr, lr, ALU.mult, ["sm"], S5)
    tt("dve", R(12), li, li, ALU.mult, ["sm"], S5)
    tt("dve", R(11), R(11), R(12), ALU.add, S5, S5)
    P.op("dve", lambda e: e.reciprocal(out=R(12), in_=R(11)), reads=S5, writes=S5)
    tt("dve", R(13), R(10), lr, ALU.mult, ["sm"] + S5, S5)
    tt("dve", R(14), lam_i, li, ALU.mult, ["sm", "LAM"], S5)
    tt("dve", R(13), R(13), R(14), ALU.add, S5, S5)
    tt("dve", R(15), R(13), R(12), ALU.mult, S5, S5)
    tt("dve", R(13), lam_i, lr, ALU.mult, ["sm", "LAM"], S5)
    tt("dve", R(14), R(10), li, ALU.mult, ["sm"] + S5, S5)
    tt("dve", R(13), R(13), R(14), ALU.subtract, S5, S5)
    tt("dve", R(16), R(13), R(12), ALU.mult, S5, S5)

    def cmul(outr, outi, ar, ai, br, bi, rd, wr, tmp):
        tt("dve", tmp[0], ar, br, ALU.mult, rd, ["cm0"])
        tt("dve", tmp[1], ai, bi, ALU.mult, rd, ["cm1"])
        tt("dve", tmp[2], ar, bi, ALU.mult, rd, ["cm2"])
        tt("dve", tmp[3], ai, br, ALU.mult, rd, ["cm3"])
        tt("dve", outr, tmp[0], tmp[1], ALU.subtract, ["cm0", "cm1"], wr)
        tt("dve", outi, tmp[2], tmp[3], ALU.add, ["cm2", "cm3"], wr)

    def bc16(a):
        return a.unsqueeze(2).to_broadcast([128, 32, 16])

    v16 = lambda a: a.rearrange("p (x c) -> p x c", c=16)
    tm512 = [v16(t4[:, i, 0, :]) for i in range(4)]
    cmul(v16(XB2[:, 0, 0, :]), v16(XB2[:, 1, 0, :]), bc16(R(15)), bc16(R(16)), v16(XB[:, 0, 0, :]), v16(XB[:, 1, 0, :]), S5 + ["XB"], ["XB2"], tm512)
    cp("dve", XB[:, :, 0, :], XB2[:, :, 0, :], ["XB2"], ["XB"])
    Cxv = Cx.rearrange("p r (x m c) -> p r x m c", m=2, c=16)
    Mxv2 = [m_.rearrange("p r (x m c) -> p r x m c", m=2, c=16) for m_ in Mx2]
    for m in range(2):
        pr = slice(64 * m, 64 * m + 64)
        cp("act", Cxv[pr, 0, :, m, :], v16(XB[pr, 0, 1, :]), ["XB"], ["Cx"])
        act(Cxv[pr, 1, :, m, :], v16(XB[pr, 1, 1, :]), AF.Copy, ["XB"], ["Cx"], scale=-1.0)
    Hstv = [h.rearrange("p d r q (m c) -> p d r q m c", m=2) for h in Hst]
    Xkeys = ["XB", "XB2"]
    tm1024 = [t4[:, i].rearrange("p w (x c) -> p w x c", c=16) for i in range(4)]
    lam_bc = lambda a: a.unsqueeze(1).unsqueeze(3).to_broadcast([128, 2, 32, 16])
    def p0s_iter(k):
        ck, nk = Xkeys[k % 2], Xkeys[(k + 1) % 2]
        Xc, Xn = (XB, XB2) if k % 2 == 0 else (XB2, XB)
        if k <= 15:
            for m in range(2):
                pr = slice(64 * m, 64 * m + 64)
                for r in range(2):
                    cp("act", Mxv2[k % 2][pr, r, :, m, :], v16(Xc[pr, r, 0, :]), [ck], [("Mx", k % 2)])
            for r in range(2):
                for d in range(2):
                    b = nb()
                    for gt in range(4):
                        c0 = (d * 16 + gt * 4) * 32
                        P.op("pe", lambda e, b=b, gt=gt, c0=c0, r=r: e.transpose(out=psb[:, b * 1024 + gt * 128:b * 1024 + (gt + 1) * 128], in_=Mx2[k % 2][:, r, c0:c0 + 128], identity=identb),
                             reads=[("Mx", k % 2), "identb"], writes=[("ps", b)], signal=(gt == 3))
                    cp("act", Swst[k % 2][:, d, :, r, :], psb[:, b * 1024:b * 1024 + 512].rearrange("p (g c) -> p g c", g=4), [("ps", b)], [("Swst", k % 2)])
            P.dma("pool", sw_dv[:, :, :, k, :, :], Swst[k % 2], reads=[("Swst", k % 2)], writes=["sw_d"], semkey=("Swst", k % 2))
            for d in range(2):
                b = nb()
                for gt in range(4):
                    c0 = (d * 16 + gt * 4) * 32
                    mm(B(b)[:, gt * 128:(gt + 1) * 128], Mx2[k % 2][:, 0, c0:c0 + 128], Cx[:, 0, c0:c0 + 128], True, False, [("Mx", k % 2), "Cx"], [("ps", b)], signal=False)
                    mm(B(b)[:, gt * 128:(gt + 1) * 128], Mx2[k % 2][:, 1, c0:c0 + 128], Cx[:, 1, c0:c0 + 128], False, True, [("Mx", k % 2), "Cx"], [("ps", b)], signal=(gt == 3))
                tt("dve", Lst[k % 2][:, d, :, :], B(b).rearrange("p (g c) -> p g c", g=4), bmask.unsqueeze(1).to_broadcast([128, 4, 128]), ALU.mult,
                   [("ps", b), "sm"], [("Lst", k % 2)])
            P.dma("pool", lw_dv[:, :, :, k, :], Lst[k % 2], reads=[("Lst", k % 2)], writes=["lw_d"], semkey=("Lst", k % 2))
        if k >= 1:
            kk = k - 1
            for d in range(2):
                j = kk if d == 0 else 15 - kk
                for m in range(2):
                    pr = slice(64 * m, 64 * m + 64)
                    src = lambda r: Xc[pr, r, 1, d * 256:(d + 1) * 256].rearrange("p (q c) -> p q c", c=16)
                    cp("act", Hstv[k % 2][pr, d, 0, :, m, :], src(0), [ck], [("Hst", k % 2)])
                    act(Hstv[k % 2][pr, d, 1, :, m, :], src(1), AF.Copy, [ck], [("Hst", k % 2)], scale=-1.0)
                for r in range(2):
                    P.dma("pool", hw_dv[:, :, j, r, d, :, :].rearrange("p g q c -> p g (q c)"), Hst[k % 2][:, d, r].rearrange("p (g q) c -> p g (q c)", g=4),
                          reads=[("Hst", k % 2)], writes=["hw_d"], semkey=("Hst", k % 2))
        if k < 16:
            cmul(Xn[:, 0].rearrange("p w (x c) -> p w x c", c=16), Xn[:, 1].rearrange("p w (x c) -> p w x c", c=16),
                 lam_bc(lam_r), lam_bc(lam_i),
                 Xc[:, 0].rearrange("p w (x c) -> p w x c", c=16), Xc[:, 1].rearrange("p w (x c) -> p w x c", c=16),
                 [ck, "LAM"], [nk], tm1024)
    def pw_gen():
        tm32 = [t4[:, i, 0, 0:32] for i in range(4)]
        cp("dve", pw[:, 0], LAM, ["LAM"], [("pw", 0)])
        cur = 0
        for _ in range(15):
            cmul(pw[:, 1 - cur, 0, :], pw[:, 1 - cur, 1, :], lam_r, lam_i, pw[:, cur, 0, :], pw[:, cur, 1, :], ["LAM", ("pw", cur)], [("pw", 1 - cur)], tm32)
            yield
            cur = 1 - cur
        MU = A.alloc([2, 32], F32)
        cp("dve", MU, pw[:, cur], [("pw", cur)], ["MU"])
        cp("dve", pw[:, 0], MU, ["MU"], [("pw", 0)])
        cur = 0
        MAv = scanMA.rearrange("p e (qt r q) -> p e qt r q", r=2, q=8)
        MBv = scanMB.rearrange("p e (qt r q) -> p e qt r q", r=2, q=8)
        for e_ in range(16):
            pr_ = pw[:, cur, 0, :].rearrange("p (qt q) -> p qt q", q=8)
            pi_ = pw[:, cur, 1, :].rearrange("p (qt q) -> p qt q", q=8)
            for r in range(2):
                cp("act", MAv[:, e_, :, r, :], pr_, [("pw", cur)], ["scanM"])
            act(MBv[:, e_, :, 0, :], pi_, AF.Copy, [("pw", cur)], ["scanM"], scale=-1.0)
            cp("act", MBv[:, e_, :, 1, :], pi_, [("pw", cur)], ["scanM"])
            if e_ < 15:
                cmul(pw[:, 1 - cur, 0, :], pw[:, 1 - cur, 1, :], MU[:, 0, :], MU[:, 1, :], pw[:, cur, 0, :], pw[:, cur, 1, :], ["MU", ("pw", cur)], [("pw", 1 - cur)], tm32)
                yield
                cur = 1 - cur

        yield
    fold_q = []

    def fold_step(i):
        sl = i % 2
        src = wout_d[:, i, :] if i < 8 else wdn_d[:, i - 8, :]
        gate = gbc[:, 0:D] if i < 8 else gbc[:, D:2 * D]
        dst = wo_d[:, i * 1024:(i + 1) * 1024] if i < 8 else wd_d[:, (i - 8) * 1024:(i - 7) * 1024]
        P.dma("sp", wtmp[sl], src, writes=[("wtmp", sl)], semkey=("wtmp", sl))
        tt("dve", wst[sl], wtmp[sl], gate, ALU.mult, [("wtmp", sl), "gbc"], [("wst", sl)])
        P.dma("pool", dst, wst[sl], reads=[("wst", sl)], writes=["wo_d" if i < 8 else "wd_d"], semkey=("wst", sl))

    pwg = pw_gen()
    pw_done = [False]

    def pw_steps(n_):
        for _ in range(n_):
            if not pw_done[0]:
                try:
                    next(pwg)
                except StopIteration:
                    pw_done[0] = True

    kq = list(range(17))
    order = [0, 1, 2, 3, 4, 5, 10, 11, 6, 7, 8, 9]
    slq = [p0a_load(order[i_]) for i_ in range(3)]
    for oi, n in enumerate(order):
        if oi + 3 < len(order):
            slq.append(p0a_load(order[oi + 3]))
        p0a_chunk(n, slq.pop(0))
        if n == 5:
            fold_q.extend(range(8))
        if n == 11:
            fold_q.extend(range(8, 8 + NF))
        for _ in range(2 if n % 2 == 0 else 1):
            if kq:
                p0s_iter(kq.pop(0))
        pw_steps(3)
        for _ in range(4):
            if fold_q:
                fold_step(fold_q.pop(0))
    while kq:
        p0s_iter(kq.pop(0))
        pw_steps(3)
    pw_steps(1000)
    while fold_q:
        fold_step(fold_q.pop(0))
    p0a_tail()
    P.barrier()
    A.release(p1_mark)
    if stop_after == "p0s":
        return finish()

    def norm_part(xsl, xkey, nsub, xn, xnkey, junk, ssb, ssbkey):
        ss, rs = ssb[:, 0:nsub], ssb[:, 4:4 + nsub]
        for s_ in range(nsub):
            act(junk, xsl[:, s_, :], AF.Square, [xkey], ["junk", ssbkey], accum_out=ss[:, s_:s_ + 1])
        ts("dve", rs, ss, 1.0 / D, EPS, ALU.mult, ALU.add, [ssbkey], [ssbkey])
        act(rs, rs, AF.Sqrt, [ssbkey], [ssbkey])
        P.op("dve", lambda e: e.reciprocal(out=rs, in_=rs), reads=[ssbkey], writes=[ssbkey])
        for s_ in range(nsub):
            ts("dve", xn[:, s_, :], xsl[:, s_, :], rs[:, s_:s_ + 1], None, ALU.mult, None, [xkey, ssbkey], [xnkey])

    def tr_part(nsub, a_col, sh_col, hn, hnkey, xn, xnkey):
        for kp in range(4):
            b = nb()
            for kk in range(2):
                k = 2 * kp + kk
                for s_ in range(nsub):
                    o0 = b * 1024 + kk * 512 + s_ * 128
                    P.op("pe", lambda e, o0=o0, s_=s_, k=k: e.transpose(out=psb[:, o0:o0 + 128], in_=xn[:, s_, k * 128:(k + 1) * 128], identity=identb),
                         reads=[xnkey, "identb"], writes=[("ps", b)], signal=(kk == 1 and s_ == nsub - 1))
            for kk in range(2):
                k = 2 * kp + kk
                o0 = b * 1024 + kk * 512
                if kk == 0:
                    ts("dve", hn[:, k, 0:nsub * 128], psb[:, o0:o0 + nsub * 128], a_col[:, k:k + 1], sh_col[:, k:k + 1], ALU.mult, ALU.add,
                       [("ps", b), "mods"], [hnkey])
                else:
                    act(hn[:, k, 0:nsub * 128], psb[:, o0:o0 + nsub * 128], AF.Identity, [("ps", b), "mods"], [hnkey],
                        scale=a_col[:, k:k + 1], bias=sh_col[:, k:k + 1])

    def norm_to_fm(xsl, xkey, nsub, a_col, sh_col, hn, hnkey, xn, junk, ssb):
        norm_part(xsl, xkey, nsub, xn, "xn", junk, ssb, "ssb")
        tr_part(nsub, a_col, sh_col, hn, hnkey, xn, "xn")

    NCT = NCC + NX
    us_jm = A.alloc([4, 16, NCT], BF16)
    p12_mark = A.mark()
    win_bf = A.alloc([8, 512], BF16)
    P.dma("pool", win_bf, win_d[:, :].rearrange("p (k c) -> p k c", k=8)[:, :, 0:512], writes=["win_bf"], semkey="win_bf")
    xs = [A.alloc([4, 1024], F32) for _ in range(2)]
    xn = A.alloc([4, 1024], BF16)
    hns = [A.alloc([8, 512], BF16) for _ in range(2)]
    junk = A.alloc([1024], BF16)
    ssb = A.alloc([8], F32)
    xn1 = [xn, A.alloc([4, 1024], BF16)]
    ssb1 = [ssb, A.alloc([8], F32)]

    def p1_load(ti):
        sl = ti % 2
        if ti == 0:
            P.dma("sp", xs[sl][:, 0:2, :], ctx_d.rearrange("(s p) d -> p s d", p=128), writes=[("xs", sl)], semkey=("xs", sl))
        else:
            t0 = (ti - 1) * 512
            P.dma("sp", xs[sl], x_d[t0:t0 + 512, :].rearrange("(s p) d -> p s d", p=128), writes=[("xs", sl)], semkey=("xs", sl))

    for hf in range(4):
        r0 = hf * (NF * 128 * 2 // 4)
        r1 = (hf + 1) * (NF * 128 * 2 // 4)
        P.dma("pool", wu_d[r0:r1, :], wup_d[r0:r1, :], writes=[("wu_d", hf)], semkey=("wucast", hf))
    p1_load(0)
    p1_load(1)
    for ti in range(NT + 1):
        sl = ti % 2
        isctx = ti == 0
        nsub = 2 if isctx else 4
        ntok = nsub * 128
        nch = ntok // T
        cof = 0 if isctx else NCC + (ti - 1) * 32
        if ti == 0:
            norm_part(xs[0], ("xs", 0), 2, xn1[0], ("xn1", 0), junk, ssb1[0], ("ssb1", 0))
        tr_part(nsub, a1x if isctx else a1, sh1x if isctx else sh1, hns[sl], ("hn", sl), xn1[sl], ("xn1", sl))
        if ti + 1 <= NT:
            nsl = (ti + 1) % 2
            norm_part(xs[nsl], ("xs", nsl), 4, xn1[nsl], ("xn1", nsl), junk, ssb1[nsl], ("ssb1", nsl))
        for gt in range(4):
            b = nb()
            for k in range(8):
                mm(B(b)[:, 0:ntok], win_bf[:, k, gt * 128:(gt + 1) * 128], hns[sl][:, k, 0:ntok], k == 0, k == 7, ["win_bf", ("hn", sl)], [("ps", b)])
            cp("act", us_jm[:, gt, :, cof:cof + nch], B(b)[:, 0:ntok].rearrange("p (c j) -> p j c", j=16), [("ps", b)], [("us", ti)])
        if ti + 2 <= NT:
            p1_load(ti + 2)
    u_dv = u_d.rearrange("(g p) (j c) -> p g j c", p=128, j=16)
    for gt in range(4):
        P.dma("pool", u_dv[:, gt], us_jm[:, gt, :, NCC:NCT], reads=[("us", ti) for ti in range(NT + 1)], writes=["u_d"], semkey=("ust", gt))
    P.barrier()
    A.release(p12_mark)
    if stop_after == "p1":
        return finish()

    uskeys = [("us", ti) for ti in range(NT + 1)]
    p3a_mark = A.mark()
    Swq1 = A.alloc([2, 16, 2, 128], BF16)
    Swq = [Swq1, Swq1]
    Wp = A.alloc([N, 32], F32)
    Hs = A.alloc([NS + 1, 32], F32)
    sct = A.alloc([2, NS, 32], F32)
    fx = [A.alloc([NS, 32], F32) for _ in range(2)]
    MA2 = A.alloc([16, 2, 32], F32)
    MB2 = A.alloc([16, 2, 32], F32)
    Hdv = Hd.rearrange("p (h d g e n) -> p h d g e n", h=2, d=2, g=2, e=8)
    for h in range(2):
        for r in range(2):
            srcA = scanMA.rearrange("p e (d h r q) -> p e d h r q", d=2, h=2, r=2)[:, :, :, h, r, :]
            srcB = scanMB.rearrange("p e (d h r q) -> p e d h r q", d=2, h=2, r=2)[:, :, :, h, r, :]
            cp("act", MA2[:, :, h, r * 16:(r + 1) * 16].rearrange("p e (d q) -> p e d q", d=2), srcA, ["scanM"], ["MA2"])
            cp("act", MB2[:, :, h, r * 16:(r + 1) * 16].rearrange("p e (d q) -> p e d q", d=2), srcB, ["scanM"], ["MA2"])

    def swp(a):
        return a.rearrange("p s (r q) -> p s r q", r=2)[:, :, ::-1, :]

    def v4(a):
        return a.rearrange("p s (r q) -> p s r q", r=2)

    wk = "Wp"
    for qh in range(2):
        Hq = us_jm[:, 2 * qh:2 * qh + 2].rearrange("p a (b e) n -> p a b e n", b=2)
        for d in range(2):
            P.dma("sp", Swq[d], sw_dv[:, d, 2 * qh:2 * qh + 2], reads=["sw_d"], writes=["Swq"], semkey="Swq")
            if d == 0:
                wx = Wp[:, NCC:N, :]
                wc = Wp[:, 0:NCC, :]
            else:
                wx = Wp[:, NCC:N, :][:, ::-1, :]
                wc = Wp[:, 0:NCC, :][:, ::-1, :]
            gbc = nbg(4)
            ckeys = [("ps", gbc + i) for i in range(4)]
            cnt = 0
            for r in range(2):
                for g2 in range(2):
                    gt = 2 * qh + g2
                    blk = r * 2 + g2
                    for i in range(16):
                        k = 15 - i if d == 0 else i
                        for qi in range(4):
                            cnt += 1
                            mm(B(gbc + qi)[:, blk * NCC:(blk + 1) * NCC], Swq[d][32 * qi:32 * qi + 32, g2, k, r, :], us_jm[32 * qi:32 * qi + 32, gt, i, 0:NCC],
                               i == 0, i == 15, ["Swq", ("us", 0), ("usrd", qh)], ckeys, signal=(cnt == 256), tile_position=(32 * qi, 0), skip_group_check=True)
            regc = ps[:, gbc * 512:(gbc + 4) * 512].rearrange("p (qi x) -> p qi x", qi=4)[:, :, 0:4 * NCC].rearrange("p qi (r g c) -> p r c g qi", r=2, g=2)
            for r in range(2):
                c0 = r * 16 + d * 8
                cp("act", wc[:, :, c0:c0 + 8].rearrange("p c (g qi) -> p c g qi", qi=4), regc[:, r], ckeys, [wk])
            for r in range(2):
                for g2 in range(2):
                    gt = 2 * qh + g2
                    gb = nbg(4)
                    skeys = [("ps", gb + i) for i in range(4)]
                    for i in range(16):
                        k = 15 - i if d == 0 else i
                        for qi in range(4):
                            mm(B(gb + qi)[:, 0:NX], Swq[d][32 * qi:32 * qi + 32, g2, k, r, :], us_jm[32 * qi:32 * qi + 32, gt, i, NCC:NCT],
                               i == 0, i == 15, ["Swq", ("usrd", qh)] + uskeys, skeys, signal=(i == 15 and qi == 3), tile_position=(32 * qi, 0), skip_group_check=True)
                    c0 = r * 16 + d * 8 + g2 * 4
                    src = ps[:, gb * 512:(gb + 4) * 512].rearrange("p (qi x) -> p qi x", qi=4)[:, :, 0:NX].rearrange("p qi n -> p n qi")
                    cp("act", wx[:, :, c0:c0 + 4], src, skeys, [wk])
        eng = "dve"
        Wv = Wp.rearrange("p (s r) c -> p s r c", r=16)
        MAe = lambda e_, n_: MA2[:, e_, qh, :].unsqueeze(1).to_broadcast([128, n_, 32])
        MBe = lambda e_, n_: v4(MB2[:, e_, qh, :].unsqueeze(1).to_broadcast([128, n_, 32]))
        t1, t2 = sct[:, 0], sct[:, 1]
        for step in range(1, 16):
            prev, cur_ = Wv[:, :, step - 1, :], Wv[:, :, step, :]
            tt(eng, t1, prev, MAe(0, NS), ALU.mult, [wk, "MA2"], ["sct1"])
            tt(eng, v4(t2), swp(prev), MBe(0, NS), ALU.mult, [wk, "MA2"], ["sct2"])
            tt(eng, t1, t1, t2, ALU.add, ["sct1", "sct2"], ["sct1"])
            tt(eng, cur_, cur_, t1, ALU.add, [wk, "sct1"], [wk])
        P.op(eng, lambda e: e.memset(Hs.rearrange("p s c -> p (s c)"), 0.0), writes=["Hs"])
        for s_ in range(NS):
            hp, hd_ = Hs[:, s_:s_ + 1, :], Hs[:, s_ + 1:s_ + 2, :]
            a1_, a2_ = sct[:, 0, 0:1, :], sct[:, 1, 0:1, :]
            tt(eng, a1_, hp, MAe(15, 1), ALU.mult, ["Hs", "MA2"], ["sct1"])
            tt(eng, v4(a2_), swp(hp), MBe(15, 1), ALU.mult, ["Hs", "MA2"], ["sct2"])
            tt(eng, a1_, a1_, a2_, ALU.add, ["sct1", "sct2"], ["sct1"])
            tt(eng, hd_, Wv[:, s_, 15:16, :], a1_, ALU.add, [wk, "sct1"], ["Hs"])
        hsp = Hs[:, 0:NS, :]
        for r_ in range(16):
            tt(eng, t1, hsp, MAe(r_, NS), ALU.mult, ["Hs", "MA2"], ["sct1"])
            tt(eng, v4(t2), swp(hsp), MBe(r_, NS), ALU.mult, ["Hs", "MA2"], ["sct2"])
            tt(eng, t1, t1, t2, ALU.add, ["sct1", "sct2"], ["sct1"])
            fxb = fx[r_ % 2]
            tt(eng, fxb, Wv[:, :, r_, :], t1, ALU.add, [wk, "sct1"], [("fx", r_ % 2)])
            fxv = fxb.rearrange("p s (r d g q) -> p s r d g q", r=2, d=2, g=2)
            for d in range(2):
                for g2 in range(2):
                    hv = Hq[:, d, g2]
                    if d == 1:
                        hv = hv[:, :, ::-1]
                    dstv = hv.rearrange("p (ri q) (s r) -> p s r ri q", ri=2, r=16)[:, :, r_, :, :]
                    cp("act" if (d + g2) % 2 == 0 else "pool", dstv, fxv[:, :, :, d, g2, :], [("fx", r_ % 2)], ["Hq", ("usrd", qh)])
        P.dma("pool", Hdv[:, qh], Hq, reads=["Hq"], writes=["Hd"], semkey="Hq")
    P.barrier()
    A.release(persist_mark)
    if stop_after == "p2":
        return finish()

    dg31 = A.alloc([4, KC, 128], BF16)
    p3b_mark = A.mark()
    cwv = SMv("confw").rearrange("p (m k) -> p m k", k=KC)
    dgq = [(m, k) for m in range(4) for k in range(KC)]
    ub = [A.alloc([16, NX], BF16) for _ in range(2)]
    Hg = [A.alloc([2, 8, N], BF16) for _ in range(2)]
    Hwg = [A.alloc([16, 2, 2, 4, 32], BF16) for _ in range(2)]
    lwg = [A.alloc([2, 16, 128], BF16) for _ in range(2)]
    ysb = [A.alloc([NX, 4], F32) for _ in range(2)]
    ygs = A.alloc([L], BF16)
    hw_dv2 = hw_d.rearrange("p (g x) -> p g x", g=4)
    lw_dv2 = lw_d.rearrange("p (d g t c) -> p d g t c", d=2, g=4, t=16)
    mix_dv = mix_d.rearrange("(g p) t -> p g t", p=128)
    Dcol = SMv("s5d")

    def p3a_load(gt):
        sl = gt % 2
        P.dma("sp", ub[sl], u_dv[:, gt], reads=["u_d"], writes=[("ub", sl)], semkey=("ub", sl))
        P.dma("sp", Hg[sl], Hdv[:, gt // 2, :, gt % 2], reads=["Hd"], writes=[("Hg", sl)], semkey=("Hg", sl))
        P.dma("sp", Hwg[sl].rearrange("p j r d q c -> p (j r d q c)"), hw_dv2[:, gt, :], reads=["hw_d"], writes=[("Hwg", sl)], semkey=("Hwg", sl))
        P.dma("sp", lwg[sl], lw_dv2[:, :, gt], reads=["lw_d"], writes=[("lwg", sl)], semkey=("lwg", sl))

    p3a_load(0)
    for gt in range(4):
        sl = gt % 2
        if gt + 1 < 4:
            p3a_load(gt + 1)
        ux = ub[sl]
        rd = [("ub", sl), ("lwg", sl), ("Hwg", sl), ("Hg", sl)]
        for jq in range(4):
            gb = nbg(4)
            ykeys = [("ps", gb + i) for i in range(4)]
            for jl in range(4):
                j = 4 * jq + jl
                ob = B(gb + jl)[:, 0:NX]
                mm(ob, lwg[sl][:, 0, 0, :], ux[:, j, :], True, False, rd, ykeys, signal=False, skip_group_check=True)
                for tau in range(1, j + 1):
                    mm(ob, lwg[sl][:, 0, tau, :], ux[:, j - tau, :], False, False, rd, ykeys, signal=False, skip_group_check=True)
                for tau in range(0, 16 - j):
                    mm(ob, lwg[sl][:, 1, tau, :], ux[:, j + tau, :], False, False, rd, ykeys, signal=False, skip_group_check=True)
                cnt = 0
                for d in range(2):
                    n0 = NCC - 1 if d == 0 else 1
                    for r in range(2):
                        for qi in range(4):
                            cnt += 1
                            last = (cnt == 16 and jl == 3)
                            mm(B(gb + jl)[32 * qi:32 * qi + 32, 0:NX], Hwg[sl][:, j, r, d, qi, :], Hg[sl][:, d, r * 4 + qi, n0:n0 + NX], False, cnt == 16, rd, ykeys,
                               signal=last, tile_position=(0, 32 * qi), skip_group_check=True)
            ysl = jq % 2
            pin = ps[:, gb * 512:(gb + 4) * 512].rearrange("p (jl x) -> p jl x", jl=4)[:, :, 0:NX].rearrange("p jl c -> p c jl")
            uin = ux[:, 4 * jq:4 * jq + 4, :].rearrange("p j c -> p c j")
            stt("dve", ysb[ysl], uin, Dcol[:, gt:gt + 1], pin, ALU.mult, ALU.add, [("ub", sl), "sm"] + ykeys, [("ysb", ysl)])
            act(ygs.rearrange("p (c j) -> p c j", j=16)[:, :, 4 * jq:4 * jq + 4], ysb[ysl], AF.Gelu_apprx_tanh, [("ysb", ysl)], ["ygs"])
            for _ in range(8):
                if dgq:
                    m_, k_ = dgq.pop(0)
                    act(dg31[:, m_, k_, :], ident, AF.Copy, ["sm"], ["dg31"], scale=cwv[:, m_, k_:k_ + 1])
        P.dma("pool", mix_dv[:, gt, :], ygs, reads=["ygs"], writes=["mix_d"], semkey="ygs")
    assert not dgq
    P.barrier()
    A.release(p3b_mark)
    if stop_after == "p3a":
        return finish()

    wglu = A.alloc([4, 512], BF16)
    P.dma("pool", wglu, wglu_d[:, :].rearrange("p (k c) -> p k c", k=4), writes=["wglu"], semkey="wglu")
    bglu = SMv("bglu")
    mixg = A.alloc([4, 512], BF16)
    sgg = None
    wvg = A.alloc([8, 1024], BF16)
    wo_bf = A.alloc([8, 1024], BF16)
    P.dma("pool", wvg, win_d[:, :].rearrange("p (k c) -> p k c", k=8)[:, :, 512:1536], writes=["wvg"], semkey="wvg")
    P.dma("sp", wo_bf.rearrange("p k c -> p (k c)"), wo_d[:, :], reads=["wo_d"], writes=["wo_bf"], semkey="wo_bf")
    xs = [A.alloc([4, 1024], F32) for _ in range(2)]
    xn = A.alloc([4, 1024], BF16)
    hn = A.alloc([8, 512], BF16)
    junk = A.alloc([1024], BF16)
    ssb = A.alloc([8], F32)
    cbuf = A.alloc([4, 8, 64 + KC - 1], BF16)
    sgm = [A.alloc([512], F32) for _ in range(2)]
    sgg = sgm
    hcb = A.alloc([4, 512], BF16)
    sq2 = A.alloc([4, 512], BF16)
    sq = A.alloc([4, 512], BF16)
    lnt = A.alloc([4, 512], F32)
    mixc = A.alloc([4, 512], BF16)
    msb = [A.alloc([4, 512], BF16) for _ in range(2)]
    P.op("pool", lambda e: e.memset(cbuf.rearrange("p m r c -> p (m r c)"), 0.0), writes=["cbuf"])
    confb, lng, lnb = SMv("confb"), SMv("lng"), SMv("lnb")

    def p3b1_load(ti):
        sl = ti % 2
        t0 = ti * 512
        P.dma("sp", xs[sl], x_d[t0:t0 + 512, :].rearrange("(s p) d -> p s d", p=128), writes=[("xs", sl)], semkey=("xs", sl))
        P.dma("sp", msb[sl], mix_dv[:, :, t0:t0 + 512], reads=["mix_d"], writes=[("msb", sl)], semkey=("msb", sl))

    hc2 = [hcb, sq2]
    mixg2 = [mixg, A.alloc([4, 512], BF16)]
    lnm = [A.alloc([512], F32) for _ in range(2)]
    lne = [A.alloc([512], F32) for _ in range(2)]
    xs3 = xs + [A.alloc([4, 1024], F32)]

    def ld3(ti):
        t0 = ti * 512
        P.dma("sp", xs3[ti % 3], x_d[t0:t0 + 512, :].rearrange("(s p) d -> p s d", p=128), writes=[("xs", ti % 3)], semkey=("xs", ti % 3))

    def ldm(ti):
        t0 = ti * 512
        P.dma("sp", msb[ti % 2], mix_dv[:, :, t0:t0 + 512], reads=["mix_d"], writes=[("msb", ti % 2)], semkey=("msb", ti % 2))

    def st_norm(ti):
        norm_part(xs3[ti % 3], ("xs", ti % 3), 4, xn, "xn", junk, ssb, "ssb")

    def st_tr(ti):
        tr_part(4, a1, sh1, hn, "hn", xn, "xn")

    def st_a1(ti):
        for m in range(4):
            bg, bv = nb(), nb()
            for k in range(8):
                mm(B(bg), wvg[:, k, 512 + m * 128:512 + (m + 1) * 128], hn[:, k, :], k == 0, k == 7, ["wvg", "hn"], [("ps", bg)])
            for k in range(8):
                mm(B(bv), wvg[:, k, m * 128:(m + 1) * 128], hn[:, k, :], k == 0, k == 7, ["wvg", "hn"], [("ps", bv)])
            act(sgm[m % 2], B(bg), AF.Sigmoid, [("ps", bg)], [("sgm", m % 2)])
            tt("dve", cbuf[:, m, :, 15:79], B(bv).rearrange("p (r c) -> p r c", c=64), sgm[m % 2].rearrange("p (r c) -> p r c", c=64), ALU.mult,
               [("ps", bv), ("sgm", m % 2), "cbuf"], [("cbuf", m)])

    def st_a2(ti):
        sl = ti % 2
        for m in range(4):
            b = nb()
            for k in range(KC):
                mm(B(b).rearrange("p (r c) -> p r c", c=64), dg31[:, m, k, :], cbuf[:, m, :, k:k + 64], k == 0, k == KC - 1, ["dg31", ("cbuf", m), "cbuf"], [("ps", b)])
            act(hc2[sl][:, m, :], B(b), AF.Identity, [("ps", b), "sm"], [("hc", sl, m)], bias=confb[:, m:m + 1])
            act(sq[:, m, :], B(b), AF.Square, [("ps", b), "sm"], [("sq", m)], bias=confb[:, m:m + 1])

    def st_a3(ti):
        sl = ti % 2
        bm_, be_ = nb(), nb()
        for m in range(4):
            mm(B(bm_), ones512, hc2[sl][:, m, :], m == 0, m == 3, ["ones512", ("hc", sl, m)], [("ps", bm_)])
        for m in range(4):
            mm(B(be_), ones512, sq[:, m, :], m == 0, m == 3, ["ones512", ("sq", m)], [("ps", be_)])
        cp("act", lnm[sl], B(bm_), [("ps", bm_)], [("lnm", sl)])
        cp("act", lne[sl], B(be_), [("ps", be_)], [("lne", sl)])
        for m in range(4):
            b = nb()
            for k in range(4):
                mm(B(b), wglu[:, k, m * 128:(m + 1) * 128], msb[sl][:, k, :], k == 0, k == 3, ["wglu", ("msb", sl)], [("ps", b)])
            act(sgg[m % 2], B(b), AF.Sigmoid, [("ps", b), "sm"], [("sgm", m % 2)], bias=bglu[:, m:m + 1])
            tt("pool", mixg2[sl][:, m, :], msb[sl][:, m, :], sgg[m % 2], ALU.mult, [("msb", sl), ("sgm", m % 2)], [("mixg", sl, m)])

    def st_b1(ti):
        sl = ti % 2
        mean_sb, msq, rstd_ = lnm[sl], lnt[:, 1, :], lnt[:, 2, :]
        tt("dve", msq, mean_sb, mean_sb, ALU.mult, [("lnm", sl)], ["ln_msq"])
        tt("dve", msq, lne[sl], msq, ALU.subtract, [("lne", sl), "ln_msq"], ["ln_msq"])
        ts("dve", msq, msq, LN_EPS, None, ALU.add, None, ["ln_msq"], ["ln_msq"])
        act(rstd_, msq, AF.Sqrt, ["ln_msq"], ["ln_rstd"])
        lts = [lnt[:, 0, :], lnt[:, 3, :]]
        for m in range(2):
            tt("dve", lts[m], hc2[sl][:, m, :], mean_sb, ALU.subtract, [("hc", sl, m), ("lnm", sl)], [("ln_t", m)])
        P.op("dve", lambda e: e.reciprocal(out=rstd_, in_=rstd_), reads=["ln_rstd"], writes=["ln_rstd"])
        for m in range(4):
            lt_, lk = lts[m % 2], ("ln_t", m % 2)
            if m >= 2:
                tt("dve", lt_, hc2[sl][:, m, :], mean_sb, ALU.subtract, [("hc", sl, m), ("lnm", sl)], [lk])
            tt("dve", lt_, lt_, rstd_, ALU.mult, [lk, "ln_rstd"], [lk])
            act(mixc[:, m, :], lt_, AF.Silu, [lk, "sm"], [("mixc", m)], scale=lng[:, m:m + 1], bias=lnb[:, m:m + 1])

    def st_b2(ti):
        sl = ti % 2
        x3 = xs3[ti % 3]
        xk = ("xs", ti % 3)
        t0 = ti * 512
        for s_ in range(4):
            for hf in range(2):
                b = nb()
                for k in range(8):
                    lh = mixg2[sl][:, k, s_ * 128:(s_ + 1) * 128] if k < 4 else mixc[:, k - 4, s_ * 128:(s_ + 1) * 128]
                    rk = ("mixg", sl, k) if k < 4 else ("mixc", k - 4)
                    mm(B(b), lh, wo_bf[:, k, hf * 512:(hf + 1) * 512], k == 0, k == 7, [rk, "wo_bf"], [("ps", b)])
                tt("dve", x3[:, s_, hf * 512:(hf + 1) * 512], B(b), x3[:, s_, hf * 512:(hf + 1) * 512], ALU.add, [("ps", b), xk], [xk])
        P.dma("pool", h1_d[t0:t0 + 512, :].rearrange("(s p) d -> p s d", p=128), x3, reads=[xk], writes=["h1_d"], semkey=("h1st", ti % 3))

    for ti in range(min(3, NT)):
        ld3(ti)
    for ti in range(min(2, NT)):
        ldm(ti)
    st_norm(0)
    st_tr(0)
    st_a1(0)
    st_a2(0)
    if NT > 1:
        st_norm(1)
        st_tr(1)
    st_a3(0)
    for ti in range(NT):
        if ti + 1 < NT:
            st_a1(ti + 1)
        st_b1(ti)
        if ti + 2 < NT:
            st_norm(ti + 2)
        if ti + 1 < NT:
            st_a2(ti + 1)
        st_b2(ti)
        if ti + 2 < NT:
            st_tr(ti + 2)
        if ti + 1 < NT:
            st_a3(ti + 1)
        if ti + 3 < NT:
            ld3(ti + 3)
        if ti + 2 < NT:
            ldm(ti + 2)
    P.barrier()
    A.release(persist_mark)
    if stop_after == "p3b1":
        return finish()

    wd_bf = A.alloc([NF, 1024], BF16)
    fg_bc = A.alloc([1024], F32)
    fwv = SMv("fcw").rearrange("p (f k) -> p f k", k=3)
    fcb = SMv("fcb")
    hs_ = [A.alloc([4, 1024], F32) for _ in range(2)]
    xn = A.alloc([4, 1024], BF16)
    hn2 = A.alloc([8, 512], BF16)
    junk = A.alloc([1024], BF16)
    ssb = A.alloc([8], F32)
    ssf = A.alloc([8], F32)
    sgf = [A.alloc([512], F32) for _ in range(3)]
    actb = A.alloc([NF, 512], BF16)
    wus = [A.alloc([8, 256], BF16) for _ in range(3)]
    ot = [A.alloc([4, 1024], F32) for _ in range(1)]
    wu_dv = wu_d.rearrange("(f p h) c -> f p (h c)", p=128, h=2)
    def p3b2_load(ti):
        sl = ti % 2
        t0 = ti * 512
        P.dma("pool", hs_[sl], h1_d[t0:t0 + 512, :].rearrange("(s p) d -> p s d", p=128), reads=["h1_d"], writes=[("hs", sl)], semkey=("hs", sl))

    wu_ctr = [0]

    def wu_load(f_):
        sl3 = wu_ctr[0] % 3
        wu_ctr[0] += 1
        P.dma("sp", wus[sl3].rearrange("p k c -> p (k c)"), wu_dv[f_], reads=[("wu_d", i_) for i_ in range(4)], writes=[("wus", sl3)], semkey=("wus", sl3))
        return sl3

    xns = [xn, A.alloc([4, 1024], BF16)]
    p3b2_load(0)
    if NT > 1:
        p3b2_load(1)
    pend = [wu_load(0), wu_load(1)]
    P.dma("sp", wd_bf.rearrange("p f c -> p (f c)"), wd_d[:, :], reads=["wd_d"], writes=["wd_bf"], semkey="wd_bf")
    P.dma("sp", fg_bc, fg_d.partition_broadcast(128), writes=["fg_bc"], semkey="fg_bc")
    norm_part(hs_[0], ("hs", 0), 4, xns[0], ("xn2", 0), junk, ssb, "ssb")
    tr_part(4, a2, sh2, hn2, "hn2", xns[0], ("xn2", 0))

    cvb = [A.alloc([512], F32) for _ in range(3)]

    def ffn_tail1(f_, bv, bg, gs):
        cv = cvb[gs]
        cvv = cv.rearrange("p (r c) -> p r c", c=64)
        gv = B(bg).rearrange("p (r c) -> p r c", c=64)
        act(cv, B(bg), AF.Copy, [("ps", bg), "sm"], [("cvb", gs)], scale=fwv[:, f_, 1:2])
        stt("dve", cvv[:, :, 1:64], gv[:, :, 0:63], fwv[:, f_, 0:1], cvv[:, :, 1:64], ALU.mult, ALU.add, [("ps", bg), "sm", ("cvb", gs)], [("cvb", gs)])
        stt("dve", cvv[:, :, 0:63], gv[:, :, 1:64], fwv[:, f_, 2:3], cvv[:, :, 0:63], ALU.mult, ALU.add, [("ps", bg), "sm", ("cvb", gs)], [("cvb", gs)])
        act(sgf[gs], cv, AF.Silu, [("cvb", gs), "sm"], [("sgf", gs)], bias=fcb[:, f_:f_ + 1])

    def ffn_tail2(f_, bv, bg, gs):
        tt("dve", actb[:, f_, :], B(bv), sgf[gs], ALU.mult, [("ps", bv), ("sgf", gs)], [("actb", f_)])

    for ti in range(NT):
        sl = ti % 2
        t0 = ti * 512
        prev = None
        prev2 = None
        for f_ in range(NF):
            w3 = pend.pop(0)
            nxt = ti * NF + f_ + 2
            if nxt < NT * NF:
                pend.append(wu_load(nxt % NF))
            bv, bg = nb(), nb()
            for k in range(8):
                mm(B(bg), wus[w3][:, k, 128:256], hn2[:, k, :], k == 0, k == 7, [("wus", w3), "hn2"], [("ps", bg)])
            for k in range(8):
                mm(B(bv), wus[w3][:, k, 0:128], hn2[:, k, :], k == 0, k == 7, [("wus", w3), "hn2"], [("ps", bv)])
            gs = f_ % 3
            if prev is not None:
                ffn_tail1(*prev)
            if prev2 is not None:
                ffn_tail2(*prev2)
            prev2 = prev
            prev = (f_, bv, bg, gs)
            if f_ == 11 and ti + 1 < NT:
                nsl = (ti + 1) % 2
                norm_part(hs_[nsl], ("hs", nsl), 4, xns[nsl], ("xn2", nsl), junk, ssb, "ssb")
        ffn_tail1(*prev)
        ffn_tail2(*prev2)
        ffn_tail2(*prev)
        if ti + 1 < NT:
            tr_part(4, a2, sh2, hn2, "hn2", xns[nsl], ("xn2", nsl))
        for s_ in range(4):
            for hf in range(2):
                b = nb()
                for f_ in range(NF):
                    mm(B(b), actb[:, f_, s_ * 128:(s_ + 1) * 128], wd_bf[:, f_, hf * 512:(hf + 1) * 512], f_ == 0, f_ == NF - 1, [("actb", f_), "wd_bf"], [("ps", b)])
                tt("dve", hs_[sl][:, s_, hf * 512:(hf + 1) * 512], B(b), hs_[sl][:, s_, hf * 512:(hf + 1) * 512], ALU.add, [("ps", b), ("hs", sl)], [("hs", sl)])
        ss, rs = ssf[:, 0:4], ssf[:, 4:8]
        for s_ in range(4):
            act(junk, hs_[sl][:, s_, :], AF.Square, [("hs", sl)], ["junk", "ssf"], accum_out=ss[:, s_:s_ + 1])
        ts("dve", rs, ss, 1.0 / D, EPS, ALU.mult, ALU.add, ["ssf"], ["ssf"])
        act(rs, rs, AF.Sqrt, ["ssf"], ["ssf"])
        P.op("dve", lambda e: e.reciprocal(out=rs, in_=rs), reads=["ssf"], writes=["ssf"])
        for s_ in range(4):
            stt("dve", ot[0][:, s_, :], hs_[sl][:, s_, :], rs[:, s_:s_ + 1], fg_bc, ALU.mult, ALU.mult, [("hs", sl), "ssf", "fg_bc"], ["ot"])
        P.dma("pool", y_d[t0:t0 + 512, :].rearrange("(s p) d -> p s d", p=128), ot[0], reads=["ot"], writes=["y_d"], semkey="yst")
        if ti + 2 < NT:
            p3b2_load(ti + 2)
    P.barrier()
    return finish()


def _col(v, nt):
    return np.ascontiguousarray(np.asarray(v, np.float32).reshape(nt, 128).T)


def _pair32(a):
    a = np.asarray(a, np.float32).reshape(2, 16, 2, 64)
    return np.ascontiguousarray(a.transpose(2, 3, 0, 1).reshape(128, 32))


def prep_shared(inp):
    f = lambda k: np.asarray(inp[k], np.float32)
    sh = {}
    sm = np.zeros((128, NSM), np.float32)

    def put(name, arr):
        o, w = _SM[name]
        sm[:, o:o + w] = arr.reshape(128, w)

    put("n1g", _col(f("norm1_g")[0], 8))
    put("n2g", _col(f("norm2_g")[0], 8))
    put("s5d", _col(f("s5_d")[0], 4))
    put("bglu", _col(f("b_glu")[0], 4))
    put("confb", _col(f("conf_b")[0], 4))
    put("lng", _col(f("conf_ln_g")[0], 4))
    put("lnb", _col(f("conf_ln_b")[0], 4))
    cw = f("conf_w")[0]
    put("confw", np.ascontiguousarray(cw.T.reshape(4, 128, KC).transpose(1, 0, 2)))
    fw = f("ffn_conv_w")[0]
    put("fcw", np.ascontiguousarray(fw.T.reshape(NF, 128, 3).transpose(1, 0, 2)))
    put("fcb", _col(f("ffn_conv_b")[0], NF))
    put("ident", np.eye(128, dtype=np.float32))
    bm = (np.arange(128)[:, None] // 16 == np.arange(128)[None, :] // 16).astype(np.float32)
    put("bmask", bm)
    put("lamr", _pair32(f("s5_lam_re")[0]))
    put("lami", _pair32(f("s5_lam_im")[0]))
    put("ldt", _pair32(np.broadcast_to(f("s5_log_dt")[0][:, :, None], (2, 32, 64))))
    sh["sm"] = sm

    def pairB(a):
        a = a.reshape(2, 16, 2, 64, 16)
        return a.transpose(2, 3, 0, 1, 4).reshape(128, 512)

    def pairC(a):
        a = a.reshape(2, 16, 2, 16, 64)
        return a.transpose(2, 4, 0, 1, 3).reshape(128, 512)

    sh["s5b"] = np.ascontiguousarray(np.stack([pairB(f("s5_b_re")[0]), pairB(f("s5_b_im")[0])], 1))
    sh["s5c"] = np.ascontiguousarray(np.stack([pairC(f("s5_c_re")[0]), pairC(f("s5_c_im")[0])], 1))
    sh["bmod"] = np.ascontiguousarray(f("b_mod")[0].reshape(1, 6 * D))
    sh["fg"] = np.ascontiguousarray(f("final_g").reshape(1, D))
    wm = f("w_mod")[0]
    sh["wmod"] = np.ascontiguousarray(wm.reshape(8, 128, 12, 512).transpose(1, 2, 0, 3).reshape(128, 12, 8 * 512))
    sh["win"] = np.ascontiguousarray(f("w_in")[0].reshape(8, 128, 1536).transpose(1, 0, 2).reshape(128, 8 * 1536))
    sh["wglu"] = np.ascontiguousarray(f("w_glu")[0].reshape(4, 128, 512).transpose(1, 0, 2).reshape(128, 4 * 512))
    sh["wout"] = np.ascontiguousarray(f("w_out")[0].reshape(8, 128, 1024).transpose(1, 0, 2))
    wu = f("ffn_w_up")[0]
    wu = wu.reshape(8, 128, 2, NF, 128)
    sh["wup"] = np.ascontiguousarray(wu.transpose(3, 1, 0, 2, 4).reshape(NF * 128 * 2, 1024))
    sh["wdn"] = np.ascontiguousarray(f("ffn_w_down")[0].reshape(NF, 128, 1024).transpose(1, 0, 2))
    return sh


def prep_core(inp, sh, b):
    m = {k: v for k, v in sh.items() if k != "sm"}
    sm = sh["sm"].copy()
    cc = np.stack([np.asarray(inp["c"], np.float32)[b], np.asarray(inp["c_ctx"], np.float32)], 1)
    o, w = _SM["cc"]
    sm[:, o:o + w] = cc.reshape(8, 128, 2).transpose(1, 0, 2).reshape(128, 16)
    m["smalls"] = sm
    m["x"] = np.ascontiguousarray(np.asarray(inp["x"], np.float32)[b])
    m["ctx"] = np.ascontiguousarray(np.asarray(inp["ctx"], np.float32)[b])
    return m


_NC_CACHE = {}


def kernel(**inputs):
    nb_, L = inputs["x"].shape[0], inputs["x"].shape[1]
    if L not in _NC_CACHE:
        _NC_CACHE[L] = build(L)
    nc = _NC_CACHE[L]
    sh = prep_shared(inputs)
    in_maps = [prep_core(inputs, sh, b) for b in range(nb_)]
    res = run_bass_kernel_spmd(nc, in_maps, core_ids=list(range(nb_)))
    return np.stack([np.asarray(r["y"], np.float32) for r in res.results], 0)
```

```python
import numpy as np
from contextlib import ExitStack
import concourse.bass as bass
import concourse.mybir as mybir
from concourse.bass_utils import run_bass_kernel_spmd

F32 = mybir.dt.float32
BF16 = mybir.dt.bfloat16
I32 = mybir.dt.int32
AF = mybir.ActivationFunctionType
ALU = mybir.AluOpType
AX = mybir.AxisListType

ENG_ATTR = {"pe": "tensor", "act": "scalar", "dve": "vector", "pool": "gpsimd", "sp": "sync"}
_DTSIZE = {F32: 4, BF16: 2, I32: 4}


class Prog:
    def __init__(self, nc, es):
        self.nc = nc
        self.es = es
        self.streams = {e: [] for e in ENG_ATTR}
        self.sem = {}
        self.tick = {}
        self.waited = {e: {} for e in ENG_ATTR}
        self.w = {}
        self.r = {}
        self.nops = {e: 0 for e in ENG_ATTR}
        for e in ENG_ATTR:
            self._mksem(("eng", e))

    def _mksem(self, key):
        if key not in self.sem:
            name = "s_" + "_".join(str(k) for k in key)
            self.sem[key] = self.es.enter_context(self.nc.semaphore(name))
            self.tick[key] = 0

    def _deps(self, eng, reads, writes):
        deps = {}

        def add(ev):
            k, v = ev
            if k == ("eng", "pe") and eng == "pe":
                return
            if deps.get(k, 0) < v:
                deps[k] = v

        for r in reads:
            if r in self.w:
                add(self.w[r])
        for w_ in writes:
            if w_ in self.w:
                add(self.w[w_])
            for k, v in self.r.get(w_, {}).items():
                add((k, v))
        out = []
        for k, v in deps.items():
            if self.waited[eng].get(k, 0) < v:
                self.waited[eng][k] = v
                out.append((self.sem[k], v))
        return out

    def _commit(self, ev, reads, writes):
        for r in reads:
            d = self.r.setdefault(r, {})
            if d.get(ev[0], 0) < ev[1]:
                d[ev[0]] = ev[1]
        for w_ in writes:
            self.w[w_] = ev
            self.r[w_] = {}

    def op(self, eng, fn, reads=(), writes=(), signal=True):
        waits = self._deps(eng, reads, writes)
        key = ("eng", eng)
        if signal:
            self.tick[key] += 1
            ev = (key, self.tick[key])
        else:
            ev = (key, self.tick[key] + 1)
        self._commit(ev, reads, writes)
        sem = self.sem[key]
        self.nops[eng] += 1

        def emit(e):
            for s, v in waits:
                e.wait_ge(s, v)
            ins = fn(e)
            if signal:
                ins.then_inc(sem, 1)

        self.streams[eng].append(emit)

    def dma(self, q, out, in_, reads=(), writes=(), semkey=None, **kw):
        key = ("dma", semkey)
        self._mksem(key)
        waits = self._deps(q, reads, writes)
        self.tick[key] += 16
        ev = (key, self.tick[key])
        self._commit(ev, reads, writes)
        sem = self.sem[key]
        self.nops[q] += 1

        def emit(e):
            for s, v in waits:
                e.wait_ge(s, v)
            e.dma_start(out=out, in_=in_, **kw).then_inc(sem, 16)

        self.streams[q].append(emit)

    def barrier(self, engines=None):
        for e in (engines or ENG_ATTR):
            waits = []
            for k, v in self.tick.items():
                if v > 0 and k != ("eng", e) and self.waited[e].get(k, 0) < v:
                    self.waited[e][k] = v
                    waits.append((self.sem[k], v))

            def emit(en, waits=waits):
                for s, v in waits:
                    en.wait_ge(s, v)

            self.streams[e].append(emit)

    def emit_all(self):
        with self.nc.Block() as block:
            for e, attr in ENG_ATTR.items():
                stream = self.streams[e]

                def body(en, stream=stream):
                    for f in stream:
                        f(en)

                getattr(block, attr)(body)


class Arena:
    def __init__(self, nc, es, name, nbytes):
        self.t = es.enter_context(nc.sbuf_tensor(name, [128, nbytes // 4], F32))
        self.cap = nbytes
        self.off = 0
        self.peak = 0

    def alloc(self, free_shape, dtype, parts=128):
        free_shape = [int(s) for s in free_shape]
        n = int(np.prod(free_shape))
        size = n * _DTSIZE[dtype]
        size = (size + 3) // 4 * 4
        off = (self.off + 63) // 64 * 64
        assert off + size <= self.cap, ("SBUF arena overflow", off, size, self.cap)
        self.off = off + size
        self.peak = max(self.peak, self.off)
        ap = self.t[0:parts, off // 4:(off + size) // 4]
        if dtype != F32:
            ap = ap.bitcast(dtype)
            ap = ap[:, 0:n]
        if len(free_shape) > 1:
            names = [chr(ord("a") + i) for i in range(len(free_shape))]
            kw = {names[i]: free_shape[i] for i in range(len(free_shape) - 1)}
            ap = ap.rearrange("p (" + " ".join(names) + ") -> p " + " ".join(names), **kw)
        return ap

    def mark(self):
        return self.off

    def release(self, m):
        self.off = m


D = 1024
DS = 512
G = 32
NST = 64
T = 16
CTX = 256
NCC = CTX // T
DFF = 2816
NF = DFF // 128
KC = 31
EPS = 1e-6
LN_EPS = 1e-5
TWO_PI = float(2 * np.pi)

_SM = {}
_off = 0
for _n, _w in [("cc", 16), ("n1g", 8), ("n2g", 8), ("s5d", 4), ("bglu", 4), ("confb", 4), ("lng", 4), ("lnb", 4),
               ("confw", 4 * KC), ("fcw", NF * 3), ("fcb", NF), ("ident", 128), ("bmask", 128),
               ("lamr", 32), ("lami", 32), ("ldt", 32)]:
    _SM[_n] = (_off, _w)
    _off += _w
NSM = _off


def build(L, dbg=False, stop_after=None):
    assert L % 1024 == 0
    NX = L // T
    N = NX + NCC
    NS = N // 16
    assert N % 16 == 0
    NT = L // 512
    nc = bass.Bass("TRN2", target_bir_lowering=False)
    dt_in = lambda name, shape: nc.dram_tensor(name, shape, F32, kind="ExternalInput").ap()
    x_d = dt_in("x", [L, D])
    ctx_d = dt_in("ctx", [CTX, D])
    smalls = dt_in("smalls", [128, NSM])
    s5b_d = dt_in("s5b", [128, 2, 512])
    s5c_d = dt_in("s5c", [128, 2, 512])
    bmod_d = dt_in("bmod", [1, 6 * D])
    fg_d = dt_in("fg", [1, D])
    wmod_d = dt_in("wmod", [128, 12, 8 * 512])
    win_d = dt_in("win", [128, 8 * 1536])
    wglu_d = dt_in("wglu", [128, 4 * 512])
    wout_d = dt_in("wout", [128, 8, 1024])
    wup_d = dt_in("wup", [NF * 128 * 2, 1024])
    wdn_d = dt_in("wdn", [128, NF, 1024])
    y_d = nc.dram_tensor("y", [L, D], F32, kind="ExternalOutput").ap()
    ikind = "ExternalOutput" if dbg else "Internal"
    u_d = nc.dram_tensor("u_d", [DS, L], BF16, kind=ikind).ap()
    Sd = nc.dram_tensor("Sd", [4, 128, N, 16], F32, kind=ikind).ap()
    Hd = nc.dram_tensor("Hd", [128, 64 * N], BF16, kind=ikind).ap()
    mix_d = nc.dram_tensor("mix_d", [DS, L], BF16, kind=ikind).ap()
    h1_d = nc.dram_tensor("h1_d", [L, D], F32, kind=ikind).ap()
    wu_d = nc.dram_tensor("wu_d", [NF * 128 * 2, 1024], BF16, kind="Internal").ap()
    wo_d = nc.dram_tensor("wo_d", [128, 8 * 1024], BF16, kind="Internal").ap()
    wd_d = nc.dram_tensor("wd_d", [128, NF * 1024], BF16, kind="Internal").ap()
    hw_d = nc.dram_tensor("hw_d", [128, 16 * 2 * 2 * 16 * 32], BF16, kind="Internal").ap()
    lw_d = nc.dram_tensor("lw_d", [128, 2 * 4 * 16 * 128], BF16, kind="Internal").ap()
    sw_d = nc.dram_tensor("sw_d", [128, 2 * 4 * 16 * 2 * 128], BF16, kind="Internal").ap()
    mc_d = nc.dram_tensor("mc_d", [128, 512], F32, kind=ikind).ap()

    es = ExitStack()
    P = Prog(nc, es)

    def finish():
        P.barrier()
        P.emit_all()
        print("[build] ops per engine:", P.nops, "arena peak KB:", A.peak / 1024, flush=True)
        es.close()
        return nc

    A = Arena(nc, es, "arena", 204 * 1024)
    pst = es.enter_context(nc.psum_tensor("ps", [128, 4096], F32))
    ps = pst[:, :]
    psb = ps.bitcast(BF16)
    bank_ctr = [0]

    def nb():
        b = bank_ctr[0] % 8
        bank_ctr[0] += 1
        return b

    def nbg(n):
        b = (bank_ctr[0] + n - 1) // n * n % 8
        bank_ctr[0] = (bank_ctr[0] + n - 1) // n * n + n
        return b

    def B(b, w=512):
        return ps[:, b * 512:b * 512 + w]

    def mm(out, lhsT, rhs, start, stop, reads, writes, signal=None, **kw):
        if signal is None:
            signal = stop
        P.op("pe", lambda e: e.matmul(out, lhsT=lhsT, rhs=rhs, start=start, stop=stop, **kw),
             reads=reads, writes=writes, signal=signal)

    def tt(eng, out, in0, in1, op, reads, writes):
        P.op(eng, lambda e: e.tensor_tensor(out=out, in0=in0, in1=in1, op=op), reads=reads, writes=writes)

    def ts(eng, out, in0, s1, s2, op0, op1, reads, writes):
        if s2 is None:
            P.op(eng, lambda e: e.tensor_scalar(out=out, in0=in0, scalar1=s1, scalar2=None, op0=op0), reads=reads, writes=writes)
        else:
            P.op(eng, lambda e: e.tensor_scalar(out=out, in0=in0, scalar1=s1, scalar2=s2, op0=op0, op1=op1), reads=reads, writes=writes)

    def stt(eng, out, in0, scalar, in1, op0, op1, reads, writes):
        P.op(eng, lambda e: e.scalar_tensor_tensor(out=out, in0=in0, scalar=scalar, in1=in1, op0=op0, op1=op1), reads=reads, writes=writes)

    def act(out, in_, func, reads, writes, **kw):
        P.op("act", lambda e: e.activation(out=out, in_=in_, func=func, **kw), reads=reads, writes=writes)

    def cp(eng, out, in_, reads, writes):
        if eng == "act":
            P.op("act", lambda e: e.copy(out=out, in_=in_), reads=reads, writes=writes)
        else:
            P.op(eng, lambda e: e.tensor_copy(out=out, in_=in_), reads=reads, writes=writes)

    sm = A.alloc([NSM], F32)
    P.dma("sp", sm, smalls[:, :], writes=["sm"], semkey="sm")

    def SMv(name):
        o, w = _SM[name]
        return sm[:, o:o + w]

    ident = SMv("ident")
    bmask = SMv("bmask")
    mods = A.alloc([96], F32)
    mcol = mods[:, 0:48]
    mxcol = mods[:, 48:64]
    a1, a1x, a2 = mods[:, 64:72], mods[:, 72:80], mods[:, 80:88]
    sh1, sh1x, sh2 = mcol[:, 0:8], mxcol[:, 0:8], mcol[:, 24:32]
    identb = A.alloc([128], BF16)
    ones512 = A.alloc([128], BF16)
    scanMA = A.alloc([16, 64], F32)
    scanMB = A.alloc([16, 64], F32)
    cp("dve", identb, ident, ["sm"], ["identb"])
    P.op("pool", lambda e: e.memset(ones512, 1.0 / 512.0), writes=["ones512"])
    persist_mark = A.mark()

    p1_mark = A.mark()
    cs = A.alloc([8, 2], F32)
    st = A.alloc([2, 8, 128], F32)
    gbc = A.alloc([2 * D], F32)
    mch = [A.alloc([512], F32) for _ in range(4)]
    bch = [A.alloc([512], F32) for _ in range(4)]
    tmp4 = A.alloc([4, 128], F32)
    wm = [A.alloc([8, 512], F32) for _ in range(4)]
    wtmp = [A.alloc([1024], F32) for _ in range(2)]
    wst = [A.alloc([1024], BF16) for _ in range(2)]
    act(cs, SMv("cc").rearrange("p (k j) -> p k j", j=2), AF.Silu, ["sm"], ["cs"])
    for j in range(2):
        cp("dve", st[:, j], cs[:, :, j:j + 1].to_broadcast([128, 8, 128]), ["cs"], [("st", j)])
    ident4 = ident.unsqueeze(1).to_broadcast([128, 4, 128])

    p0a_ctr = [0]

    def p0a_load(n):
        sl = p0a_ctr[0] % 4
        p0a_ctr[0] += 1
        P.dma("sp", wm[sl], wmod_d[:, n, :].rearrange("p (k c) -> p k c", k=8), writes=[("wm", sl)], semkey=("wm", sl))
        P.dma("sp", bch[sl], bmod_d[:, n * 512:(n + 1) * 512].partition_broadcast(128), writes=[("bch", sl)], semkey=("bch", sl))
        return sl

    def p0a_chunk(n, sl):
        b = nb()
        for k in range(8):
            mm(B(b), st[:, 0, k, :], wm[sl][:, k, :], k == 0, k == 7, [("st", 0), ("wm", sl)], [("ps", b)])
        if n in (4, 5, 10, 11):
            o0 = {4: 0, 5: 512, 10: 1024, 11: 1536}[n]
            dst, dk = gbc[:, o0:o0 + 512], "gbc"
        else:
            dst, dk = mch[sl], ("mch", sl)
        tt("dve", dst, B(b), bch[sl], ALU.add, [("ps", b), ("bch", sl)], [dk])
        tt("dve", tmp4, dst.rearrange("p (j i) -> p j i", i=128), ident4, ALU.mult, [dk, "sm"], ["tmp4"])
        P.op("dve", lambda e: e.tensor_reduce(out=mcol[:, 4 * n:4 * n + 4], in_=tmp4, axis=AX.X, op=ALU.add), reads=["tmp4"], writes=["mods"])
        if n < 4:
            b = nb()
            for k in range(8):
                mm(B(b), st[:, 1, k, :], wm[sl][:, k, :], k == 0, k == 7, [("st", 1), ("wm", sl)], [("ps", b)])
            tt("dve", mch[sl], B(b), bch[sl], ALU.add, [("ps", b), ("bch", sl)], [("mch", sl)])
            tt("dve", tmp4, mch[sl].rearrange("p (j i) -> p j i", i=128), ident4, ALU.mult, [("mch", sl), "sm"], ["tmp4"])
            P.op("dve", lambda e: e.tensor_reduce(out=mxcol[:, 4 * n:4 * n + 4], in_=tmp4, axis=AX.X, op=ALU.add), reads=["tmp4"], writes=["mods"])

    def p0a_tail():
        stt("dve", a1, mcol[:, 8:16], 1.0, SMv("n1g"), ALU.add, ALU.mult, ["mods", "sm"], ["mods"])
        stt("dve", a1x, mxcol[:, 8:16], 1.0, SMv("n1g"), ALU.add, ALU.mult, ["mods", "sm"], ["mods"])
        stt("dve", a2, mcol[:, 32:40], 1.0, SMv("n2g"), ALU.add, ALU.mult, ["mods", "sm"], ["mods"])
        if dbg:
            P.dma("pool", mc_d[:, 0:96], mods, reads=["mods"], semkey="mcd")

    Swst = [A.alloc([2, 4, 2, 128], BF16) for _ in range(2)]
    sw_dv = sw_d.rearrange("p (d g k r c) -> p d g k r c", d=2, g=4, k=16, r=2)
    Hst = [A.alloc([2, 2, 16, 32], BF16) for _ in range(2)]
    Lst = [A.alloc([2, 4, 128], BF16) for _ in range(2)]
    hw_dv = hw_d.rearrange("p (g j r d q c) -> p g j r d q c", g=4, j=16, r=2, d=2, q=4)
    lw_dv = lw_d.rearrange("p (d g t c) -> p d g t c", d=2, g=4, t=16)
    XB = A.alloc([2, 2, 512], F32)
    XB2 = A.alloc([2, 2, 512], F32)
    Mx2 = [A.alloc([2, 1024], BF16) for _ in range(2)]
    Cx = A.alloc([2, 1024], BF16)
    s5t = A.alloc([24, 32], F32)
    s5i = A.alloc([32], I32)
    pw = A.alloc([2, 2, 32], F32)
    t4 = A.alloc([4, 2, 512], F32)
    P.dma("sp", XB[:, :, 0, :], s5b_d[:, :, :], writes=["XB"], semkey="s5b")
    P.dma("sp", XB[:, :, 1, :], s5c_d[:, :, :], writes=["XB"], semkey="s5c")
    for i_ in range(2):
        P.op("pool", lambda e, i_=i_: e.memset(Hst[i_].rearrange("p d r q c -> p (d r q c)"), 0.0), writes=[("Hst", i_)])
    for i_ in range(2):
        P.op("pool", lambda e, i_=i_: e.memset(Mx2[i_].rearrange("p r c -> p (r c)"), 0.0), writes=[("Mx", i_)])
    P.op("pool", lambda e: e.memset(Cx.rearrange("p r c -> p (r c)"), 0.0), writes=["Cx"])
    lr, li, ldt = SMv("lamr"), SMv("lami"), SMv("ldt")
    R = lambda i: s5t[:, i, :]
    S5 = ["s5t"]
    act(R(0), ldt, AF.Exp, ["sm"], S5)
    tt("dve", R(1), lr, R(0), ALU.mult, ["sm"] + S5, S5)
    tt("dve", R(2), li, R(0), ALU.mult, ["sm"] + S5, S5)
    act(R(3), R(1), AF.Exp, S5, S5)
    ts("dve", R(4), R(2), 1.0 / TWO_PI, None, ALU.mult, None, S5, S5)
    cp("dve", s5i, R(4), S5, ["s5i"])
    cp("dve", R(4), s5i, ["s5i"], S5)
    stt("dve", R(5), R(4), -TWO_PI, R(2), ALU.mult, ALU.add, S5, S5)
    act(R(6), R(5), AF.Sin, S5, S5, scale=0.5)
    act(R(7), R(5), AF.Sin, S5, S5, scale=0.5, bias=float(np.pi / 2))
    stt("dve", R(8), R(6), 2.0, R(7), ALU.mult, ALU.mult, S5, S5)
    tt("dve", R(9), R(6), R(6), ALU.mult, S5, S5)
    ts("dve", R(9), R(9), -2.0, 1.0, ALU.mult, ALU.add, S5, S5)
    lamr, lami = pw[:, 0, 0, :], pw[:, 0, 1, :]
    LAM = A.alloc([2, 32], F32)
    tt("dve", LAM[:, 0, :], R(3), R(9), ALU.mult, S5, ["LAM"])
    tt("dve", LAM[:, 1, :], R(3), R(8), ALU.mult, S5, ["LAM"])
    lam_r, lam_i = LAM[:, 0, :], LAM[:, 1, :]
    ts("dve", R(10), lam_r, -1.0, None, ALU.add, None, ["LAM"], S5)
    tt("dve", R(11), lr, lr, ALU.mult, ["sm"], S5)
    tt("dve", R(12), li, li, ALU.mult, ["sm"], S5)
    tt("dve", R(11), R(11), R(12), ALU.add, S5, S5)
    P.op("dve", lambda e: e.reciprocal(out=R(12), in_=R(11)), reads=S5, writes=S5)
    tt("dve", R(13), R(10), lr, ALU.mult, ["sm"] + S5, S5)
    tt("dve", R(14), lam_i, li, ALU.mult, ["sm", "LAM"], S5)
    tt("dve", R(13), R(13), R(14), ALU.add, S5, S5)
    tt("dve", R(15), R(13), R(12), ALU.mult, S5, S5)
    tt("dve", R(13), lam_i, lr, ALU.mult, ["sm", "LAM"], S5)
    tt("dve", R(14), R(10), li, ALU.mult, ["sm"] + S5, S5)
    tt("dve", R(13), R(13), R(14), ALU.subtract, S5, S5)
    tt("dve", R(16), R(13), R(12), ALU.mult, S5, S5)

    def cmul(outr, outi, ar, ai, br, bi, rd, wr, tmp):
        tt("dve", tmp[0], ar, br, ALU.mult, rd, ["cm0"])
        tt("dve", tmp[1], ai, bi, ALU.mult, rd, ["cm1"])
        tt("dve", tmp[2], ar, bi, ALU.mult, rd, ["cm2"])
        tt("dve", tmp[3], ai, br, ALU.mult, rd, ["cm3"])
        tt("dve", outr, tmp[0], tmp[1], ALU.subtract, ["cm0", "cm1"], wr)
        tt("dve", outi, tmp[2], tmp[3], ALU.add, ["cm2", "cm3"], wr)

    def bc16(a):
        return a.unsqueeze(2).to_broadcast([128, 32, 16])

    v16 = lambda a: a.rearrange("p (x c) -> p x c", c=16)
    tm512 = [v16(t4[:, i, 0, :]) for i in range(4)]
    cmul(v16(XB2[:, 0, 0, :]), v16(XB2[:, 1, 0, :]), bc16(R(15)), bc16(R(16)), v16(XB[:, 0, 0, :]), v16(XB[:, 1, 0, :]), S5 + ["XB"], ["XB2"], tm512)
    cp("dve", XB[:, :, 0, :], XB2[:, :, 0, :], ["XB2"], ["XB"])
    Cxv = Cx.rearrange("p r (x m c) -> p r x m c", m=2, c=16)
    Mxv2 = [m_.rearrange("p r (x m c) -> p r x m c", m=2, c=16) for m_ in Mx2]
    for m in range(2):
        pr = slice(64 * m, 64 * m + 64)
        cp("act", Cxv[pr, 0, :, m, :], v16(XB[pr, 0, 1, :]), ["XB"], ["Cx"])
        act(Cxv[pr, 1, :, m, :], v16(XB[pr, 1, 1, :]), AF.Copy, ["XB"], ["Cx"], scale=-1.0)
    Hstv = [h.rearrange("p d r q (m c) -> p d r q m c", m=2) for h in Hst]
    Xkeys = ["XB", "XB2"]
    tm1024 = [t4[:, i].rearrange("p w (x c) -> p w x c", c=16) for i in range(4)]
    lam_bc = lambda a: a.unsqueeze(1).unsqueeze(3).to_broadcast([128, 2, 32, 16])
    def p0s_iter(k):
        ck, nk = Xkeys[k % 2], Xkeys[(k + 1) % 2]
        Xc, Xn = (XB, XB2) if k % 2 == 0 else (XB2, XB)
        if k <= 15:
            for m in range(2):
                pr = slice(64 * m, 64 * m + 64)
                for r in range(2):
                    cp("act", Mxv2[k % 2][pr, r, :, m, :], v16(Xc[pr, r, 0, :]), [ck], [("Mx", k % 2)])
            for r in range(2):
                for d in range(2):
                    b = nb()
                    for gt in range(4):
                        c0 = (d * 16 + gt * 4) * 32
                        P.op("pe", lambda e, b=b, gt=gt, c0=c0, r=r: e.transpose(out=psb[:, b * 1024 + gt * 128:b * 1024 + (gt + 1) * 128], in_=Mx2[k % 2][:, r, c0:c0 + 128], identity=identb),
                             reads=[("Mx", k % 2), "identb"], writes=[("ps", b)], signal=(gt == 3))
                    cp("act", Swst[k % 2][:, d, :, r, :], psb[:, b * 1024:b * 1024 + 512].rearrange("p (g c) -> p g c", g=4), [("ps", b)], [("Swst", k % 2)])
            P.dma("pool", sw_dv[:, :, :, k, :, :], Swst[k % 2], reads=[("Swst", k % 2)], writes=["sw_d"], semkey=("Swst", k % 2))
            for d in range(2):
                b = nb()
                for gt in range(4):
                    c0 = (d * 16 + gt * 4) * 32
                    mm(B(b)[:, gt * 128:(gt + 1) * 128], Mx2[k % 2][:, 0, c0:c0 + 128], Cx[:, 0, c0:c0 + 128], True, False, [("Mx", k % 2), "Cx"], [("ps", b)], signal=False)
                    mm(B(b)[:, gt * 128:(gt + 1) * 128], Mx2[k % 2][:, 1, c0:c0 + 128], Cx[:, 1, c0:c0 + 128], False, True, [("Mx", k % 2), "Cx"], [("ps", b)], signal=(gt == 3))
                tt("dve", Lst[k % 2][:, d, :, :], B(b).rearrange("p (g c) -> p g c", g=4), bmask.unsqueeze(1).to_broadcast([128, 4, 128]), ALU.mult,
                   [("ps", b), "sm"], [("Lst", k % 2)])
            P.dma("pool", lw_dv[:, :, :, k, :], Lst[k % 2], reads=[("Lst", k % 2)], writes=["lw_d"], semkey=("Lst", k % 2))
        if k >= 1:
            kk = k - 1
            for d in range(2):
                j = kk if d == 0 else 15 - kk
                for m in range(2):
                    pr = slice(64 * m, 64 * m + 64)
                    src = lambda r: Xc[pr, r, 1, d * 256:(d + 1) * 256].rearrange("p (q c) -> p q c", c=16)
                    cp("act", Hstv[k % 2][pr, d, 0, :, m, :], src(0), [ck], [("Hst", k % 2)])
                    act(Hstv[k % 2][pr, d, 1, :, m, :], src(1), AF.Copy, [ck], [("Hst", k % 2)], scale=-1.0)
                for r in range(2):
                    P.dma("pool", hw_dv[:, :, j, r, d, :, :].rearrange("p g q c -> p g (q c)"), Hst[k % 2][:, d, r].rearrange("p (g q) c -> p g (q c)", g=4),
                          reads=[("Hst", k % 2)], writes=["hw_d"], semkey=("Hst", k % 2))
        if k < 16:
            cmul(Xn[:, 0].rearrange("p w (x c) -> p w x c", c=16), Xn[:, 1].rearrange("p w (x c) -> p w x c", c=16),
                 lam_bc(lam_r), lam_bc(lam_i),
                 Xc[:, 0].rearrange("p w (x c) -> p w x c", c=16), Xc[:, 1].rearrange("p w (x c) -> p w x c", c=16),
                 [ck, "LAM"], [nk], tm1024)
    def pw_gen():
        tm32 = [t4[:, i, 0, 0:32] for i in range(4)]
        cp("dve", pw[:, 0], LAM, ["LAM"], [("pw", 0)])
        cur = 0
        for _ in range(15):
            cmul(pw[:, 1 - cur, 0, :], pw[:, 1 - cur, 1, :], lam_r, lam_i, pw[:, cur, 0, :], pw[:, cur, 1, :], ["LAM", ("pw", cur)], [("pw", 1 - cur)], tm32)
            yield
            cur = 1 - cur
        MU = A.alloc([2, 32], F32)
        cp("dve", MU, pw[:, cur], [("pw", cur)], ["MU"])
        cp("dve", pw[:, 0], MU, ["MU"], [("pw", 0)])
        cur = 0
        MAv = scanMA.rearrange("p e (qt r q) -> p e qt r q", r=2, q=8)
        MBv = scanMB.rearrange("p e (qt r q) -> p e qt r q", r=2, q=8)
        for e_ in range(16):
            pr_ = pw[:, cur, 0, :].rearrange("p (qt q) -> p qt q", q=8)
            pi_ = pw[:, cur, 1, :].rearrange("p (qt q) -> p qt q", q=8)
            for r in range(2):
                cp("act", MAv[:, e_, :, r, :], pr_, [("pw", cur)], ["scanM"])
            act(MBv[:, e_, :, 0, :], pi_, AF.Copy, [("pw", cur)], ["scanM"], scale=-1.0)
            cp("act", MBv[:, e_, :, 1, :], pi_, [("pw", cur)], ["scanM"])
            if e_ < 15:
                cmul(pw[:, 1 - cur, 0, :], pw[:, 1 - cur, 1, :], MU[:, 0, :], MU[:, 1, :], pw[:, cur, 0, :], pw[:, cur, 1, :], ["MU", ("pw", cur)], [("pw", 1 - cur)], tm32)
                yield
                cur = 1 - cur

        yield
    fold_q = []

    def fold_step(i):
        sl = i % 2
        src = wout_d[:, i, :] if i < 8 else wdn_d[:, i - 8, :]
        gate = gbc[:, 0:D] if i < 8 else gbc[:, D:2 * D]
        dst = wo_d[:, i * 1024:(i + 1) * 1024] if i < 8 else wd_d[:, (i - 8) * 1024:(i - 7) * 1024]
        P.dma("sp", wtmp[sl], src, writes=[("wtmp", sl)], semkey=("wtmp", sl))
        tt("dve", wst[sl], wtmp[sl], gate, ALU.mult, [("wtmp", sl), "gbc"], [("wst", sl)])
        P.dma("pool", dst, wst[sl], reads=[("wst", sl)], writes=["wo_d" if i < 8 else "wd_d"], semkey=("wst", sl))

    pwg = pw_gen()
    pw_done = [False]

    def pw_steps(n_):
        for _ in range(n_):
            if not pw_done[0]:
                try:
                    next(pwg)
                except StopIteration:
                    pw_done[0] = True

    kq = list(range(17))
    order = [0, 1, 2, 3, 4, 5, 10, 11, 6, 7, 8, 9]
    slq = [p0a_load(order[i_]) for i_ in range(3)]
    for oi, n in enumerate(order):
        if oi + 3 < len(order):
            slq.append(p0a_load(order[oi + 3]))
        p0a_chunk(n, slq.pop(0))
        if n == 5:
            fold_q.extend(range(8))
        if n == 11:
            fold_q.extend(range(8, 8 + NF))
        for _ in range(2 if n % 2 == 0 else 1):
            if kq:
                p0s_iter(kq.pop(0))
        pw_steps(3)
        for _ in range(4):
            if fold_q:
                fold_step(fold_q.pop(0))
    while kq:
        p0s_iter(kq.pop(0))
        pw_steps(3)
    pw_steps(1000)
    while fold_q:
        fold_step(fold_q.pop(0))
    p0a_tail()
    P.barrier()
    A.release(p1_mark)
    if stop_after == "p0s":
        return finish()

    def norm_part(xsl, xkey, nsub, xn, xnkey, junk, ssb, ssbkey):
        ss, rs = ssb[:, 0:nsub], ssb[:, 4:4 + nsub]
        for s_ in range(nsub):
            act(junk, xsl[:, s_, :], AF.Square, [xkey], ["junk", ssbkey], accum_out=ss[:, s_:s_ + 1])
        ts("dve", rs, ss, 1.0 / D, EPS, ALU.mult, ALU.add, [ssbkey], [ssbkey])
        act(rs, rs, AF.Sqrt, [ssbkey], [ssbkey])
        P.op("dve", lambda e: e.reciprocal(out=rs, in_=rs), reads=[ssbkey], writes=[ssbkey])
        for s_ in range(nsub):
            ts("dve", xn[:, s_, :], xsl[:, s_, :], rs[:, s_:s_ + 1], None, ALU.mult, None, [xkey, ssbkey], [xnkey])

    def tr_part(nsub, a_col, sh_col, hn, hnkey, xn, xnkey):
        for kp in range(4):
            b = nb()
            for kk in range(2):
                k = 2 * kp + kk
                for s_ in range(nsub):
                    o0 = b * 1024 + kk * 512 + s_ * 128
                    P.op("pe", lambda e, o0=o0, s_=s_, k=k: e.transpose(out=psb[:, o0:o0 + 128], in_=xn[:, s_, k * 128:(k + 1) * 128], identity=identb),
                         reads=[xnkey, "identb"], writes=[("ps", b)], signal=(kk == 1 and s_ == nsub - 1))
            for kk in range(2):
                k = 2 * kp + kk
                o0 = b * 1024 + kk * 512
                if kk == 0:
                    ts("dve", hn[:, k, 0:nsub * 128], psb[:, o0:o0 + nsub * 128], a_col[:, k:k + 1], sh_col[:, k:k + 1], ALU.mult, ALU.add,
                       [("ps", b), "mods"], [hnkey])
                else:
                    act(hn[:, k, 0:nsub * 128], psb[:, o0:o0 + nsub * 128], AF.Identity, [("ps", b), "mods"], [hnkey],
                        scale=a_col[:, k:k + 1], bias=sh_col[:, k:k + 1])

    def norm_to_fm(xsl, xkey, nsub, a_col, sh_col, hn, hnkey, xn, junk, ssb):
        norm_part(xsl, xkey, nsub, xn, "xn", junk, ssb, "ssb")
        tr_part(nsub, a_col, sh_col, hn, hnkey, xn, "xn")

    NCT = NCC + NX
    us_jm = A.alloc([4, 16, NCT], BF16)
    p12_mark = A.mark()
    win_bf = A.alloc([8, 512], BF16)
    P.dma("pool", win_bf, win_d[:, :].rearrange("p (k c) -> p k c", k=8)[:, :, 0:512], writes=["win_bf"], semkey="win_bf")
    xs = [A.alloc([4, 1024], F32) for _ in range(2)]
    xn = A.alloc([4, 1024], BF16)
    hns = [A.alloc([8, 512], BF16) for _ in range(2)]
    junk = A.alloc([1024], BF16)
    ssb = A.alloc([8], F32)
    xn1 = [xn, A.alloc([4, 1024], BF16)]
    ssb1 = [ssb, A.alloc([8], F32)]

    def p1_load(ti):
        sl = ti % 2
        if ti == 0:
            P.dma("sp", xs[sl][:, 0:2, :], ctx_d.rearrange("(s p) d -> p s d", p=128), writes=[("xs", sl)], semkey=("xs", sl))
        else:
            t0 = (ti - 1) * 512
            P.dma("sp", xs[sl], x_d[t0:t0 + 512, :].rearrange("(s p) d -> p s d", p=128), writes=[("xs", sl)], semkey=("xs", sl))

    for hf in range(4):
        r0 = hf * (NF * 128 * 2 // 4)
        r1 = (hf + 1) * (NF * 128 * 2 // 4)
        P.dma("pool", wu_d[r0:r1, :], wup_d[r0:r1, :], writes=[("wu_d", hf)], semkey=("wucast", hf))
    p1_load(0)
    p1_load(1)
    for ti in range(NT + 1):
        sl = ti % 2
        isctx = ti == 0
        nsub = 2 if isctx else 4
        ntok = nsub * 128
        nch = ntok // T
        cof = 0 if isctx else NCC + (ti - 1) * 32
        if ti == 0:
            norm_part(xs[0], ("xs", 0), 2, xn1[0], ("xn1", 0), junk, ssb1[0], ("ssb1", 0))
        tr_part(nsub, a1x if isctx else a1, sh1x if isctx else sh1, hns[sl], ("hn", sl), xn1[sl], ("xn1", sl))
        if ti + 1 <= NT:
            nsl = (ti + 1) % 2
            norm_part(xs[nsl], ("xs", nsl), 4, xn1[nsl], ("xn1", nsl), junk, ssb1[nsl], ("ssb1", nsl))
        for gt in range(4):
            b = nb()
            for k in range(8):
                mm(B(b)[:, 0:ntok], win_bf[:, k, gt * 128:(gt + 1) * 128], hns[sl][:, k, 0:ntok], k == 0, k == 7, ["win_bf", ("hn", sl)], [("ps", b)])
            cp("act", us_jm[:, gt, :, cof:cof + nch], B(b)[:, 0:ntok].rearrange("p (c j) -> p j c", j=16), [("ps", b)], [("us", ti)])
        if ti + 2 <= NT:
            p1_load(ti + 2)
    u_dv = u_d.rearrange("(g p) (j c) -> p g j c", p=128, j=16)
    for gt in range(4):
        P.dma("pool", u_dv[:, gt], us_jm[:, gt, :, NCC:NCT], reads=[("us", ti) for ti in range(NT + 1)], writes=["u_d"], semkey=("ust", gt))
    P.barrier()
    A.release(p12_mark)
    if stop_after == "p1":
        return finish()

    uskeys = [("us", ti) for ti in range(NT + 1)]
    p3a_mark = A.mark()
    Swq1 = A.alloc([2, 16, 2, 128], BF16)
    Swq = [Swq1, Swq1]
    Wp = A.alloc([N, 32], F32)
    Hs = A.alloc([NS + 1, 32], F32)
    sct = A.alloc([2, NS, 32], F32)
    fx = [A.alloc([NS, 32], F32) for _ in range(2)]
    MA2 = A.alloc([16, 2, 32], F32)
    MB2 = A.alloc([16, 2, 32], F32)
    Hdv = Hd.rearrange("p (h d g e n) -> p h d g e n", h=2, d=2, g=2, e=8)
    for h in range(2):
        for r in range(2):
            srcA = scanMA.rearrange("p e (d h r q) -> p e d h r q", d=2, h=2, r=2)[:, :, :, h, r, :]
            srcB = scanMB.rearrange("p e (d h r q) -> p e d h r q", d=2, h=2, r=2)[:, :, :, h, r, :]
            cp("act", MA2[:, :, h, r * 16:(r + 1) * 16].rearrange("p e (d q) -> p e d q", d=2), srcA, ["scanM"], ["MA2"])
            cp("act", MB2[:, :, h, r * 16:(r + 1) * 16].rearrange("p e (d q) -> p e d q", d=2), srcB, ["scanM"], ["MA2"])

    def swp(a):
        return a.rearrange("p s (r q) -> p s r q", r=2)[:, :, ::-1, :]

    def v4(a):
        return a.rearrange("p s (r q) -> p s r q", r=2)

    wk = "Wp"
    for qh in range(2):
        Hq = us_jm[:, 2 * qh:2 * qh + 2].rearrange("p a (b e) n -> p a b e n", b=2)
        for d in range(2):
            P.dma("sp", Swq[d], sw_dv[:, d, 2 * qh:2 * qh + 2], reads=["sw_d"], writes=["Swq"], semkey="Swq")
            if d == 0:
                wx = Wp[:, NCC:N, :]
                wc = Wp[:, 0:NCC, :]
            else:
                wx = Wp[:, NCC:N, :][:, ::-1, :]
                wc = Wp[:, 0:NCC, :][:, ::-1, :]
            gbc = nbg(4)
            ckeys = [("ps", gbc + i) for i in range(4)]
            cnt = 0
            for r in range(2):
                for g2 in range(2):
                    gt = 2 * qh + g2
                    blk = r * 2 + g2
                    for i in range(16):
                        k = 15 - i if d == 0 else i
                        for qi in range(4):
                            cnt += 1
                            mm(B(gbc + qi)[:, blk * NCC:(blk + 1) * NCC], Swq[d][32 * qi:32 * qi + 32, g2, k, r, :], us_jm[32 * qi:32 * qi + 32, gt, i, 0:NCC],
                               i == 0, i == 15, ["Swq", ("us", 0), ("usrd", qh)], ckeys, signal=(cnt == 256), tile_position=(32 * qi, 0), skip_group_check=True)
            regc = ps[:, gbc * 512:(gbc + 4) * 512].rearrange("p (qi x) -> p qi x", qi=4)[:, :, 0:4 * NCC].rearrange("p qi (r g c) -> p r c g qi", r=2, g=2)
            for r in range(2):
                c0 = r * 16 + d * 8
                cp("act", wc[:, :, c0:c0 + 8].rearrange("p c (g qi) -> p c g qi", qi=4), regc[:, r], ckeys, [wk])
            for r in range(2):
                for g2 in range(2):
                    gt = 2 * qh + g2
                    gb = nbg(4)
                    skeys = [("ps", gb + i) for i in range(4)]
                    for i in range(16):
                        k = 15 - i if d == 0 else i
                        for qi in range(4):
                            mm(B(gb + qi)[:, 0:NX], Swq[d][32 * qi:32 * qi + 32, g2, k, r, :], us_jm[32 * qi:32 * qi + 32, gt, i, NCC:NCT],
                               i == 0, i == 15, ["Swq", ("usrd", qh)] + uskeys, skeys, signal=(i == 15 and qi == 3), tile_position=(32 * qi, 0), skip_group_check=True)
                    c0 = r * 16 + d * 8 + g2 * 4
                    src = ps[:, gb * 512:(gb + 4) * 512].rearrange("p (qi x) -> p qi x", qi=4)[:, :, 0:NX].rearrange("p qi n -> p n qi")
                    cp("act", wx[:, :, c0:c0 + 4], src, skeys, [wk])
        eng = "dve"
        Wv = Wp.rearrange("p (s r) c -> p s r c", r=16)
        MAe = lambda e_, n_: MA2[:, e_, qh, :].unsqueeze(1).to_broadcast([128, n_, 32])
        MBe = lambda e_, n_: v4(MB2[:, e_, qh, :].unsqueeze(1).to_broadcast([128, n_, 32]))
        t1, t2 = sct[:, 0], sct[:, 1]
        for step in range(1, 16):
            prev, cur_ = Wv[:, :, step - 1, :], Wv[:, :, step, :]
            tt("pool", v4(t2), swp(prev), MBe(0, NS), ALU.mult, [wk, "MA2"], ["sct2"])
            tt(eng, t1, prev, MAe(0, NS), ALU.mult, [wk, "MA2"], ["sct1"])
            tt(eng, cur_, cur_, t1, ALU.add, [wk, "sct1"], [wk])
            tt(eng, cur_, cur_, t2, ALU.add, [wk, "sct2"], [wk])
        P.op(eng, lambda e: e.memset(Hs.rearrange("p s c -> p (s c)"), 0.0), writes=["Hs"])
        for s_ in range(NS):
            hp, hd_ = Hs[:, s_:s_ + 1, :], Hs[:, s_ + 1:s_ + 2, :]
            a1_, a2_ = sct[:, 0, 0:1, :], sct[:, 1, 0:1, :]
            tt(eng, a1_, hp, MAe(15, 1), ALU.mult, ["Hs", "MA2"], ["sct1"])
            tt(eng, v4(a2_), swp(hp), MBe(15, 1), ALU.mult, ["Hs", "MA2"], ["sct2"])
            tt(eng, a1_, a1_, a2_, ALU.add, ["sct1", "sct2"], ["sct1"])
            tt(eng, hd_, Wv[:, s_, 15:16, :], a1_, ALU.add, [wk, "sct1"], ["Hs"])
        hsp = Hs[:, 0:NS, :]
        for r_ in range(16):
            tt("pool", v4(t2), swp(hsp), MBe(r_, NS), ALU.mult, ["Hs", "MA2"], ["sct2"])
            tt(eng, t1, hsp, MAe(r_, NS), ALU.mult, ["Hs", "MA2"], ["sct1"])
            fxb = fx[r_ % 2]
            tt(eng, fxb, Wv[:, :, r_, :], t1, ALU.add, [wk, "sct1"], [("fx", r_ % 2)])
            tt(eng, fxb, fxb, t2, ALU.add, [("fx", r_ % 2), "sct2"], [("fx", r_ % 2)])
            fxv = fxb.rearrange("p s (r d g q) -> p s r d g q", r=2, d=2, g=2)
            for d in range(2):
                for g2 in range(2):
                    hv = Hq[:, d, g2]
                    if d == 1:
                        hv = hv[:, :, ::-1]
                    dstv = hv.rearrange("p (ri q) (s r) -> p s r ri q", ri=2, r=16)[:, :, r_, :, :]
                    cp("act", dstv, fxv[:, :, :, d, g2, :], [("fx", r_ % 2)], ["Hq", ("usrd", qh)])
        P.dma("pool", Hdv[:, qh], Hq, reads=["Hq"], writes=["Hd"], semkey="Hq")
    P.barrier()
    A.release(persist_mark)
    if stop_after == "p2":
        return finish()

    dg31 = A.alloc([4, KC, 128], BF16)
    p3b_mark = A.mark()
    cwv = SMv("confw").rearrange("p (m k) -> p m k", k=KC)
    dgq = [(m, k) for m in range(4) for k in range(KC)]
    ub = [A.alloc([16, NX], BF16) for _ in range(2)]
    Hg = [A.alloc([2, 8, N], BF16) for _ in range(2)]
    Hwg = [A.alloc([16, 2, 2, 4, 32], BF16) for _ in range(2)]
    lwg = [A.alloc([2, 16, 128], BF16) for _ in range(2)]
    ysb = [A.alloc([NX, 4], F32) for _ in range(2)]
    ygs = A.alloc([L], BF16)
    hw_dv2 = hw_d.rearrange("p (g x) -> p g x", g=4)
    lw_dv2 = lw_d.rearrange("p (d g t c) -> p d g t c", d=2, g=4, t=16)
    mix_dv = mix_d.rearrange("(g p) t -> p g t", p=128)
    Dcol = SMv("s5d")

    def p3a_load(gt):
        sl = gt % 2
        P.dma("sp", ub[sl], u_dv[:, gt], reads=["u_d"], writes=[("ub", sl)], semkey=("ub", sl))
        P.dma("sp", Hg[sl], Hdv[:, gt // 2, :, gt % 2], reads=["Hd"], writes=[("Hg", sl)], semkey=("Hg", sl))
        P.dma("sp", Hwg[sl].rearrange("p j r d q c -> p (j r d q c)"), hw_dv2[:, gt, :], reads=["hw_d"], writes=[("Hwg", sl)], semkey=("Hwg", sl))
        P.dma("sp", lwg[sl], lw_dv2[:, :, gt], reads=["lw_d"], writes=[("lwg", sl)], semkey=("lwg", sl))

    p3a_load(0)
    for gt in range(4):
        sl = gt % 2
        if gt + 1 < 4:
            p3a_load(gt + 1)
        ux = ub[sl]
        rd = [("ub", sl), ("lwg", sl), ("Hwg", sl), ("Hg", sl)]
        for jq in range(4):
            gb = nbg(4)
            ykeys = [("ps", gb + i) for i in range(4)]
            for jl in range(4):
                j = 4 * jq + jl
                ob = B(gb + jl)[:, 0:NX]
                mm(ob, lwg[sl][:, 0, 0, :], ux[:, j, :], True, False, rd, ykeys, signal=False, skip_group_check=True)
                for tau in range(1, j + 1):
                    mm(ob, lwg[sl][:, 0, tau, :], ux[:, j - tau, :], False, False, rd, ykeys, signal=False, skip_group_check=True)
                for tau in range(0, 16 - j):
                    mm(ob, lwg[sl][:, 1, tau, :], ux[:, j + tau, :], False, False, rd, ykeys, signal=False, skip_group_check=True)
                cnt = 0
                for d in range(2):
                    n0 = NCC - 1 if d == 0 else 1
                    for r in range(2):
                        for qi in range(4):
                            cnt += 1
                            last = (cnt == 16 and jl == 3)
                            mm(B(gb + jl)[32 * qi:32 * qi + 32, 0:NX], Hwg[sl][:, j, r, d, qi, :], Hg[sl][:, d, r * 4 + qi, n0:n0 + NX], False, cnt == 16, rd, ykeys,
                               signal=last, tile_position=(0, 32 * qi), skip_group_check=True)
            ysl = jq % 2
            pin = ps[:, gb * 512:(gb + 4) * 512].rearrange("p (jl x) -> p jl x", jl=4)[:, :, 0:NX].rearrange("p jl c -> p c jl")
            uin = ux[:, 4 * jq:4 * jq + 4, :].rearrange("p j c -> p c j")
            stt("dve", ysb[ysl], uin, Dcol[:, gt:gt + 1], pin, ALU.mult, ALU.add, [("ub", sl), "sm"] + ykeys, [("ysb", ysl)])
            act(ygs.rearrange("p (c j) -> p c j", j=16)[:, :, 4 * jq:4 * jq + 4], ysb[ysl], AF.Gelu_apprx_tanh, [("ysb", ysl)], ["ygs"])
            for _ in range(8):
                if dgq:
                    m_, k_ = dgq.pop(0)
                    act(dg31[:, m_, k_, :], ident, AF.Copy, ["sm"], ["dg31"], scale=cwv[:, m_, k_:k_ + 1])
        P.dma("pool", mix_dv[:, gt, :], ygs, reads=["ygs"], writes=["mix_d"], semkey="ygs")
    assert not dgq
    P.barrier()
    A.release(p3b_mark)
    if stop_after == "p3a":
        return finish()

    wglu = A.alloc([4, 512], BF16)
    P.dma("pool", wglu, wglu_d[:, :].rearrange("p (k c) -> p k c", k=4), writes=["wglu"], semkey="wglu")
    bglu = SMv("bglu")
    mixg = A.alloc([4, 512], BF16)
    sgg = None
    wvg = A.alloc([8, 1024], BF16)
    wo_bf = A.alloc([8, 1024], BF16)
    P.dma("pool", wvg, win_d[:, :].rearrange("p (k c) -> p k c", k=8)[:, :, 512:1536], writes=["wvg"], semkey="wvg")
    P.dma("sp", wo_bf.rearrange("p k c -> p (k c)"), wo_d[:, :], reads=["wo_d"], writes=["wo_bf"], semkey="wo_bf")
    xs = [A.alloc([4, 1024], F32) for _ in range(2)]
    xn = A.alloc([4, 1024], BF16)
    hn = A.alloc([8, 512], BF16)
    junk = A.alloc([1024], BF16)
    ssb = A.alloc([8], F32)
    cbuf = A.alloc([4, 8, 64 + KC - 1], BF16)
    sgm = [A.alloc([512], F32) for _ in range(2)]
    sgg = sgm
    hcb = A.alloc([4, 512], BF16)
    sq2 = A.alloc([4, 512], BF16)
    sq = A.alloc([4, 512], BF16)
    lnt = A.alloc([4, 512], F32)
    mixc = A.alloc([4, 512], BF16)
    msb = [A.alloc([4, 512], BF16) for _ in range(2)]
    P.op("pool", lambda e: e.memset(cbuf.rearrange("p m r c -> p (m r c)"), 0.0), writes=["cbuf"])
    confb, lng, lnb = SMv("confb"), SMv("lng"), SMv("lnb")

    def p3b1_load(ti):
        sl = ti % 2
        t0 = ti * 512
        P.dma("sp", xs[sl], x_d[t0:t0 + 512, :].rearrange("(s p) d -> p s d", p=128), writes=[("xs", sl)], semkey=("xs", sl))
        P.dma("sp", msb[sl], mix_dv[:, :, t0:t0 + 512], reads=["mix_d"], writes=[("msb", sl)], semkey=("msb", sl))

    hc2 = [hcb, sq2]
    mixg2 = [mixg, A.alloc([4, 512], BF16)]
    lnm = [A.alloc([512], F32) for _ in range(2)]
    lne = [A.alloc([512], F32) for _ in range(2)]
    xs3 = xs + [A.alloc([4, 1024], F32)]

    def ld3(ti):
        t0 = ti * 512
        P.dma("sp", xs3[ti % 3], x_d[t0:t0 + 512, :].rearrange("(s p) d -> p s d", p=128), writes=[("xs", ti % 3)], semkey=("xs", ti % 3))

    def ldm(ti):
        t0 = ti * 512
        P.dma("sp", msb[ti % 2], mix_dv[:, :, t0:t0 + 512], reads=["mix_d"], writes=[("msb", ti % 2)], semkey=("msb", ti % 2))

    def st_norm(ti):
        norm_part(xs3[ti % 3], ("xs", ti % 3), 4, xn, "xn", junk, ssb, "ssb")

    def st_tr(ti):
        tr_part(4, a1, sh1, hn, "hn", xn, "xn")

    def st_a1(ti):
        for m in range(4):
            bg, bv = nb(), nb()
            for k in range(8):
                mm(B(bg), wvg[:, k, 512 + m * 128:512 + (m + 1) * 128], hn[:, k, :], k == 0, k == 7, ["wvg", "hn"], [("ps", bg)])
            for k in range(8):
                mm(B(bv), wvg[:, k, m * 128:(m + 1) * 128], hn[:, k, :], k == 0, k == 7, ["wvg", "hn"], [("ps", bv)])
            act(sgm[m % 2], B(bg), AF.Sigmoid, [("ps", bg)], [("sgm", m % 2)])
            tt("dve", cbuf[:, m, :, 15:79], B(bv).rearrange("p (r c) -> p r c", c=64), sgm[m % 2].rearrange("p (r c) -> p r c", c=64), ALU.mult,
               [("ps", bv), ("sgm", m % 2), "cbuf"], [("cbuf", m)])

    def st_a2(ti):
        sl = ti % 2
        for m in range(4):
            b = nb()
            for k in range(KC):
                mm(B(b).rearrange("p (r c) -> p r c", c=64), dg31[:, m, k, :], cbuf[:, m, :, k:k + 64], k == 0, k == KC - 1, ["dg31", ("cbuf", m), "cbuf"], [("ps", b)])
            act(hc2[sl][:, m, :], B(b), AF.Identity, [("ps", b), "sm"], [("hc", sl, m)], bias=confb[:, m:m + 1])
            act(sq[:, m, :], B(b), AF.Square, [("ps", b), "sm"], [("sq", m)], bias=confb[:, m:m + 1])

    def st_a3(ti):
        sl = ti % 2
        bm_, be_ = nb(), nb()
        for m in range(4):
            mm(B(bm_), ones512, hc2[sl][:, m, :], m == 0, m == 3, ["ones512", ("hc", sl, m)], [("ps", bm_)])
        for m in range(4):
            mm(B(be_), ones512, sq[:, m, :], m == 0, m == 3, ["ones512", ("sq", m)], [("ps", be_)])
        cp("act", lnm[sl], B(bm_), [("ps", bm_)], [("lnm", sl)])
        cp("act", lne[sl], B(be_), [("ps", be_)], [("lne", sl)])
        for m in range(4):
            b = nb()
            for k in range(4):
                mm(B(b), wglu[:, k, m * 128:(m + 1) * 128], msb[sl][:, k, :], k == 0, k == 3, ["wglu", ("msb", sl)], [("ps", b)])
            act(sgg[m % 2], B(b), AF.Sigmoid, [("ps", b), "sm"], [("sgm", m % 2)], bias=bglu[:, m:m + 1])
            tt("pool", mixg2[sl][:, m, :], msb[sl][:, m, :], sgg[m % 2], ALU.mult, [("msb", sl), ("sgm", m % 2)], [("mixg", sl, m)])

    def st_b1(ti):
        sl = ti % 2
        mean_sb, msq, rstd_ = lnm[sl], lnt[:, 1, :], lnt[:, 2, :]
        tt("dve", msq, mean_sb, mean_sb, ALU.mult, [("lnm", sl)], ["ln_msq"])
        tt("dve", msq, lne[sl], msq, ALU.subtract, [("lne", sl), "ln_msq"], ["ln_msq"])
        ts("dve", msq, msq, LN_EPS, None, ALU.add, None, ["ln_msq"], ["ln_msq"])
        act(rstd_, msq, AF.Sqrt, ["ln_msq"], ["ln_rstd"])
        lts = [lnt[:, 0, :], lnt[:, 3, :]]
        for m in range(2):
            tt("dve", lts[m], hc2[sl][:, m, :], mean_sb, ALU.subtract, [("hc", sl, m), ("lnm", sl)], [("ln_t", m)])
        P.op("dve", lambda e: e.reciprocal(out=rstd_, in_=rstd_), reads=["ln_rstd"], writes=["ln_rstd"])
        for m in range(4):
            lt_, lk = lts[m % 2], ("ln_t", m % 2)
            if m >= 2:
                tt("dve", lt_, hc2[sl][:, m, :], mean_sb, ALU.subtract, [("hc", sl, m), ("lnm", sl)], [lk])
            tt("dve", lt_, lt_, rstd_, ALU.mult, [lk, "ln_rstd"], [lk])
            act(mixc[:, m, :], lt_, AF.Silu, [lk, "sm"], [("mixc", m)], scale=lng[:, m:m + 1], bias=lnb[:, m:m + 1])

    def st_b2(ti):
        sl = ti % 2
        x3 = xs3[ti % 3]
        xk = ("xs", ti % 3)
        t0 = ti * 512
        for s_ in range(4):
            for hf in range(2):
                b = nb()
                for k in range(8):
                    lh = mixg2[sl][:, k, s_ * 128:(s_ + 1) * 128] if k < 4 else mixc[:, k - 4, s_ * 128:(s_ + 1) * 128]
                    rk = ("mixg", sl, k) if k < 4 else ("mixc", k - 4)
                    mm(B(b), lh, wo_bf[:, k, hf * 512:(hf + 1) * 512], k == 0, k == 7, [rk, "wo_bf"], [("ps", b)])
                tt("dve", x3[:, s_, hf * 512:(hf + 1) * 512], B(b), x3[:, s_, hf * 512:(hf + 1) * 512], ALU.add, [("ps", b), xk], [xk])
        P.dma("pool", h1_d[t0:t0 + 512, :].rearrange("(s p) d -> p s d", p=128), x3, reads=[xk], writes=["h1_d"], semkey=("h1st", ti % 3))

    for ti in range(min(3, NT)):
        ld3(ti)
    for ti in range(min(2, NT)):
        ldm(ti)
    st_norm(0)
    st_tr(0)
    st_a1(0)
    st_a2(0)
    if NT > 1:
        st_norm(1)
        st_tr(1)
    st_a3(0)
    for ti in range(NT):
        if ti + 1 < NT:
            st_a1(ti + 1)
        st_b1(ti)
        if ti + 2 < NT:
            st_norm(ti + 2)
        if ti + 1 < NT:
            st_a2(ti + 1)
        st_b2(ti)
        if ti + 2 < NT:
            st_tr(ti + 2)
        if ti + 1 < NT:
            st_a3(ti + 1)
        if ti + 3 < NT:
            ld3(ti + 3)
        if ti + 2 < NT:
            ldm(ti + 2)
    P.barrier()
    A.release(persist_mark)
    if stop_after == "p3b1":
        return finish()

    wd_bf = A.alloc([NF, 1024], BF16)
    fg_bc = A.alloc([1024], F32)
    P.dma("sp", wd_bf.rearrange("p f c -> p (f c)"), wd_d[:, :], reads=["wd_d"], writes=["wd_bf"], semkey="wd_bf")
    P.dma("sp", fg_bc, fg_d.partition_broadcast(128), writes=["fg_bc"], semkey="fg_bc")
    fwv = SMv("fcw").rearrange("p (f k) -> p f k", k=3)
    fcb = SMv("fcb")
    hs_ = [A.alloc([4, 1024], F32) for _ in range(2)]
    xn = A.alloc([4, 1024], BF16)
    hn2 = A.alloc([8, 512], BF16)
    junk = A.alloc([1024], BF16)
    ssb = A.alloc([8], F32)
    ssf = A.alloc([8], F32)
    sgf = [A.alloc([512], F32) for _ in range(3)]
    actb = A.alloc([NF, 512], BF16)
    wus = [A.alloc([8, 256], BF16) for _ in range(3)]
    ot = [A.alloc([4, 1024], F32) for _ in range(1)]
    wu_dv = wu_d.rearrange("(f p h) c -> f p (h c)", p=128, h=2)
    def p3b2_load(ti):
        sl = ti % 2
        t0 = ti * 512
        P.dma("pool", hs_[sl], h1_d[t0:t0 + 512, :].rearrange("(s p) d -> p s d", p=128), reads=["h1_d"], writes=[("hs", sl)], semkey=("hs", sl))

    wu_ctr = [0]

    def wu_load(f_):
        sl3 = wu_ctr[0] % 3
        wu_ctr[0] += 1
        P.dma("sp", wus[sl3].rearrange("p k c -> p (k c)"), wu_dv[f_], reads=[("wu_d", i_) for i_ in range(4)], writes=[("wus", sl3)], semkey=("wus", sl3))
        return sl3

    xns = [xn, A.alloc([4, 1024], BF16)]
    p3b2_load(0)
    if NT > 1:
        p3b2_load(1)
    pend = [wu_load(0), wu_load(1)]
    norm_part(hs_[0], ("hs", 0), 4, xns[0], ("xn2", 0), junk, ssb, "ssb")
    tr_part(4, a2, sh2, hn2, "hn2", xns[0], ("xn2", 0))

    cvb = [A.alloc([512], F32) for _ in range(3)]

    def ffn_tail1(f_, bv, bg, gs):
        cv = cvb[gs]
        cvv = cv.rearrange("p (r c) -> p r c", c=64)
        gv = B(bg).rearrange("p (r c) -> p r c", c=64)
        act(cv, B(bg), AF.Copy, [("ps", bg), "sm"], [("cvb", gs)], scale=fwv[:, f_, 1:2])
        stt("dve", cvv[:, :, 1:64], gv[:, :, 0:63], fwv[:, f_, 0:1], cvv[:, :, 1:64], ALU.mult, ALU.add, [("ps", bg), "sm", ("cvb", gs)], [("cvb", gs)])
        stt("dve", cvv[:, :, 0:63], gv[:, :, 1:64], fwv[:, f_, 2:3], cvv[:, :, 0:63], ALU.mult, ALU.add, [("ps", bg), "sm", ("cvb", gs)], [("cvb", gs)])
        act(sgf[gs], cv, AF.Silu, [("cvb", gs), "sm"], [("sgf", gs)], bias=fcb[:, f_:f_ + 1])

    def ffn_tail2(f_, bv, bg, gs):
        tt("dve", actb[:, f_, :], B(bv), sgf[gs], ALU.mult, [("ps", bv), ("sgf", gs)], [("actb", f_)])

    for ti in range(NT):
        sl = ti % 2
        t0 = ti * 512
        prev = None
        prev2 = None
        for f_ in range(NF):
            w3 = pend.pop(0)
            nxt = ti * NF + f_ + 2
            if nxt < NT * NF:
                pend.append(wu_load(nxt % NF))
            bv, bg = nb(), nb()
            for k in range(8):
                mm(B(bg), wus[w3][:, k, 128:256], hn2[:, k, :], k == 0, k == 7, [("wus", w3), "hn2"], [("ps", bg)])
            for k in range(8):
                mm(B(bv), wus[w3][:, k, 0:128], hn2[:, k, :], k == 0, k == 7, [("wus", w3), "hn2"], [("ps", bv)])
            gs = f_ % 3
            if prev is not None:
                ffn_tail1(*prev)
            if prev2 is not None:
                ffn_tail2(*prev2)
            prev2 = prev
            prev = (f_, bv, bg, gs)
            if f_ == 11 and ti + 1 < NT:
                nsl = (ti + 1) % 2
                norm_part(hs_[nsl], ("hs", nsl), 4, xns[nsl], ("xn2", nsl), junk, ssb, "ssb")
        ffn_tail1(*prev)
        ffn_tail2(*prev2)
        ffn_tail2(*prev)
        if ti + 1 < NT:
            tr_part(4, a2, sh2, hn2, "hn2", xns[nsl], ("xn2", nsl))
        for s_ in range(4):
            for hf in range(2):
                b = nb()
                for f_ in range(NF):
                    mm(B(b), actb[:, f_, s_ * 128:(s_ + 1) * 128], wd_bf[:, f_, hf * 512:(hf + 1) * 512], f_ == 0, f_ == NF - 1, [("actb", f_), "wd_bf"], [("ps", b)])
                tt("dve", hs_[sl][:, s_, hf * 512:(hf + 1) * 512], B(b), hs_[sl][:, s_, hf * 512:(hf + 1) * 512], ALU.add, [("ps", b), ("hs", sl)], [("hs", sl)])
        ss, rs = ssf[:, 0:4], ssf[:, 4:8]
        for s_ in range(4):
            act(junk, hs_[sl][:, s_, :], AF.Square, [("hs", sl)], ["junk", "ssf"], accum_out=ss[:, s_:s_ + 1])
        ts("dve", rs, ss, 1.0 / D, EPS, ALU.mult, ALU.add, ["ssf"], ["ssf"])
        act(rs, rs, AF.Sqrt, ["ssf"], ["ssf"])
        P.op("dve", lambda e: e.reciprocal(out=rs, in_=rs), reads=["ssf"], writes=["ssf"])
        for s_ in range(4):
            stt("dve", ot[0][:, s_, :], hs_[sl][:, s_, :], rs[:, s_:s_ + 1], fg_bc, ALU.mult, ALU.mult, [("hs", sl), "ssf", "fg_bc"], ["ot"])
        P.dma("pool", y_d[t0:t0 + 512, :].rearrange("(s p) d -> p s d", p=128), ot[0], reads=["ot"], writes=["y_d"], semkey="yst")
        if ti + 2 < NT:
            p3b2_load(ti + 2)
    P.barrier()
    return finish()


def _col(v, nt):
    return np.ascontiguousarray(np.asarray(v, np.float32).reshape(nt, 128).T)


def _pair32(a):
    a = np.asarray(a, np.float32).reshape(2, 16, 2, 64)
    return np.ascontiguousarray(a.transpose(2, 3, 0, 1).reshape(128, 32))


def prep_shared(inp):
    f = lambda k: np.asarray(inp[k], np.float32)
    sh = {}
    sm = np.zeros((128, NSM), np.float32)

    def put(name, arr):
        o, w = _SM[name]
        sm[:, o:o + w] = arr.reshape(128, w)

    put("n1g", _col(f("norm1_g")[0], 8))
    put("n2g", _col(f("norm2_g")[0], 8))
    put("s5d", _col(f("s5_d")[0], 4))
    put("bglu", _col(f("b_glu")[0], 4))
    put("confb", _col(f("conf_b")[0], 4))
    put("lng", _col(f("conf_ln_g")[0], 4))
    put("lnb", _col(f("conf_ln_b")[0], 4))
    cw = f("conf_w")[0]
    put("confw", np.ascontiguousarray(cw.T.reshape(4, 128, KC).transpose(1, 0, 2)))
    fw = f("ffn_conv_w")[0]
    put("fcw", np.ascontiguousarray(fw.T.reshape(NF, 128, 3).transpose(1, 0, 2)))
    put("fcb", _col(f("ffn_conv_b")[0], NF))
    put("ident", np.eye(128, dtype=np.float32))
    bm = (np.arange(128)[:, None] // 16 == np.arange(128)[None, :] // 16).astype(np.float32)
    put("bmask", bm)
    put("lamr", _pair32(f("s5_lam_re")[0]))
    put("lami", _pair32(f("s5_lam_im")[0]))
    put("ldt", _pair32(np.broadcast_to(f("s5_log_dt")[0][:, :, None], (2, 32, 64))))
    sh["sm"] = sm

    def pairB(a):
        a = a.reshape(2, 16, 2, 64, 16)
        return a.transpose(2, 3, 0, 1, 4).reshape(128, 512)

    def pairC(a):
        a = a.reshape(2, 16, 2, 16, 64)
        return a.transpose(2, 4, 0, 1, 3).reshape(128, 512)

    sh["s5b"] = np.ascontiguousarray(np.stack([pairB(f("s5_b_re")[0]), pairB(f("s5_b_im")[0])], 1))
    sh["s5c"] = np.ascontiguousarray(np.stack([pairC(f("s5_c_re")[0]), pairC(f("s5_c_im")[0])], 1))
    sh["bmod"] = np.ascontiguousarray(f("b_mod")[0].reshape(1, 6 * D))
    sh["fg"] = np.ascontiguousarray(f("final_g").reshape(1, D))
    wm = f("w_mod")[0]
    sh["wmod"] = np.ascontiguousarray(wm.reshape(8, 128, 12, 512).transpose(1, 2, 0, 3).reshape(128, 12, 8 * 512))
    sh["win"] = np.ascontiguousarray(f("w_in")[0].reshape(8, 128, 1536).transpose(1, 0, 2).reshape(128, 8 * 1536))
    sh["wglu"] = np.ascontiguousarray(f("w_glu")[0].reshape(4, 128, 512).transpose(1, 0, 2).reshape(128, 4 * 512))
    sh["wout"] = np.ascontiguousarray(f("w_out")[0].reshape(8, 128, 1024).transpose(1, 0, 2))
    wu = f("ffn_w_up")[0]
    wu = wu.reshape(8, 128, 2, NF, 128)
    sh["wup"] = np.ascontiguousarray(wu.transpose(3, 1, 0, 2, 4).reshape(NF * 128 * 2, 1024))
    sh["wdn"] = np.ascontiguousarray(f("ffn_w_down")[0].reshape(NF, 128, 1024).transpose(1, 0, 2))
    return sh


def prep_core(inp, sh, b):
    m = {k: v for k, v in sh.items() if k != "sm"}
    sm = sh["sm"].copy()
    cc = np.stack([np.asarray(inp["c"], np.float32)[b], np.asarray(inp["c_ctx"], np.float32)], 1)
    o, w = _SM["cc"]
    sm[:, o:o + w] = cc.reshape(8, 128, 2).transpose(1, 0, 2).reshape(128, 16)
    m["smalls"] = sm
    m["x"] = np.ascontiguousarray(np.asarray(inp["x"], np.float32)[b])
    m["ctx"] = np.ascontiguousarray(np.asarray(inp["ctx"], np.float32)[b])
    return m


_NC_CACHE = {}


def kernel(**inputs):
    nb_, L = inputs["x"].shape[0], inputs["x"].shape[1]
    if L not in _NC_CACHE:
        _NC_CACHE[L] = build(L)
    nc = _NC_CACHE[L]
    sh = prep_shared(inputs)
    in_maps = [prep_core(inputs, sh, b) for b in range(nb_)]
    res = run_bass_kernel_spmd(nc, in_maps, core_ids=list(range(nb_)))
    return np.stack([np.asarray(r["y"], np.float32) for r in res.results], 0)
```

```python
import numpy as np
from contextlib import ExitStack
import concourse.bass as bass
import concourse.mybir as mybir
from concourse.bass_utils import run_bass_kernel_spmd

F32 = mybir.dt.float32
BF16 = mybir.dt.bfloat16
I32 = mybir.dt.int32
AF = mybir.ActivationFunctionType
ALU = mybir.AluOpType
AX = mybir.AxisListType

ENG_ATTR = {"pe": "tensor", "act": "scalar", "dve": "vector", "pool": "gpsimd", "sp": "sync"}
_DTSIZE = {F32: 4, BF16: 2, I32: 4}


class Prog:
    def __init__(self, nc, es):
        self.nc = nc
        self.es = es
        self.streams = {e: [] for e in ENG_ATTR}
        self.sem = {}
        self.tick = {}
        self.waited = {e: {} for e in ENG_ATTR}
        self.w = {}
        self.r = {}
        self.nops = {e: 0 for e in ENG_ATTR}
        for e in ENG_ATTR:
            self._mksem(("eng", e))

    def _mksem(self, key):
        if key not in self.sem:
            name = "s_" + "_".join(str(k) for k in key)
            self.sem[key] = self.es.enter_context(self.nc.semaphore(name))
            self.tick[key] = 0

    def _deps(self, eng, reads, writes):
        deps = {}

        def add(ev):
            k, v = ev
            if k == ("eng", "pe") and eng == "pe":
                return
            if deps.get(k, 0) < v:
                deps[k] = v

        for r in reads:
            if r in self.w:
                add(self.w[r])
        for w_ in writes:
            if w_ in self.w:
                add(self.w[w_])
            for k, v in self.r.get(w_, {}).items():
                add((k, v))
        out = []
        for k, v in deps.items():
            if self.waited[eng].get(k, 0) < v:
                self.waited[eng][k] = v
                out.append((self.sem[k], v))
        return out

    def _commit(self, ev, reads, writes):
        for r in reads:
            d = self.r.setdefault(r, {})
            if d.get(ev[0], 0) < ev[1]:
                d[ev[0]] = ev[1]
        for w_ in writes:
            self.w[w_] = ev
            self.r[w_] = {}

    def op(self, eng, fn, reads=(), writes=(), signal=True):
        waits = self._deps(eng, reads, writes)
        key = ("eng", eng)
        if signal:
            self.tick[key] += 1
            ev = (key, self.tick[key])
        else:
            ev = (key, self.tick[key] + 1)
        self._commit(ev, reads, writes)
        sem = self.sem[key]
        self.nops[eng] += 1

        def emit(e):
            for s, v in waits:
                e.wait_ge(s, v)
            ins = fn(e)
            if signal:
                ins.then_inc(sem, 1)

        self.streams[eng].append(emit)

    def dma(self, q, out, in_, reads=(), writes=(), semkey=None, **kw):
        key = ("dma", semkey)
        self._mksem(key)
        waits = self._deps(q, reads, writes)
        self.tick[key] += 16
        ev = (key, self.tick[key])
        self._commit(ev, reads, writes)
        sem = self.sem[key]
        self.nops[q] += 1

        def emit(e):
            for s, v in waits:
                e.wait_ge(s, v)
            e.dma_start(out=out, in_=in_, **kw).then_inc(sem, 16)

        self.streams[q].append(emit)

    def barrier(self, engines=None, skip=()):
        for e in (engines or ENG_ATTR):
            waits = []
            for k, v in self.tick.items():
                if k in skip:
                    continue
                if v > 0 and k != ("eng", e) and self.waited[e].get(k, 0) < v:
                    self.waited[e][k] = v
                    waits.append((self.sem[k], v))

            def emit(en, waits=waits):
                for s, v in waits:
                    en.wait_ge(s, v)

            self.streams[e].append(emit)

    def emit_all(self):
        with self.nc.Block() as block:
            for e, attr in ENG_ATTR.items():
                stream = self.streams[e]

                def body(en, stream=stream):
                    for f in stream:
                        f(en)

                getattr(block, attr)(body)


class Arena:
    def __init__(self, nc, es, name, nbytes):
        self.t = es.enter_context(nc.sbuf_tensor(name, [128, nbytes // 4], F32))
        self.cap = nbytes
        self.off = 0
        self.peak = 0

    def alloc(self, free_shape, dtype, parts=128):
        free_shape = [int(s) for s in free_shape]
        n = int(np.prod(free_shape))
        size = n * _DTSIZE[dtype]
        size = (size + 3) // 4 * 4
        off = (self.off + 63) // 64 * 64
        assert off + size <= self.cap, ("SBUF arena overflow", off, size, self.cap)
        self.off = off + size
        self.peak = max(self.peak, self.off)
        ap = self.t[0:parts, off // 4:(off + size) // 4]
        if dtype != F32:
            ap = ap.bitcast(dtype)
            ap = ap[:, 0:n]
        if len(free_shape) > 1:
            names = [chr(ord("a") + i) for i in range(len(free_shape))]
            kw = {names[i]: free_shape[i] for i in range(len(free_shape) - 1)}
            ap = ap.rearrange("p (" + " ".join(names) + ") -> p " + " ".join(names), **kw)
        return ap

    def mark(self):
        return self.off

    def release(self, m):
        self.off = m


D = 1024
DS = 512
G = 32
NST = 64
T = 16
CTX = 256
NCC = CTX // T
DFF = 2816
NF = DFF // 128
KC = 31
EPS = 1e-6
LN_EPS = 1e-5
TWO_PI = float(2 * np.pi)

_SM = {}
_off = 0
for _n, _w in [("cc", 16), ("n1g", 8), ("n2g", 8), ("s5d", 4), ("bglu", 4), ("confb", 4), ("lng", 4), ("lnb", 4),
               ("confw", 4 * KC), ("fcw", NF * 3), ("fcb", NF), ("ident", 128), ("bmask", 128),
               ("lamr", 32), ("lami", 32), ("ldt", 32)]:
    _SM[_n] = (_off, _w)
    _off += _w
NSM = _off


def build(L, dbg=False, stop_after=None):
    assert L % 1024 == 0
    NX = L // T
    N = NX + NCC
    NS = N // 16
    assert N % 16 == 0
    NT = L // 512
    nc = bass.Bass("TRN2", target_bir_lowering=False)
    dt_in = lambda name, shape: nc.dram_tensor(name, shape, F32, kind="ExternalInput").ap()
    x_d = dt_in("x", [L, D])
    ctx_d = dt_in("ctx", [CTX, D])
    smalls = dt_in("smalls", [128, NSM])
    s5b_d = dt_in("s5b", [128, 2, 512])
    s5c_d = dt_in("s5c", [128, 2, 512])
    bmod_d = dt_in("bmod", [1, 6 * D])
    fg_d = dt_in("fg", [1, D])
    wmod_d = dt_in("wmod", [128, 12, 8 * 512])
    win_d = dt_in("win", [128, 8 * 1536])
    wglu_d = dt_in("wglu", [128, 4 * 512])
    wout_d = dt_in("wout", [128, 8, 1024])
    wup_d = dt_in("wup", [NF * 128 * 2, 1024])
    wdn_d = dt_in("wdn", [128, NF, 1024])
    y_d = nc.dram_tensor("y", [L, D], F32, kind="ExternalOutput").ap()
    ikind = "ExternalOutput" if dbg else "Internal"
    u_d = nc.dram_tensor("u_d", [DS, L], BF16, kind=ikind).ap()
    Sd = nc.dram_tensor("Sd", [4, 128, N, 16], F32, kind=ikind).ap()
    Hd = nc.dram_tensor("Hd", [128, 64 * N], BF16, kind=ikind).ap()
    mix_d = nc.dram_tensor("mix_d", [DS, L], BF16, kind=ikind).ap()
    h1_d = nc.dram_tensor("h1_d", [L, D], F32, kind=ikind).ap()
    wu_d = nc.dram_tensor("wu_d", [NF * 128 * 2, 1024], BF16, kind="Internal").ap()
    wo_d = nc.dram_tensor("wo_d", [128, 8 * 1024], BF16, kind="Internal").ap()
    wd_d = nc.dram_tensor("wd_d", [128, NF * 1024], BF16, kind="Internal").ap()
    hw_d = nc.dram_tensor("hw_d", [128, 16 * 2 * 2 * 16 * 32], BF16, kind="Internal").ap()
    lw_d = nc.dram_tensor("lw_d", [128, 2 * 4 * 16 * 128], BF16, kind="Internal").ap()
    sw_d = nc.dram_tensor("sw_d", [128, 2 * 4 * 16 * 2 * 128], BF16, kind="Internal").ap()
    mc_d = nc.dram_tensor("mc_d", [128, 512], F32, kind=ikind).ap()

    es = ExitStack()
    P = Prog(nc, es)

    def finish():
        P.barrier()
        P.emit_all()
        print("[build] ops per engine:", P.nops, "arena peak KB:", A.peak / 1024, flush=True)
        es.close()
        return nc

    A = Arena(nc, es, "arena", 204 * 1024)
    pst = es.enter_context(nc.psum_tensor("ps", [128, 4096], F32))
    ps = pst[:, :]
    psb = ps.bitcast(BF16)
    bank_ctr = [0]

    def nb():
        b = bank_ctr[0] % 8
        bank_ctr[0] += 1
        return b

    def nbg(n):
        b = (bank_ctr[0] + n - 1) // n * n % 8
        bank_ctr[0] = (bank_ctr[0] + n - 1) // n * n + n
        return b

    def B(b, w=512):
        return ps[:, b * 512:b * 512 + w]

    def mm(out, lhsT, rhs, start, stop, reads, writes, signal=None, **kw):
        if signal is None:
            signal = stop
        P.op("pe", lambda e: e.matmul(out, lhsT=lhsT, rhs=rhs, start=start, stop=stop, **kw),
             reads=reads, writes=writes, signal=signal)

    def tt(eng, out, in0, in1, op, reads, writes):
        P.op(eng, lambda e: e.tensor_tensor(out=out, in0=in0, in1=in1, op=op), reads=reads, writes=writes)

    def ts(eng, out, in0, s1, s2, op0, op1, reads, writes):
        if s2 is None:
            P.op(eng, lambda e: e.tensor_scalar(out=out, in0=in0, scalar1=s1, scalar2=None, op0=op0), reads=reads, writes=writes)
        else:
            P.op(eng, lambda e: e.tensor_scalar(out=out, in0=in0, scalar1=s1, scalar2=s2, op0=op0, op1=op1), reads=reads, writes=writes)

    def stt(eng, out, in0, scalar, in1, op0, op1, reads, writes):
        P.op(eng, lambda e: e.scalar_tensor_tensor(out=out, in0=in0, scalar=scalar, in1=in1, op0=op0, op1=op1), reads=reads, writes=writes)

    def act(out, in_, func, reads, writes, **kw):
        P.op("act", lambda e: e.activation(out=out, in_=in_, func=func, **kw), reads=reads, writes=writes)

    def cp(eng, out, in_, reads, writes):
        if eng == "act":
            P.op("act", lambda e: e.copy(out=out, in_=in_), reads=reads, writes=writes)
        else:
            P.op(eng, lambda e: e.tensor_copy(out=out, in_=in_), reads=reads, writes=writes)

    sm = A.alloc([NSM], F32)
    P.dma("sp", sm, smalls[:, :], writes=["sm"], semkey="sm")

    def SMv(name):
        o, w = _SM[name]
        return sm[:, o:o + w]

    ident = SMv("ident")
    bmask = SMv("bmask")
    mods = A.alloc([96], F32)
    mcol = mods[:, 0:48]
    mxcol = mods[:, 48:64]
    a1, a1x, a2 = mods[:, 64:72], mods[:, 72:80], mods[:, 80:88]
    sh1, sh1x, sh2 = mcol[:, 0:8], mxcol[:, 0:8], mcol[:, 24:32]
    identb = A.alloc([128], BF16)
    ones512 = A.alloc([128], BF16)
    scanMA = A.alloc([16, 64], F32)
    scanMB = A.alloc([16, 64], F32)
    cp("dve", identb, ident, ["sm"], ["identb"])
    P.op("pool", lambda e: e.memset(ones512, 1.0 / 512.0), writes=["ones512"])
    persist_mark = A.mark()

    p1_mark = A.mark()
    cs = A.alloc([8, 2], F32)
    st = A.alloc([2, 8, 128], F32)
    gbc = A.alloc([2 * D], F32)
    mch = [A.alloc([512], F32) for _ in range(4)]
    bch = [A.alloc([512], F32) for _ in range(4)]
    tmp4 = A.alloc([4, 128], F32)
    wm = [A.alloc([8, 512], F32) for _ in range(4)]
    wtmp = [A.alloc([1024], F32) for _ in range(2)]
    wst = [A.alloc([1024], BF16) for _ in range(2)]
    act(cs, SMv("cc").rearrange("p (k j) -> p k j", j=2), AF.Silu, ["sm"], ["cs"])
    for j in range(2):
        cp("dve", st[:, j], cs[:, :, j:j + 1].to_broadcast([128, 8, 128]), ["cs"], [("st", j)])
    ident4 = ident.unsqueeze(1).to_broadcast([128, 4, 128])

    p0a_ctr = [0]

    def p0a_load(n):
        sl = p0a_ctr[0] % 4
        p0a_ctr[0] += 1
        P.dma("sp", wm[sl], wmod_d[:, n, :].rearrange("p (k c) -> p k c", k=8), writes=[("wm", sl)], semkey=("wm", sl))
        P.dma("sp", bch[sl], bmod_d[:, n * 512:(n + 1) * 512].partition_broadcast(128), writes=[("bch", sl)], semkey=("bch", sl))
        return sl

    def p0a_chunk(n, sl):
        b = nb()
        for k in range(8):
            mm(B(b), st[:, 0, k, :], wm[sl][:, k, :], k == 0, k == 7, [("st", 0), ("wm", sl)], [("ps", b)])
        if n in (4, 5, 10, 11):
            o0 = {4: 0, 5: 512, 10: 1024, 11: 1536}[n]
            dst, dk = gbc[:, o0:o0 + 512], "gbc"
        else:
            dst, dk = mch[sl], ("mch", sl)
        tt("dve", dst, B(b), bch[sl], ALU.add, [("ps", b), ("bch", sl)], [dk])
        tt("dve", tmp4, dst.rearrange("p (j i) -> p j i", i=128), ident4, ALU.mult, [dk, "sm"], ["tmp4"])
        P.op("dve", lambda e: e.tensor_reduce(out=mcol[:, 4 * n:4 * n + 4], in_=tmp4, axis=AX.X, op=ALU.add), reads=["tmp4"], writes=["mods"])
        if n < 4:
            b = nb()
            for k in range(8):
                mm(B(b), st[:, 1, k, :], wm[sl][:, k, :], k == 0, k == 7, [("st", 1), ("wm", sl)], [("ps", b)])
            tt("dve", mch[sl], B(b), bch[sl], ALU.add, [("ps", b), ("bch", sl)], [("mch", sl)])
            tt("dve", tmp4, mch[sl].rearrange("p (j i) -> p j i", i=128), ident4, ALU.mult, [("mch", sl), "sm"], ["tmp4"])
            P.op("dve", lambda e: e.tensor_reduce(out=mxcol[:, 4 * n:4 * n + 4], in_=tmp4, axis=AX.X, op=ALU.add), reads=["tmp4"], writes=["mods"])

    def p0a_tail():
        stt("dve", a1, mcol[:, 8:16], 1.0, SMv("n1g"), ALU.add, ALU.mult, ["mods", "sm"], ["mods"])
        stt("dve", a1x, mxcol[:, 8:16], 1.0, SMv("n1g"), ALU.add, ALU.mult, ["mods", "sm"], ["mods"])
        stt("dve", a2, mcol[:, 32:40], 1.0, SMv("n2g"), ALU.add, ALU.mult, ["mods", "sm"], ["mods"])
        if dbg:
            P.dma("pool", mc_d[:, 0:96], mods, reads=["mods"], semkey="mcd")

    Swst = [A.alloc([2, 4, 2, 128], BF16) for _ in range(2)]
    sw_dv = sw_d.rearrange("p (d g k r c) -> p d g k r c", d=2, g=4, k=16, r=2)
    Hst = [A.alloc([2, 2, 16, 32], BF16) for _ in range(2)]
    Lst = [A.alloc([2, 4, 128], BF16) for _ in range(2)]
    hw_dv = hw_d.rearrange("p (g j r d q c) -> p g j r d q c", g=4, j=16, r=2, d=2, q=4)
    lw_dv = lw_d.rearrange("p (d g t c) -> p d g t c", d=2, g=4, t=16)
    XB = A.alloc([2, 2, 512], F32)
    XB2 = A.alloc([2, 2, 512], F32)
    Mx2 = [A.alloc([2, 1024], BF16) for _ in range(2)]
    Cx = A.alloc([2, 1024], BF16)
    s5t = A.alloc([24, 32], F32)
    s5i = A.alloc([32], I32)
    pw = A.alloc([2, 2, 32], F32)
    t4 = A.alloc([4, 2, 512], F32)
    P.dma("sp", XB[:, :, 0, :], s5b_d[:, :, :], writes=["XB"], semkey="s5b")
    P.dma("sp", XB[:, :, 1, :], s5c_d[:, :, :], writes=["XB"], semkey="s5c")
    for i_ in range(2):
        P.op("pool", lambda e, i_=i_: e.memset(Hst[i_].rearrange("p d r q c -> p (d r q c)"), 0.0), writes=[("Hst", i_)])
    for i_ in range(2):
        P.op("pool", lambda e, i_=i_: e.memset(Mx2[i_].rearrange("p r c -> p (r c)"), 0.0), writes=[("Mx", i_)])
    P.op("pool", lambda e: e.memset(Cx.rearrange("p r c -> p (r c)"), 0.0), writes=["Cx"])
    lr, li, ldt = SMv("lamr"), SMv("lami"), SMv("ldt")
    R = lambda i: s5t[:, i, :]
    S5 = ["s5t"]
    act(R(0), ldt, AF.Exp, ["sm"], S5)
    tt("dve", R(1), lr, R(0), ALU.mult, ["sm"] + S5, S5)
    tt("dve", R(2), li, R(0), ALU.mult, ["sm"] + S5, S5)
    act(R(3), R(1), AF.Exp, S5, S5)
    ts("dve", R(4), R(2), 1.0 / TWO_PI, None, ALU.mult, None, S5, S5)
    cp("dve", s5i, R(4), S5, ["s5i"])
    cp("dve", R(4), s5i, ["s5i"], S5)
    stt("dve", R(5), R(4), -TWO_PI, R(2), ALU.mult, ALU.add, S5, S5)
    act(R(6), R(5), AF.Sin, S5, S5, scale=0.5)
    act(R(7), R(5), AF.Sin, S5, S5, scale=0.5, bias=float(np.pi / 2))
    stt("dve", R(8), R(6), 2.0, R(7), ALU.mult, ALU.mult, S5, S5)
    tt("dve", R(9), R(6), R(6), ALU.mult, S5, S5)
    ts("dve", R(9), R(9), -2.0, 1.0, ALU.mult, ALU.add, S5, S5)
    lamr, lami = pw[:, 0, 0, :], pw[:, 0, 1, :]
    LAM = A.alloc([2, 32], F32)
    tt("dve", LAM[:, 0, :], R(3), R(9), ALU.mult, S5, ["LAM"])
    tt("dve", LAM[:, 1, :], R(3), R(8), ALU.mult, S5, ["LAM"])
    lam_r, lam_i = LAM[:, 0, :], LAM[:, 1, :]
    ts("dve", R(10), lam_r, -1.0, None, ALU.add, None, ["LAM"], S5)
    tt("dve", R(11), lr, lr, ALU.mult, ["sm"], S5)
    tt("dve", R(12), li, li, ALU.mult, ["sm"], S5)
    tt("dve", R(11), R(11), R(12), ALU.add, S5, S5)
    P.op("dve", lambda e: e.reciprocal(out=R(12), in_=R(11)), reads=S5, writes=S5)
    tt("dve", R(13), R(10), lr, ALU.mult, ["sm"] + S5, S5)
    tt("dve", R(14), lam_i, li, ALU.mult, ["sm", "LAM"], S5)
    tt("dve", R(13), R(13), R(14), ALU.add, S5, S5)
    tt("dve", R(15), R(13), R(12), ALU.mult, S5, S5)
    tt("dve", R(13), lam_i, lr, ALU.mult, ["sm", "LAM"], S5)
    tt("dve", R(14), R(10), li, ALU.mult, ["sm"] + S5, S5)
    tt("dve", R(13), R(13), R(14), ALU.subtract, S5, S5)
    tt("dve", R(16), R(13), R(12), ALU.mult, S5, S5)

    def cmul(outr, outi, ar, ai, br, bi, rd, wr, tmp):
        tt("dve", tmp[0], ar, br, ALU.mult, rd, ["cm0"])
        tt("dve", tmp[1], ai, bi, ALU.mult, rd, ["cm1"])
        tt("dve", tmp[2], ar, bi, ALU.mult, rd, ["cm2"])
        tt("dve", tmp[3], ai, br, ALU.mult, rd, ["cm3"])
        tt("dve", outr, tmp[0], tmp[1], ALU.subtract, ["cm0", "cm1"], wr)
        tt("dve", outi, tmp[2], tmp[3], ALU.add, ["cm2", "cm3"], wr)

    def bc16(a):
        return a.unsqueeze(2).to_broadcast([128, 32, 16])

    v16 = lambda a: a.rearrange("p (x c) -> p x c", c=16)
    tm512 = [v16(t4[:, i, 0, :]) for i in range(4)]
    cmul(v16(XB2[:, 0, 0, :]), v16(XB2[:, 1, 0, :]), bc16(R(15)), bc16(R(16)), v16(XB[:, 0, 0, :]), v16(XB[:, 1, 0, :]), S5 + ["XB"], ["XB2"], tm512)
    cp("dve", XB[:, :, 0, :], XB2[:, :, 0, :], ["XB2"], ["XB"])
    Cxv = Cx.rearrange("p r (x m c) -> p r x m c", m=2, c=16)
    Mxv2 = [m_.rearrange("p r (x m c) -> p r x m c", m=2, c=16) for m_ in Mx2]
    for m in range(2):
        pr = slice(64 * m, 64 * m + 64)
        cp("act", Cxv[pr, 0, :, m, :], v16(XB[pr, 0, 1, :]), ["XB"], ["Cx"])
        act(Cxv[pr, 1, :, m, :], v16(XB[pr, 1, 1, :]), AF.Copy, ["XB"], ["Cx"], scale=-1.0)
    Hstv = [h.rearrange("p d r q (m c) -> p d r q m c", m=2) for h in Hst]
    Xkeys = ["XB", "XB2"]
    tm1024 = [t4[:, i].rearrange("p w (x c) -> p w x c", c=16) for i in range(4)]
    lam_bc = lambda a: a.unsqueeze(1).unsqueeze(3).to_broadcast([128, 2, 32, 16])
    def p0s_iter(k):
        ck, nk = Xkeys[k % 2], Xkeys[(k + 1) % 2]
        Xc, Xn = (XB, XB2) if k % 2 == 0 else (XB2, XB)
        if k <= 15:
            for m in range(2):
                pr = slice(64 * m, 64 * m + 64)
                for r in range(2):
                    cp("act", Mxv2[k % 2][pr, r, :, m, :], v16(Xc[pr, r, 0, :]), [ck], [("Mx", k % 2)])
            for r in range(2):
                for d in range(2):
                    b = nb()
                    for gt in range(4):
                        c0 = (d * 16 + gt * 4) * 32
                        P.op("pe", lambda e, b=b, gt=gt, c0=c0, r=r: e.transpose(out=psb[:, b * 1024 + gt * 128:b * 1024 + (gt + 1) * 128], in_=Mx2[k % 2][:, r, c0:c0 + 128], identity=identb),
                             reads=[("Mx", k % 2), "identb"], writes=[("ps", b)], signal=(gt == 3))
                    cp("act", Swst[k % 2][:, d, :, r, :], psb[:, b * 1024:b * 1024 + 512].rearrange("p (g c) -> p g c", g=4), [("ps", b)], [("Swst", k % 2)])
            P.dma("pool", sw_dv[:, :, :, k, :, :], Swst[k % 2], reads=[("Swst", k % 2)], writes=["sw_d"], semkey=("Swst", k % 2))
            for d in range(2):
                b = nb()
                for gt in range(4):
                    c0 = (d * 16 + gt * 4) * 32
                    mm(B(b)[:, gt * 128:(gt + 1) * 128], Mx2[k % 2][:, 0, c0:c0 + 128], Cx[:, 0, c0:c0 + 128], True, False, [("Mx", k % 2), "Cx"], [("ps", b)], signal=False)
                    mm(B(b)[:, gt * 128:(gt + 1) * 128], Mx2[k % 2][:, 1, c0:c0 + 128], Cx[:, 1, c0:c0 + 128], False, True, [("Mx", k % 2), "Cx"], [("ps", b)], signal=(gt == 3))
                tt("dve", Lst[k % 2][:, d, :, :], B(b).rearrange("p (g c) -> p g c", g=4), bmask.unsqueeze(1).to_broadcast([128, 4, 128]), ALU.mult,
                   [("ps", b), "sm"], [("Lst", k % 2)])
            P.dma("pool", lw_dv[:, :, :, k, :], Lst[k % 2], reads=[("Lst", k % 2)], writes=["lw_d"], semkey=("Lst", k % 2))
        if k >= 1:
            kk = k - 1
            for d in range(2):
                j = kk if d == 0 else 15 - kk
                for m in range(2):
                    pr = slice(64 * m, 64 * m + 64)
                    src = lambda r: Xc[pr, r, 1, d * 256:(d + 1) * 256].rearrange("p (q c) -> p q c", c=16)
                    cp("act", Hstv[k % 2][pr, d, 0, :, m, :], src(0), [ck], [("Hst", k % 2)])
                    act(Hstv[k % 2][pr, d, 1, :, m, :], src(1), AF.Copy, [ck], [("Hst", k % 2)], scale=-1.0)
                for r in range(2):
                    P.dma("pool", hw_dv[:, :, j, r, d, :, :].rearrange("p g q c -> p g (q c)"), Hst[k % 2][:, d, r].rearrange("p (g q) c -> p g (q c)", g=4),
                          reads=[("Hst", k % 2)], writes=["hw_d"], semkey=("Hst", k % 2))
        if k < 16:
            cmul(Xn[:, 0].rearrange("p w (x c) -> p w x c", c=16), Xn[:, 1].rearrange("p w (x c) -> p w x c", c=16),
                 lam_bc(lam_r), lam_bc(lam_i),
                 Xc[:, 0].rearrange("p w (x c) -> p w x c", c=16), Xc[:, 1].rearrange("p w (x c) -> p w x c", c=16),
                 [ck, "LAM"], [nk], tm1024)
    def pw_gen():
        tm32 = [t4[:, i, 0, 0:32] for i in range(4)]
        cp("dve", pw[:, 0], LAM, ["LAM"], [("pw", 0)])
        cur = 0
        for _ in range(15):
            cmul(pw[:, 1 - cur, 0, :], pw[:, 1 - cur, 1, :], lam_r, lam_i, pw[:, cur, 0, :], pw[:, cur, 1, :], ["LAM", ("pw", cur)], [("pw", 1 - cur)], tm32)
            yield
            cur = 1 - cur
        MU = A.alloc([2, 32], F32)
        cp("dve", MU, pw[:, cur], [("pw", cur)], ["MU"])
        cp("dve", pw[:, 0], MU, ["MU"], [("pw", 0)])
        cur = 0
        MAv = scanMA.rearrange("p e (qt r q) -> p e qt r q", r=2, q=8)
        MBv = scanMB.rearrange("p e (qt r q) -> p e qt r q", r=2, q=8)
        for e_ in range(16):
            pr_ = pw[:, cur, 0, :].rearrange("p (qt q) -> p qt q", q=8)
            pi_ = pw[:, cur, 1, :].rearrange("p (qt q) -> p qt q", q=8)
            for r in range(2):
                cp("act", MAv[:, e_, :, r, :], pr_, [("pw", cur)], ["scanM"])
            act(MBv[:, e_, :, 0, :], pi_, AF.Copy, [("pw", cur)], ["scanM"], scale=-1.0)
            cp("act", MBv[:, e_, :, 1, :], pi_, [("pw", cur)], ["scanM"])
            if e_ < 15:
                cmul(pw[:, 1 - cur, 0, :], pw[:, 1 - cur, 1, :], MU[:, 0, :], MU[:, 1, :], pw[:, cur, 0, :], pw[:, cur, 1, :], ["MU", ("pw", cur)], [("pw", 1 - cur)], tm32)
                yield
                cur = 1 - cur

        yield
    fold_q = []

    def fold_step(i):
        sl = i % 2
        src = wout_d[:, i, :] if i < 8 else wdn_d[:, i - 8, :]
        gate = gbc[:, 0:D] if i < 8 else gbc[:, D:2 * D]
        dst = wo_d[:, i * 1024:(i + 1) * 1024] if i < 8 else wd_d[:, (i - 8) * 1024:(i - 7) * 1024]
        P.dma("sp", wtmp[sl], src, writes=[("wtmp", sl)], semkey=("wtmp", sl))
        tt("dve", wst[sl], wtmp[sl], gate, ALU.mult, [("wtmp", sl), "gbc"], [("wst", sl)])
        P.dma("pool", dst, wst[sl], reads=[("wst", sl)], writes=["wo_d" if i < 8 else "wd_d"], semkey=("wst", sl))

    pwg = pw_gen()
    pw_done = [False]

    def pw_steps(n_):
        for _ in range(n_):
            if not pw_done[0]:
                try:
                    next(pwg)
                except StopIteration:
                    pw_done[0] = True

    kq = list(range(17))
    order = [0, 1, 2, 3, 4, 5, 10, 11, 6, 7, 8, 9]
    slq = [p0a_load(order[i_]) for i_ in range(3)]
    for oi, n in enumerate(order):
        if oi + 3 < len(order):
            slq.append(p0a_load(order[oi + 3]))
        p0a_chunk(n, slq.pop(0))
        if n == 5:
            fold_q.extend(range(8))
        if n == 11:
            fold_q.extend(range(8, 8 + NF))
        for _ in range(2 if n % 2 == 0 else 1):
            if kq:
                p0s_iter(kq.pop(0))
        pw_steps(3)
        for _ in range(4):
            if fold_q:
                fold_step(fold_q.pop(0))
    while kq:
        p0s_iter(kq.pop(0))
        pw_steps(3)
    pw_steps(1000)
    while fold_q:
        fold_step(fold_q.pop(0))
    p0a_tail()
    P.barrier()
    A.release(p1_mark)
    if stop_after == "p0s":
        return finish()

    def norm_part(xsl, xkey, nsub, xn, xnkey, junk, ssb, ssbkey):
        ss, rs = ssb[:, 0:nsub], ssb[:, 4:4 + nsub]
        for s_ in range(nsub):
            act(junk, xsl[:, s_, :], AF.Square, [xkey], ["junk", ssbkey], accum_out=ss[:, s_:s_ + 1])
        ts("dve", rs, ss, 1.0 / D, EPS, ALU.mult, ALU.add, [ssbkey], [ssbkey])
        act(rs, rs, AF.Sqrt, [ssbkey], [ssbkey])
        P.op("dve", lambda e: e.reciprocal(out=rs, in_=rs), reads=[ssbkey], writes=[ssbkey])
        for s_ in range(nsub):
            ts("dve", xn[:, s_, :], xsl[:, s_, :], rs[:, s_:s_ + 1], None, ALU.mult, None, [xkey, ssbkey], [xnkey])

    def tr_part(nsub, a_col, sh_col, hn, hnkey, xn, xnkey):
        for kp in range(4):
            b = nb()
            for kk in range(2):
                k = 2 * kp + kk
                for s_ in range(nsub):
                    o0 = b * 1024 + kk * 512 + s_ * 128
                    P.op("pe", lambda e, o0=o0, s_=s_, k=k: e.transpose(out=psb[:, o0:o0 + 128], in_=xn[:, s_, k * 128:(k + 1) * 128], identity=identb),
                         reads=[xnkey, "identb"], writes=[("ps", b)], signal=(kk == 1 and s_ == nsub - 1))
            for kk in range(2):
                k = 2 * kp + kk
                o0 = b * 1024 + kk * 512
                if kk == 0:
                    ts("dve", hn[:, k, 0:nsub * 128], psb[:, o0:o0 + nsub * 128], a_col[:, k:k + 1], sh_col[:, k:k + 1], ALU.mult, ALU.add,
                       [("ps", b), "mods"], [hnkey])
                else:
                    act(hn[:, k, 0:nsub * 128], psb[:, o0:o0 + nsub * 128], AF.Identity, [("ps", b), "mods"], [hnkey],
                        scale=a_col[:, k:k + 1], bias=sh_col[:, k:k + 1])

    def norm_to_fm(xsl, xkey, nsub, a_col, sh_col, hn, hnkey, xn, junk, ssb):
        norm_part(xsl, xkey, nsub, xn, "xn", junk, ssb, "ssb")
        tr_part(nsub, a_col, sh_col, hn, hnkey, xn, "xn")

    NCT = NCC + NX
    us_jm = A.alloc([4, 16, NCT], BF16)
    p12_mark = A.mark()
    win_bf = A.alloc([8, 512], BF16)
    P.dma("pool", win_bf, win_d[:, :].rearrange("p (k c) -> p k c", k=8)[:, :, 0:512], writes=["win_bf"], semkey="win_bf")
    xs = [A.alloc([4, 1024], F32) for _ in range(2)]
    xn = A.alloc([4, 1024], BF16)
    hns = [A.alloc([8, 512], BF16) for _ in range(2)]
    junk = A.alloc([1024], BF16)
    ssb = A.alloc([8], F32)
    xn1 = [xn, A.alloc([4, 1024], BF16)]
    ssb1 = [ssb, A.alloc([8], F32)]

    def p1_load(ti):
        sl = ti % 2
        if ti == 0:
            P.dma("sp", xs[sl][:, 0:2, :], ctx_d.rearrange("(s p) d -> p s d", p=128), writes=[("xs", sl)], semkey=("xs", sl))
        else:
            t0 = (ti - 1) * 512
            P.dma("sp", xs[sl], x_d[t0:t0 + 512, :].rearrange("(s p) d -> p s d", p=128), writes=[("xs", sl)], semkey=("xs", sl))

    for hf in range(4):
        r0 = hf * (NF * 128 * 2 // 4)
        r1 = (hf + 1) * (NF * 128 * 2 // 4)
        P.dma("pool", wu_d[r0:r1, :], wup_d[r0:r1, :], writes=[("wu_d", hf)], semkey=("wucast", hf))
    p1_load(0)
    p1_load(1)
    for ti in range(NT + 1):
        sl = ti % 2
        isctx = ti == 0
        nsub = 2 if isctx else 4
        ntok = nsub * 128
        nch = ntok // T
        cof = 0 if isctx else NCC + (ti - 1) * 32
        if ti == 0:
            norm_part(xs[0], ("xs", 0), 2, xn1[0], ("xn1", 0), junk, ssb1[0], ("ssb1", 0))
        tr_part(nsub, a1x if isctx else a1, sh1x if isctx else sh1, hns[sl], ("hn", sl), xn1[sl], ("xn1", sl))
        if ti + 1 <= NT:
            nsl = (ti + 1) % 2
            norm_part(xs[nsl], ("xs", nsl), 4, xn1[nsl], ("xn1", nsl), junk, ssb1[nsl], ("ssb1", nsl))
        for gt in range(4):
            b = nb()
            for k in range(8):
                mm(B(b)[:, 0:ntok], win_bf[:, k, gt * 128:(gt + 1) * 128], hns[sl][:, k, 0:ntok], k == 0, k == 7, ["win_bf", ("hn", sl)], [("ps", b)])
            cp("act", us_jm[:, gt, :, cof:cof + nch], B(b)[:, 0:ntok].rearrange("p (c j) -> p j c", j=16), [("ps", b)], [("us", ti)])
        if ti + 2 <= NT:
            p1_load(ti + 2)
    u_dv = u_d.rearrange("(g p) (j c) -> p g j c", p=128, j=16)
    for gt in range(4):
        P.dma("pool", u_dv[:, gt], us_jm[:, gt, :, NCC:NCT], reads=[("us", ti) for ti in range(NT + 1)] + ["usdma"], writes=["u_d"], semkey=("ust", gt))
    P.barrier(skip=[("dma", ("ust", gt)) for gt in range(4)])
    A.release(p12_mark)
    if stop_after == "p1":
        return finish()

    uskeys = [("us", ti) for ti in range(NT + 1)]
    p3a_mark = A.mark()
    Swq1 = A.alloc([2, 16, 2, 128], BF16)
    Swq = [Swq1, Swq1]
    Wp = A.alloc([N, 32], F32)
    Hs = A.alloc([NS + 1, 32], F32)
    sct = A.alloc([2, NS, 32], F32)
    fx = [A.alloc([NS, 32], F32) for _ in range(2)]
    MA2 = A.alloc([16, 2, 32], F32)
    MB2 = A.alloc([16, 2, 32], F32)
    Hdv = Hd.rearrange("p (h d g e n) -> p h d g e n", h=2, d=2, g=2, e=8)
    for h in range(2):
        for r in range(2):
            srcA = scanMA.rearrange("p e (d h r q) -> p e d h r q", d=2, h=2, r=2)[:, :, :, h, r, :]
            srcB = scanMB.rearrange("p e (d h r q) -> p e d h r q", d=2, h=2, r=2)[:, :, :, h, r, :]
            cp("act", MA2[:, :, h, r * 16:(r + 1) * 16].rearrange("p e (d q) -> p e d q", d=2), srcA, ["scanM"], ["MA2"])
            cp("act", MB2[:, :, h, r * 16:(r + 1) * 16].rearrange("p e (d q) -> p e d q", d=2), srcB, ["scanM"], ["MA2"])

    def swp(a):
        return a.rearrange("p s (r q) -> p s r q", r=2)[:, :, ::-1, :]

    def v4(a):
        return a.rearrange("p s (r q) -> p s r q", r=2)

    wk = "Wp"
    for qh in range(2):
        Hq = us_jm[:, 2 * qh:2 * qh + 2].rearrange("p a (b e) n -> p a b e n", b=2)
        for d in range(2):
            P.dma("sp", Swq[d], sw_dv[:, d, 2 * qh:2 * qh + 2], reads=["sw_d"], writes=["Swq"], semkey="Swq")
            if d == 0:
                wx = Wp[:, NCC:N, :]
                wc = Wp[:, 0:NCC, :]
            else:
                wx = Wp[:, NCC:N, :][:, ::-1, :]
                wc = Wp[:, 0:NCC, :][:, ::-1, :]
            gbc = nbg(4)
            ckeys = [("ps", gbc + i) for i in range(4)]
            cnt = 0
            for r in range(2):
                for g2 in range(2):
                    gt = 2 * qh + g2
                    blk = r * 2 + g2
                    for i in range(16):
                        k = 15 - i if d == 0 else i
                        for qi in range(4):
                            cnt += 1
                            mm(B(gbc + qi)[:, blk * NCC:(blk + 1) * NCC], Swq[d][32 * qi:32 * qi + 32, g2, k, r, :], us_jm[32 * qi:32 * qi + 32, gt, i, 0:NCC],
                               i == 0, i == 15, ["Swq", ("us", 0), ("usrd", qh)], ckeys, signal=(cnt == 256), tile_position=(32 * qi, 0), skip_group_check=True)
            regc = ps[:, gbc * 512:(gbc + 4) * 512].rearrange("p (qi x) -> p qi x", qi=4)[:, :, 0:4 * NCC].rearrange("p qi (r g c) -> p r c g qi", r=2, g=2)
            for r in range(2):
                c0 = r * 16 + d * 8
                cp("act", wc[:, :, c0:c0 + 8].rearrange("p c (g qi) -> p c g qi", qi=4), regc[:, r], ckeys, [wk])
            for r in range(2):
                for g2 in range(2):
                    gt = 2 * qh + g2
                    gb = nbg(4)
                    skeys = [("ps", gb + i) for i in range(4)]
                    for i in range(16):
                        k = 15 - i if d == 0 else i
                        for qi in range(4):
                            mm(B(gb + qi)[:, 0:NX], Swq[d][32 * qi:32 * qi + 32, g2, k, r, :], us_jm[32 * qi:32 * qi + 32, gt, i, NCC:NCT],
                               i == 0, i == 15, ["Swq", ("usrd", qh)] + uskeys, skeys, signal=(i == 15 and qi == 3), tile_position=(32 * qi, 0), skip_group_check=True)
                    c0 = r * 16 + d * 8 + g2 * 4
                    src = ps[:, gb * 512:(gb + 4) * 512].rearrange("p (qi x) -> p qi x", qi=4)[:, :, 0:NX].rearrange("p qi n -> p n qi")
                    cp("act", wx[:, :, c0:c0 + 4], src, skeys, [wk])
        eng = "dve"
        Wv = Wp.rearrange("p (s r) c -> p s r c", r=16)
        MAe = lambda e_, n_: MA2[:, e_, qh, :].unsqueeze(1).to_broadcast([128, n_, 32])
        MBe = lambda e_, n_: v4(MB2[:, e_, qh, :].unsqueeze(1).to_broadcast([128, n_, 32]))
        t1, t2 = sct[:, 0], sct[:, 1]
        for step in range(1, 16):
            prev, cur_ = Wv[:, :, step - 1, :], Wv[:, :, step, :]
            tt(eng, t1, prev, MAe(0, NS), ALU.mult, [wk, "MA2"], ["sct1"])
            tt(eng, v4(t2), swp(prev), MBe(0, NS), ALU.mult, [wk, "MA2"], ["sct2"])
            tt(eng, t1, t1, t2, ALU.add, ["sct1", "sct2"], ["sct1"])
            tt(eng, cur_, cur_, t1, ALU.add, [wk, "sct1"], [wk])
        P.op(eng, lambda e: e.memset(Hs.rearrange("p s c -> p (s c)"), 0.0), writes=["Hs"])
        for s_ in range(NS):
            hp, hd_ = Hs[:, s_:s_ + 1, :], Hs[:, s_ + 1:s_ + 2, :]
            a1_, a2_ = sct[:, 0, 0:1, :], sct[:, 1, 0:1, :]
            tt(eng, a1_, hp, MAe(15, 1), ALU.mult, ["Hs", "MA2"], ["sct1"])
            tt(eng, v4(a2_), swp(hp), MBe(15, 1), ALU.mult, ["Hs", "MA2"], ["sct2"])
            tt(eng, a1_, a1_, a2_, ALU.add, ["sct1", "sct2"], ["sct1"])
            tt(eng, hd_, Wv[:, s_, 15:16, :], a1_, ALU.add, [wk, "sct1"], ["Hs"])
        hsp = Hs[:, 0:NS, :]
        for r_ in range(16):
            tt(eng, t1, hsp, MAe(r_, NS), ALU.mult, ["Hs", "MA2"], ["sct1"])
            tt(eng, v4(t2), swp(hsp), MBe(r_, NS), ALU.mult, ["Hs", "MA2"], ["sct2"])
            tt(eng, t1, t1, t2, ALU.add, ["sct1", "sct2"], ["sct1"])
            fxb = fx[r_ % 2]
            tt(eng, fxb, Wv[:, :, r_, :], t1, ALU.add, [wk, "sct1"], [("fx", r_ % 2)])
            fxv = fxb.rearrange("p s (r d g q) -> p s r d g q", r=2, d=2, g=2)
            for d in range(2):
                for g2 in range(2):
                    hv = Hq[:, d, g2]
                    if d == 1:
                        hv = hv[:, :, ::-1]
                    dstv = hv.rearrange("p (ri q) (s r) -> p s r ri q", ri=2, r=16)[:, :, r_, :, :]
                    cp("act" if (d + g2) % 2 == 0 else "pool", dstv, fxv[:, :, :, d, g2, :], [("fx", r_ % 2)], ["Hq", ("usrd", qh)] + (["usdma"] if r_ == 0 else []))
        P.dma("pool", Hdv[:, qh], Hq, reads=["Hq"], writes=["Hd"], semkey="Hq")
    P.barrier()
    A.release(persist_mark)
    if stop_after == "p2":
        return finish()

    dg31 = A.alloc([4, KC, 128], BF16)
    p3b_mark = A.mark()
    cwv = SMv("confw").rearrange("p (m k) -> p m k", k=KC)
    dgq = [(m, k) for m in range(4) for k in range(KC)]
    ub = [A.alloc([16, NX], BF16) for _ in range(2)]
    Hg = [A.alloc([2, 8, N], BF16) for _ in range(2)]
    Hwg = [A.alloc([16, 2, 2, 4, 32], BF16) for _ in range(2)]
    lwg = [A.alloc([2, 16, 128], BF16) for _ in range(2)]
    ysb = [A.alloc([NX, 4], F32) for _ in range(2)]
    ygs = A.alloc([L], BF16)
    hw_dv2 = hw_d.rearrange("p (g x) -> p g x", g=4)
    lw_dv2 = lw_d.rearrange("p (d g t c) -> p d g t c", d=2, g=4, t=16)
    mix_dv = mix_d.rearrange("(g p) t -> p g t", p=128)
    Dcol = SMv("s5d")

    def p3a_load(gt):
        sl = gt % 2
        P.dma("sp", ub[sl], u_dv[:, gt], reads=["u_d"], writes=[("ub", sl)], semkey=("ub", sl))
        P.dma("sp", Hg[sl], Hdv[:, gt // 2, :, gt % 2], reads=["Hd"], writes=[("Hg", sl)], semkey=("Hg", sl))
        P.dma("sp", Hwg[sl].rearrange("p j r d q c -> p (j r d q c)"), hw_dv2[:, gt, :], reads=["hw_d"], writes=[("Hwg", sl)], semkey=("Hwg", sl))
        P.dma("sp", lwg[sl], lw_dv2[:, :, gt], reads=["lw_d"], writes=[("lwg", sl)], semkey=("lwg", sl))

    p3a_load(0)
    for gt in range(4):
        sl = gt % 2
        if gt + 1 < 4:
            p3a_load(gt + 1)
        ux = ub[sl]
        rd = [("ub", sl), ("lwg", sl), ("Hwg", sl), ("Hg", sl)]
        for jq in range(4):
            gb = nbg(4)
            ykeys = [("ps", gb + i) for i in range(4)]
            for jl in range(4):
                j = 4 * jq + jl
                ob = B(gb + jl)[:, 0:NX]
                mm(ob, lwg[sl][:, 0, 0, :], ux[:, j, :], True, False, rd, ykeys, signal=False, skip_group_check=True)
                for tau in range(1, j + 1):
                    mm(ob, lwg[sl][:, 0, tau, :], ux[:, j - tau, :], False, False, rd, ykeys, signal=False, skip_group_check=True)
                for tau in range(0, 16 - j):
                    mm(ob, lwg[sl][:, 1, tau, :], ux[:, j + tau, :], False, False, rd, ykeys, signal=False, skip_group_check=True)
                cnt = 0
                for d in range(2):
                    n0 = NCC - 1 if d == 0 else 1
                    for r in range(2):
                        for qi in range(4):
                            cnt += 1
                            last = (cnt == 16 and jl == 3)
                            mm(B(gb + jl)[32 * qi:32 * qi + 32, 0:NX], Hwg[sl][:, j, r, d, qi, :], Hg[sl][:, d, r * 4 + qi, n0:n0 + NX], False, cnt == 16, rd, ykeys,
                               signal=last, tile_position=(0, 32 * qi), skip_group_check=True)
            ysl = jq % 2
            pin = ps[:, gb * 512:(gb + 4) * 512].rearrange("p (jl x) -> p jl x", jl=4)[:, :, 0:NX].rearrange("p jl c -> p c jl")
            uin = ux[:, 4 * jq:4 * jq + 4, :].rearrange("p j c -> p c j")
            stt("dve", ysb[ysl], uin, Dcol[:, gt:gt + 1], pin, ALU.mult, ALU.add, [("ub", sl), "sm"] + ykeys, [("ysb", ysl)])
            act(ygs.rearrange("p (c j) -> p c j", j=16)[:, :, 4 * jq:4 * jq + 4], ysb[ysl], AF.Gelu_apprx_tanh, [("ysb", ysl)], ["ygs"])
            for _ in range(8):
                if dgq:
                    m_, k_ = dgq.pop(0)
                    act(dg31[:, m_, k_, :], ident, AF.Copy, ["sm"], ["dg31"], scale=cwv[:, m_, k_:k_ + 1])
        P.dma("pool", mix_dv[:, gt, :], ygs, reads=["ygs"], writes=["mix_d"], semkey="ygs")
    assert not dgq
    P.barrier()
    A.release(p3b_mark)
    if stop_after == "p3a":
        return finish()

    wglu = A.alloc([4, 512], BF16)
    P.dma("pool", wglu, wglu_d[:, :].rearrange("p (k c) -> p k c", k=4), writes=["wglu"], semkey="wglu")
    bglu = SMv("bglu")
    mixg = A.alloc([4, 512], BF16)
    sgg = None
    wvg = A.alloc([8, 1024], BF16)
    wo_bf = A.alloc([8, 1024], BF16)
    P.dma("pool", wvg, win_d[:, :].rearrange("p (k c) -> p k c", k=8)[:, :, 512:1536], writes=["wvg"], semkey="wvg")
    P.dma("sp", wo_bf.rearrange("p k c -> p (k c)"), wo_d[:, :], reads=["wo_d"], writes=["wo_bf"], semkey="wo_bf")
    xs = [A.alloc([4, 1024], F32) for _ in range(2)]
    xn = A.alloc([4, 1024], BF16)
    hn = A.alloc([8, 512], BF16)
    junk = A.alloc([1024], BF16)
    ssb = A.alloc([8], F32)
    cbuf = A.alloc([4, 8, 64 + KC - 1], BF16)
    sgm = [A.alloc([512], F32) for _ in range(2)]
    sgg = sgm
    hcb = A.alloc([4, 512], BF16)
    sq2 = A.alloc([4, 512], BF16)
    sq = A.alloc([4, 512], BF16)
    lnt = A.alloc([4, 512], F32)
    mixc = A.alloc([4, 512], BF16)
    msb = [A.alloc([4, 512], BF16) for _ in range(2)]
    P.op("pool", lambda e: e.memset(cbuf.rearrange("p m r c -> p (m r c)"), 0.0), writes=["cbuf"])
    confb, lng, lnb = SMv("confb"), SMv("lng"), SMv("lnb")

    def p3b1_load(ti):
        sl = ti % 2
        t0 = ti * 512
        P.dma("sp", xs[sl], x_d[t0:t0 + 512, :].rearrange("(s p) d -> p s d", p=128), writes=[("xs", sl)], semkey=("xs", sl))
        P.dma("sp", msb[sl], mix_dv[:, :, t0:t0 + 512], reads=["mix_d"], writes=[("msb", sl)], semkey=("msb", sl))

    hc2 = [hcb, sq2]
    mixg2 = [mixg, A.alloc([4, 512], BF16)]
    lnm = [A.alloc([512], F32) for _ in range(2)]
    lne = [A.alloc([512], F32) for _ in range(2)]
    xs3 = xs + [A.alloc([4, 1024], F32)]

    def ld3(ti):
        t0 = ti * 512
        P.dma("sp", xs3[ti % 3], x_d[t0:t0 + 512, :].rearrange("(s p) d -> p s d", p=128), writes=[("xs", ti % 3)], semkey=("xs", ti % 3))

    def ldm(ti):
        t0 = ti * 512
        P.dma("sp", msb[ti % 2], mix_dv[:, :, t0:t0 + 512], reads=["mix_d"], writes=[("msb", ti % 2)], semkey=("msb", ti % 2))

    def st_norm(ti):
        norm_part(xs3[ti % 3], ("xs", ti % 3), 4, xn, "xn", junk, ssb, "ssb")

    def st_tr(ti):
        tr_part(4, a1, sh1, hn, "hn", xn, "xn")

    def st_a1(ti):
        for m in range(4):
            bg, bv = nb(), nb()
            for k in range(8):
                mm(B(bg), wvg[:, k, 512 + m * 128:512 + (m + 1) * 128], hn[:, k, :], k == 0, k == 7, ["wvg", "hn"], [("ps", bg)])
            for k in range(8):
                mm(B(bv), wvg[:, k, m * 128:(m + 1) * 128], hn[:, k, :], k == 0, k == 7, ["wvg", "hn"], [("ps", bv)])
            act(sgm[m % 2], B(bg), AF.Sigmoid, [("ps", bg)], [("sgm", m % 2)])
            tt("dve", cbuf[:, m, :, 15:79], B(bv).rearrange("p (r c) -> p r c", c=64), sgm[m % 2].rearrange("p (r c) -> p r c", c=64), ALU.mult,
               [("ps", bv), ("sgm", m % 2), "cbuf"], [("cbuf", m)])

    def st_a2(ti):
        sl = ti % 2
        for m in range(4):
            b = nb()
            for k in range(KC):
                mm(B(b).rearrange("p (r c) -> p r c", c=64), dg31[:, m, k, :], cbuf[:, m, :, k:k + 64], k == 0, k == KC - 1, ["dg31", ("cbuf", m), "cbuf"], [("ps", b)])
            act(hc2[sl][:, m, :], B(b), AF.Identity, [("ps", b), "sm"], [("hc", sl, m)], bias=confb[:, m:m + 1])
            act(sq[:, m, :], B(b), AF.Square, [("ps", b), "sm"], [("sq", m)], bias=confb[:, m:m + 1])

    def st_a3(ti):
        sl = ti % 2
        bm_, be_ = nb(), nb()
        for m in range(4):
            mm(B(bm_), ones512, hc2[sl][:, m, :], m == 0, m == 3, ["ones512", ("hc", sl, m)], [("ps", bm_)])
        for m in range(4):
            mm(B(be_), ones512, sq[:, m, :], m == 0, m == 3, ["ones512", ("sq", m)], [("ps", be_)])
        cp("act", lnm[sl], B(bm_), [("ps", bm_)], [("lnm", sl)])
        cp("act", lne[sl], B(be_), [("ps", be_)], [("lne", sl)])
        for m in range(4):
            b = nb()
            for k in range(4):
                mm(B(b), wglu[:, k, m * 128:(m + 1) * 128], msb[sl][:, k, :], k == 0, k == 3, ["wglu", ("msb", sl)], [("ps", b)])
            act(sgg[m % 2], B(b), AF.Sigmoid, [("ps", b), "sm"], [("sgm", m % 2)], bias=bglu[:, m:m + 1])
            tt("pool", mixg2[sl][:, m, :], msb[sl][:, m, :], sgg[m % 2], ALU.mult, [("msb", sl), ("sgm", m % 2)], [("mixg", sl, m)])

    def st_b1(ti):
        sl = ti % 2
        mean_sb, msq, rstd_ = lnm[sl], lnt[:, 1, :], lnt[:, 2, :]
        tt("dve", msq, mean_sb, mean_sb, ALU.mult, [("lnm", sl)], ["ln_msq"])
        tt("dve", msq, lne[sl], msq, ALU.subtract, [("lne", sl), "ln_msq"], ["ln_msq"])
        ts("dve", msq, msq, LN_EPS, None, ALU.add, None, ["ln_msq"], ["ln_msq"])
        act(rstd_, msq, AF.Sqrt, ["ln_msq"], ["ln_rstd"])
        lts = [lnt[:, 0, :], lnt[:, 3, :]]
        for m in range(2):
            tt("dve", lts[m], hc2[sl][:, m, :], mean_sb, ALU.subtract, [("hc", sl, m), ("lnm", sl)], [("ln_t", m)])
        P.op("dve", lambda e: e.reciprocal(out=rstd_, in_=rstd_), reads=["ln_rstd"], writes=["ln_rstd"])
        for m in range(4):
            lt_, lk = lts[m % 2], ("ln_t", m % 2)
            if m >= 2:
                tt("dve", lt_, hc2[sl][:, m, :], mean_sb, ALU.subtract, [("hc", sl, m), ("lnm", sl)], [lk])
            tt("dve", lt_, lt_, rstd_, ALU.mult, [lk, "ln_rstd"], [lk])
            act(mixc[:, m, :], lt_, AF.Silu, [lk, "sm"], [("mixc", m)], scale=lng[:, m:m + 1], bias=lnb[:, m:m + 1])

    def st_b2(ti):
        sl = ti % 2
        x3 = xs3[ti % 3]
        xk = ("xs", ti % 3)
        t0 = ti * 512
        for s_ in range(4):
            for hf in range(2):
                b = nb()
                for k in range(8):
                    lh = mixg2[sl][:, k, s_ * 128:(s_ + 1) * 128] if k < 4 else mixc[:, k - 4, s_ * 128:(s_ + 1) * 128]
                    rk = ("mixg", sl, k) if k < 4 else ("mixc", k - 4)
                    mm(B(b), lh, wo_bf[:, k, hf * 512:(hf + 1) * 512], k == 0, k == 7, [rk, "wo_bf"], [("ps", b)])
                tt("dve", x3[:, s_, hf * 512:(hf + 1) * 512], B(b), x3[:, s_, hf * 512:(hf + 1) * 512], ALU.add, [("ps", b), xk], [xk])
        P.dma("pool", h1_d[t0:t0 + 512, :].rearrange("(s p) d -> p s d", p=128), x3, reads=[xk], writes=["h1_d"], semkey=("h1st", ti % 3))

    for ti in range(min(3, NT)):
        ld3(ti)
    for ti in range(min(2, NT)):
        ldm(ti)
    st_norm(0)
    st_tr(0)
    st_a1(0)
    st_a2(0)
    if NT > 1:
        st_norm(1)
        st_tr(1)
    st_a3(0)
    for ti in range(NT):
        if ti + 1 < NT:
            st_a1(ti + 1)
        st_b1(ti)
        if ti + 2 < NT:
            st_norm(ti + 2)
        if ti + 1 < NT:
            st_a2(ti + 1)
        st_b2(ti)
        if ti + 2 < NT:
            st_tr(ti + 2)
        if ti + 1 < NT:
            st_a3(ti + 1)
        if ti + 3 < NT:
            ld3(ti + 3)
        if ti + 2 < NT:
            ldm(ti + 2)
    P.barrier()
    A.release(persist_mark)
    if stop_after == "p3b1":
        return finish()

    wd_bf = A.alloc([NF, 1024], BF16)
    fg_bc = A.alloc([1024], F32)
    P.dma("sp", wd_bf.rearrange("p f c -> p (f c)"), wd_d[:, :], reads=["wd_d"], writes=["wd_bf"], semkey="wd_bf")
    P.dma("sp", fg_bc, fg_d.partition_broadcast(128), writes=["fg_bc"], semkey="fg_bc")
    fwv = SMv("fcw").rearrange("p (f k) -> p f k", k=3)
    fcb = SMv("fcb")
    hs_ = [A.alloc([4, 1024], F32) for _ in range(2)]
    xn = A.alloc([4, 1024], BF16)
    hn2 = A.alloc([8, 512], BF16)
    junk = A.alloc([1024], BF16)
    ssb = A.alloc([8], F32)
    ssf = A.alloc([8], F32)
    sgf = [A.alloc([512], F32) for _ in range(3)]
    actb = A.alloc([NF, 512], BF16)
    wus = [A.alloc([8, 256], BF16) for _ in range(3)]
    ot = [A.alloc([4, 1024], F32) for _ in range(1)]
    wu_dv = wu_d.rearrange("(f p h) c -> f p (h c)", p=128, h=2)
    def p3b2_load(ti):
        sl = ti % 2
        t0 = ti * 512
        P.dma("pool", hs_[sl], h1_d[t0:t0 + 512, :].rearrange("(s p) d -> p s d", p=128), reads=["h1_d"], writes=[("hs", sl)], semkey=("hs", sl))

    wu_ctr = [0]

    def wu_load(f_):
        sl3 = wu_ctr[0] % 3
        wu_ctr[0] += 1
        P.dma("sp", wus[sl3].rearrange("p k c -> p (k c)"), wu_dv[f_], reads=[("wu_d", i_) for i_ in range(4)], writes=[("wus", sl3)], semkey=("wus", sl3))
        return sl3

    xns = [xn, A.alloc([4, 1024], BF16)]
    p3b2_load(0)
    if NT > 1:
        p3b2_load(1)
    pend = [wu_load(0), wu_load(1)]
    norm_part(hs_[0], ("hs", 0), 4, xns[0], ("xn2", 0), junk, ssb, "ssb")
    tr_part(4, a2, sh2, hn2, "hn2", xns[0], ("xn2", 0))

    cvb = [A.alloc([512], F32) for _ in range(3)]

    def ffn_tail1(f_, bv, bg, gs):
        cv = cvb[gs]
        cvv = cv.rearrange("p (r c) -> p r c", c=64)
        gv = B(bg).rearrange("p (r c) -> p r c", c=64)
        act(cv, B(bg), AF.Copy, [("ps", bg), "sm"], [("cvb", gs)], scale=fwv[:, f_, 1:2])
        stt("dve", cvv[:, :, 1:64], gv[:, :, 0:63], fwv[:, f_, 0:1], cvv[:, :, 1:64], ALU.mult, ALU.add, [("ps", bg), "sm", ("cvb", gs)], [("cvb", gs)])
        stt("dve", cvv[:, :, 0:63], gv[:, :, 1:64], fwv[:, f_, 2:3], cvv[:, :, 0:63], ALU.mult, ALU.add, [("ps", bg), "sm", ("cvb", gs)], [("cvb", gs)])
        act(sgf[gs], cv, AF.Silu, [("cvb", gs), "sm"], [("sgf", gs)], bias=fcb[:, f_:f_ + 1])

    def ffn_tail2(f_, bv, bg, gs):
        tt("dve", actb[:, f_, :], B(bv), sgf[gs], ALU.mult, [("ps", bv), ("sgf", gs)], [("actb", f_)])

    for ti in range(NT):
        sl = ti % 2
        t0 = ti * 512
        prev = None
        prev2 = None
        for f_ in range(NF):
            w3 = pend.pop(0)
            nxt = ti * NF + f_ + 2
            if nxt < NT * NF:
                pend.append(wu_load(nxt % NF))
            bv, bg = nb(), nb()
            for k in range(8):
                mm(B(bg), wus[w3][:, k, 128:256], hn2[:, k, :], k == 0, k == 7, [("wus", w3), "hn2"], [("ps", bg)])
            for k in range(8):
                mm(B(bv), wus[w3][:, k, 0:128], hn2[:, k, :], k == 0, k == 7, [("wus", w3), "hn2"], [("ps", bv)])
            gs = f_ % 3
            if prev is not None:
                ffn_tail1(*prev)
            if prev2 is not None:
                ffn_tail2(*prev2)
            prev2 = prev
            prev = (f_, bv, bg, gs)
            if f_ == 11 and ti + 1 < NT:
                nsl = (ti + 1) % 2
                norm_part(hs_[nsl], ("hs", nsl), 4, xns[nsl], ("xn2", nsl), junk, ssb, "ssb")
        ffn_tail1(*prev)
        ffn_tail2(*prev2)
        ffn_tail2(*prev)
        if ti + 1 < NT:
            tr_part(4, a2, sh2, hn2, "hn2", xns[nsl], ("xn2", nsl))
        for s_ in range(4):
            for hf in range(2):
                b = nb()
                for f_ in range(NF):
                    mm(B(b), actb[:, f_, s_ * 128:(s_ + 1) * 128], wd_bf[:, f_, hf * 512:(hf + 1) * 512], f_ == 0, f_ == NF - 1, [("actb", f_), "wd_bf"], [("ps", b)])
                tt("dve", hs_[sl][:, s_, hf * 512:(hf + 1) * 512], B(b), hs_[sl][:, s_, hf * 512:(hf + 1) * 512], ALU.add, [("ps", b), ("hs", sl)], [("hs", sl)])
        ss, rs = ssf[:, 0:4], ssf[:, 4:8]
        for s_ in range(4):
            act(junk, hs_[sl][:, s_, :], AF.Square, [("hs", sl)], ["junk", "ssf"], accum_out=ss[:, s_:s_ + 1])
        ts("dve", rs, ss, 1.0 / D, EPS, ALU.mult, ALU.add, ["ssf"], ["ssf"])
        act(rs, rs, AF.Sqrt, ["ssf"], ["ssf"])
        P.op("dve", lambda e: e.reciprocal(out=rs, in_=rs), reads=["ssf"], writes=["ssf"])
        for s_ in range(4):
            stt("dve", ot[0][:, s_, :], hs_[sl][:, s_, :], rs[:, s_:s_ + 1], fg_bc, ALU.mult, ALU.mult, [("hs", sl), "ssf", "fg_bc"], ["ot"])
        P.dma("pool", y_d[t0:t0 + 512, :].rearrange("(s p) d -> p s d", p=128), ot[0], reads=["ot"], writes=["y_d"], semkey="yst")
        if ti + 2 < NT:
            p3b2_load(ti + 2)
    P.barrier()
    return finish()


def _col(v, nt):
    return np.ascontiguousarray(np.asarray(v, np.float32).reshape(nt, 128).T)


def _pair32(a):
    a = np.asarray(a, np.float32).reshape(2, 16, 2, 64)
    return np.ascontiguousarray(a.transpose(2, 3, 0, 1).reshape(128, 32))


def prep_shared(inp):
    f = lambda k: np.asarray(inp[k], np.float32)
    sh = {}
    sm = np.zeros((128, NSM), np.float32)

    def put(name, arr):
        o, w = _SM[name]
        sm[:, o:o + w] = arr.reshape(128, w)

    put("n1g", _col(f("norm1_g")[0], 8))
    put("n2g", _col(f("norm2_g")[0], 8))
    put("s5d", _col(f("s5_d")[0], 4))
    put("bglu", _col(f("b_glu")[0], 4))
    put("confb", _col(f("conf_b")[0], 4))
    put("lng", _col(f("conf_ln_g")[0], 4))
    put("lnb", _col(f("conf_ln_b")[0], 4))
    cw = f("conf_w")[0]
    put("confw", np.ascontiguousarray(cw.T.reshape(4, 128, KC).transpose(1, 0, 2)))
    fw = f("ffn_conv_w")[0]
    put("fcw", np.ascontiguousarray(fw.T.reshape(NF, 128, 3).transpose(1, 0, 2)))
    put("fcb", _col(f("ffn_conv_b")[0], NF))
    put("ident", np.eye(128, dtype=np.float32))
    bm = (np.arange(128)[:, None] // 16 == np.arange(128)[None, :] // 16).astype(np.float32)
    put("bmask", bm)
    put("lamr", _pair32(f("s5_lam_re")[0]))
    put("lami", _pair32(f("s5_lam_im")[0]))
    put("ldt", _pair32(np.broadcast_to(f("s5_log_dt")[0][:, :, None], (2, 32, 64))))
    sh["sm"] = sm

    def pairB(a):
        a = a.reshape(2, 16, 2, 64, 16)
        return a.transpose(2, 3, 0, 1, 4).reshape(128, 512)

    def pairC(a):
        a = a.reshape(2, 16, 2, 16, 64)
        return a.transpose(2, 4, 0, 1, 3).reshape(128, 512)

    sh["s5b"] = np.ascontiguousarray(np.stack([pairB(f("s5_b_re")[0]), pairB(f("s5_b_im")[0])], 1))
    sh["s5c"] = np.ascontiguousarray(np.stack([pairC(f("s5_c_re")[0]), pairC(f("s5_c_im")[0])], 1))
    sh["bmod"] = np.ascontiguousarray(f("b_mod")[0].reshape(1, 6 * D))
    sh["fg"] = np.ascontiguousarray(f("final_g").reshape(1, D))
    wm = f("w_mod")[0]
    sh["wmod"] = np.ascontiguousarray(wm.reshape(8, 128, 12, 512).transpose(1, 2, 0, 3).reshape(128, 12, 8 * 512))
    sh["win"] = np.ascontiguousarray(f("w_in")[0].reshape(8, 128, 1536).transpose(1, 0, 2).reshape(128, 8 * 1536))
    sh["wglu"] = np.ascontiguousarray(f("w_glu")[0].reshape(4, 128, 512).transpose(1, 0, 2).reshape(128, 4 * 512))
    sh["wout"] = np.ascontiguousarray(f("w_out")[0].reshape(8, 128, 1024).transpose(1, 0, 2))
    wu = f("ffn_w_up")[0]
    wu = wu.reshape(8, 128, 2, NF, 128)
    sh["wup"] = np.ascontiguousarray(wu.transpose(3, 1, 0, 2, 4).reshape(NF * 128 * 2, 1024))
    sh["wdn"] = np.ascontiguousarray(f("ffn_w_down")[0].reshape(NF, 128, 1024).transpose(1, 0, 2))
    return sh


def prep_core(inp, sh, b):
    m = {k: v for k, v in sh.items() if k != "sm"}
    sm = sh["sm"].copy()
    cc = np.stack([np.asarray(inp["c"], np.float32)[b], np.asarray(inp["c_ctx"], np.float32)], 1)
    o, w = _SM["cc"]
    sm[:, o:o + w] = cc.reshape(8, 128, 2).transpose(1, 0, 2).reshape(128, 16)
    m["smalls"] = sm
    m["x"] = np.ascontiguousarray(np.asarray(inp["x"], np.float32)[b])
    m["ctx"] = np.ascontiguousarray(np.asarray(inp["ctx"], np.float32)[b])
    return m


_NC_CACHE = {}


def kernel(**inputs):
    nb_, L = inputs["x"].shape[0], inputs["x"].shape[1]
    if L not in _NC_CACHE:
        _NC_CACHE[L] = build(L)
    nc = _NC_CACHE[L]
    sh = prep_shared(inputs)
    in_maps = [prep_core(inputs, sh, b) for b in range(nb_)]
    res = run_bass_kernel_spmd(nc, in_maps, core_ids=list(range(nb_)))
    return np.stack([np.asarray(r["y"], np.float32) for r in res.results], 0)
```
